# Optimizing a Trainium2 kernel written in Bass

```python
import jax, jax.numpy as jnp
from jax import lax
import numpy as np

D_MODEL = 4096
BATCH = 1
SEQ = 8192
DEPTH = 1

MIX_WIDTH = D_MODEL
CONV_WIDTH = MIX_WIDTH // 2
ATTN_WIDTH = MIX_WIDTH - CONV_WIDTH
HEAD_DIM = 128
N_Q_HEADS = ATTN_WIDTH // HEAD_DIM
N_KV_HEADS = max(1, N_Q_HEADS // 4)
GQA_GROUP = N_Q_HEADS // N_KV_HEADS
KV_WIDTH = N_KV_HEADS * HEAD_DIM
WINDOW = 128
BLOCK = 128
DN_ALPHA = (2.0 * DEPTH) ** 0.25
DN_BETA = (8.0 * DEPTH) ** -0.25
LN_EPS = 1e-5
NEG_INF = -1e30

OFF_CB = 0
OFF_CC = OFF_CB + CONV_WIDTH
OFF_CH = OFF_CC + CONV_WIDTH
OFF_CZ = OFF_CH + CONV_WIDTH
OFF_Q = OFF_CZ + CONV_WIDTH
OFF_K = OFF_Q + ATTN_WIDTH
OFF_V = OFF_K + KV_WIDTH
OFF_AZ = OFF_V + KV_WIDTH
PROJ_WIDTH = OFF_AZ + ATTN_WIDTH

kernel_name = "hybrid_shortconv_swa_deepnorm_encoder"


def layer_norm(x, g, b):
    xf = x.astype(jnp.float32)
    mu = jnp.mean(xf, axis=-1, keepdims=True)
    var = jnp.mean(jnp.square(xf - mu), axis=-1, keepdims=True)
    y = (xf - mu) * lax.rsqrt(var + LN_EPS) * g.astype(jnp.float32) + b.astype(jnp.float32)
    return y.astype(x.dtype)


def alibi_slopes(n_heads):
    h = jnp.arange(1, n_heads + 1, dtype=jnp.float32)
    return jnp.exp2(-8.0 * h / n_heads)


def centred_short_conv(u, w):
    up = jnp.pad(u, ((0, 0), (1, 1), (0, 0)))
    return up[:, :-2] * w[0] + up[:, 1:-1] * w[1] + up[:, 2:] * w[2]


def banded_window_attention(q, k, v, sink):
    b, s, _ = q.shape
    nb = s // BLOCK
    qb = q.reshape(b, nb, BLOCK, N_KV_HEADS, GQA_GROUP, HEAD_DIM).astype(jnp.float32)

    def band(t):
        t = t.reshape(b, nb, BLOCK, N_KV_HEADS, HEAD_DIM).astype(jnp.float32)
        tp = jnp.pad(t, ((0, 0), (1, 1), (0, 0), (0, 0), (0, 0)))
        return jnp.concatenate([tp[:, :-2], tp[:, 1:-1], tp[:, 2:]], axis=2)

    kb, vb = band(k), band(v)
    scores = jnp.einsum("bnqhgd,bnkhd->bnhgqk", qb, kb) * (HEAD_DIM ** -0.5)

    blk = jnp.arange(nb, dtype=jnp.int32)[:, None]
    q_pos = blk * BLOCK + jnp.arange(BLOCK, dtype=jnp.int32)[None, :]
    k_pos = (blk - 1) * BLOCK + jnp.arange(3 * BLOCK, dtype=jnp.int32)[None, :]
    dist = jnp.abs(q_pos[:, :, None] - k_pos[:, None, :])
    valid = (dist <= WINDOW) & (k_pos[:, None, :] >= 0) & (k_pos[:, None, :] < s)

    slopes = alibi_slopes(N_Q_HEADS).reshape(N_KV_HEADS, GQA_GROUP)
    bias = -slopes[None, None, :, :, None, None] * dist[None, :, None, None].astype(jnp.float32)
    scores = jnp.where(valid[None, :, None, None], scores + bias, NEG_INF)

    sink_l = sink.astype(jnp.float32).reshape(1, 1, N_KV_HEADS, GQA_GROUP, 1, 1)
    m = jnp.maximum(jnp.max(scores, axis=-1, keepdims=True), sink_l)
    p = jnp.exp(scores - m)
    denom = jnp.sum(p, axis=-1, keepdims=True) + jnp.exp(sink_l - m)
    out = jnp.einsum("bnhgqk,bnkhd->bnqhgd", p / denom, vb)
    return out.reshape(b, s, ATTN_WIDTH).astype(q.dtype)


def hybrid_layer(x, w_in, conv_w, sink, w_out, ln_g, ln_b):
    p = x @ w_in
    c_b = p[..., OFF_CB:OFF_CC]
    c_c = p[..., OFF_CC:OFF_CH]
    c_h = p[..., OFF_CH:OFF_CZ]
    c_z = p[..., OFF_CZ:OFF_Q]
    a_q = p[..., OFF_Q:OFF_K]
    a_k = p[..., OFF_K:OFF_V]
    a_v = p[..., OFF_V:OFF_AZ]
    a_z = p[..., OFF_AZ:PROJ_WIDTH]

    y_conv = c_b * centred_short_conv(c_c * c_h, conv_w) * jax.nn.silu(c_z)
    y_attn = banded_window_attention(a_q, a_k, a_v, sink) * jax.nn.silu(a_z)

    y = jnp.concatenate([y_conv, y_attn], axis=-1) @ w_out
    return layer_norm(DN_ALPHA * x + y, ln_g, ln_b)


def setup_inputs(seed: int = 0) -> dict:
    key = jax.random.key(seed)
    ks = jax.random.split(key, 10)
    x = jax.random.normal(ks[0], (BATCH, SEQ, D_MODEL), jnp.float32)
    emb_ln_g = 1.0 + 0.02 * jax.random.normal(ks[1], (D_MODEL,), jnp.float32)
    emb_ln_b = 0.02 * jax.random.normal(ks[2], (D_MODEL,), jnp.float32)
    col_scale = (jnp.ones((PROJ_WIDTH,), jnp.float32)
                 .at[OFF_CH:OFF_CZ].set(DN_BETA)
                 .at[OFF_V:OFF_AZ].set(DN_BETA))
    w_in = (jax.random.normal(ks[3], (DEPTH, D_MODEL, PROJ_WIDTH), jnp.float32)
            * (D_MODEL ** -0.5) * col_scale)
    conv_w = jax.random.normal(ks[4], (DEPTH, 3, CONV_WIDTH), jnp.float32) * (3.0 ** -0.5)
    sink = 0.5 * jax.random.normal(ks[5], (DEPTH, N_Q_HEADS), jnp.float32)
    w_out = (jax.random.normal(ks[6], (DEPTH, MIX_WIDTH, D_MODEL), jnp.float32)
             * (MIX_WIDTH ** -0.5) * DN_BETA)
    ln_g = 1.0 + 0.02 * jax.random.normal(ks[7], (DEPTH, D_MODEL), jnp.float32)
    ln_b = 0.02 * jax.random.normal(ks[8], (DEPTH, D_MODEL), jnp.float32)
    return {"x": x, "emb_ln_g": emb_ln_g, "emb_ln_b": emb_ln_b, "w_in": w_in,
            "conv_w": conv_w, "sink": sink, "w_out": w_out, "ln_g": ln_g, "ln_b": ln_b}


def reference(x, emb_ln_g, emb_ln_b, w_in, conv_w, sink, w_out, ln_g, ln_b):
    h = layer_norm(x, emb_ln_g, emb_ln_b)
    for l in range(DEPTH):
        h = hybrid_layer(h, w_in[l], conv_w[l], sink[l], w_out[l], ln_g[l], ln_b[l])
    return h
```

```python
import contextlib
import numpy as np
import concourse.bass as bass
import concourse.mybir as mybir
from concourse.bass_utils import run_bass_kernel_spmd

F32 = mybir.dt.float32
BF16 = mybir.dt.bfloat16
ALU = mybir.AluOpType
ACTF = mybir.ActivationFunctionType

NCORES = 8
D_MODEL = 4096
SEQ = 8192
TOK = SEQ // NCORES
NTC = TOK // 128
KC = D_MODEL // 128
CONV_W = 2048
HD = 128
OFF_CB, OFF_CC, OFF_CH, OFF_CZ = 0, 2048, 4096, 6144
OFF_Q, OFF_K, OFF_V, OFF_AZ = 8192, 10240, 10752, 11264
PROJ = 13312
DN_ALPHA = (2.0 * 1) ** 0.25
LN_EPS = 1e-5
SCALE = HD ** -0.5
NEGBIG = -30000.0
NW = 3
NFILL = 6
import os as _os
SKIP = set(_os.environ.get("KSKIP", "").split(","))


class Op:
    __slots__ = ("eng", "emit", "deps", "sig", "tick", "csem", "is_dma", "uid", "name")


class Rec:
    ENGS = ("sp", "act", "pool", "dve", "pe")

    def __init__(self):
        self.ops = {e: [] for e in self.ENGS}
        self.keyw = {}
        self.keyr = {}
        self.tok = {}
        self.uid = 0
        self.dma_counts = {}

    def op(self, eng, emit, reads=(), writes=(), toks=(), dma_sem=None, name="", extra_deps=()):
        if SKIP and ((eng in SKIP) or (name and name.split(" ")[0] in SKIP)):
            emit = None if dma_sem is None else emit
            if dma_sem is not None:
                return None
        ps_r = [k for k in reads if isinstance(k, tuple) and k[0] == "ps"]
        if ps_r:
            reads = [k for k in reads if not (isinstance(k, tuple) and k[0] == "ps")]
            writes = list(writes) + [k for k in ps_r if k not in writes]
        o = Op()
        o.eng, o.emit, o.deps, o.sig, o.tick = eng, emit, set(), False, None
        o.is_dma = dma_sem is not None
        o.csem = dma_sem
        o.uid = self.uid
        o.name = name
        self.uid += 1
        ek = ("dma", o.uid) if o.is_dma else eng
        for k in reads:
            for w in self.keyw.get(k, {}).values():
                o.deps.add(w)
        for k in writes:
            for w in self.keyw.get(k, {}).values():
                o.deps.add(w)
            for r in self.keyr.get(k, {}).values():
                o.deps.add(r)
        for (tname, ph) in toks:
            st = self.tok.setdefault(tname, [ph, {}, {}])
            assert ph >= st[0], (tname, ph, st[0])
            if ph > st[0]:
                st[0], st[2], st[1] = ph, st[1], {}
            for p in st[2].values():
                o.deps.add(p)
        for d in extra_deps:
            o.deps.add(d)
        o.deps.discard(o)
        if eng == "pe":
            o.deps = {d for d in o.deps if d.is_dma or d.eng != "pe"}
        for d in o.deps:
            d.sig = True
        for k in reads:
            self.keyr.setdefault(k, {})[ek] = o
        for k in writes:
            if self.keyr.get(k):
                self.keyw[k] = {ek: o}
                self.keyr[k] = {}
            else:
                self.keyw.setdefault(k, {})[ek] = o
        for (tname, ph) in toks:
            self.tok[tname][1][ek] = o
        if o.is_dma:
            c = self.dma_counts.get(id(dma_sem), 0) + 16
            self.dma_counts[id(dma_sem)] = c
            o.tick = c
            o.sig = True
        self.ops[eng].append(o)
        return o

    def finalize(self, eng_sems):
        for e in self.ENGS:
            n = 0
            for o in self.ops[e]:
                if o.is_dma:
                    continue
                if o.sig:
                    n += 1
                    o.tick = n
                    o.csem = eng_sems[e]

    def emit_engine(self, e, handle):
        waited = {}
        for o in self.ops[e]:
            for d in sorted(o.deps, key=lambda t: t.uid):
                assert d.tick is not None, (d.name, o.name)
                k = id(d.csem)
                if waited.get(k, 0) >= d.tick:
                    continue
                handle.wait_ge(d.csem, d.tick)
                waited[k] = d.tick
            if o.emit is None:
                if o.sig and not o.is_dma:
                    handle.nop().then_inc(o.csem, 1)
                continue
            ins = o.emit(handle)
            if o.is_dma:
                ins.then_inc(o.csem, 16)
            elif o.sig:
                ins.then_inc(o.csem, 1)


def build_program(stop=99):
    nc = bass.Bass("TRN2", target_bir_lowering=False)
    x_in = nc.dram_tensor("x_in", [TOK + 256, D_MODEL], F32, kind="ExternalInput").ap()
    w_in = nc.dram_tensor("w_in", [D_MODEL, PROJ], F32, kind="ExternalInput").ap()
    w_out = nc.dram_tensor("w_out", [D_MODEL, D_MODEL], F32, kind="ExternalInput").ap()
    gbT = nc.dram_tensor("gbT", [128, 2 * KC], F32, kind="ExternalInput").ap()
    cwT = nc.dram_tensor("cwT", [128, 16 * 3], F32, kind="ExternalInput").ap()
    sinkb = nc.dram_tensor("sinkb", [128, 16], F32, kind="ExternalInput").ap()
    kbias_d = nc.dram_tensor("kbias", [128, 12], F32, kind="ExternalInput").ap()
    prow = nc.dram_tensor("prow", [4, 128, D_MODEL], F32, kind="ExternalInput").ap()
    out = nc.dram_tensor("out", [TOK, D_MODEL], F32, kind="ExternalOutput").ap()

    wob = nc.dram_tensor("wob", [D_MODEL, D_MODEL], BF16).ap()
    w_in_v = w_in.rearrange("(kc p) f -> p kc f", p=128)
    w_out_v = wob.rearrange("(c p) d -> p c d", p=128)

    R = Rec()
    es = contextlib.ExitStack()
    with es:
        KB = 1024
        ARENA_BYTES = 205 * KB
        arena = es.enter_context(nc.sbuf_tensor("arena", [128, ARENA_BYTES // 2], BF16))

        def carve(off_b, nbytes, dt, shape=None):
            a = arena[:, off_b // 2:(off_b + nbytes) // 2]
            if dt == F32:
                a = a.bitcast(F32)
            if shape is not None and len(shape) > 1:
                names = " ".join("a%d" % i for i in range(len(shape)))
                kw = {"a%d" % i: s for i, s in enumerate(shape[:-1])}
                a = a.rearrange("p (%s) -> p %s" % (names, names), **kw)
            return a

        A0 = 0
        B0 = 64 * KB
        C0 = 128 * KB
        D0 = C0 + NW * 8 * KB
        G0 = D0 + 44 * KB
        hT = carve(A0, 64 * KB, BF16, [KC, TOK])
        zbuf = carve(A0, 64 * KB, F32, [4, D_MODEL])
        ycatT = carve(B0, 64 * KB, BF16, [KC, TOK])
        emask = carve(B0, 12 * KB, BF16, [12, 512])
        pT = carve(B0 + 12 * KB, 6 * KB, BF16, [2, 3, 512])
        ebuf = carve(B0 + 18 * KB, 6 * KB, BF16, [2, 3, 512])
        densb = carve(B0 + 24 * KB, 2 * KB, F32, [512])
        o1 = carve(B0 + 26 * KB, 2 * KB, F32, [512])
        pvsb = carve(B0 + 28 * KB, 2 * KB, F32, [512])
        thz = carve(B0 + 30 * KB, 2 * KB, F32, [1, 512])
        xs = carve(B0, 64 * KB, F32, [4, D_MODEL])
        wring = carve(C0, NW * 8 * KB, BF16, [NW, KC, 128])
        wring2 = carve(C0, NW * 8 * KB, BF16, [NW, 8, 512])
        hTh = carve(D0, 16 * KB, BF16, [KC, 256])
        siluz = carve(D0, 16 * KB, F32, [4, TOK])
        KT = carve(D0 + 16 * KB, 10 * KB, BF16, [4, 10 * 128])
        Vt = carve(D0 + 26 * KB, 10 * KB, BF16, [10, 512])
        QT = carve(D0 + 36 * KB, 8 * KB, BF16, [4, TOK])
        CW = 1032
        Hs = carve(D0, CW * 4, F32, [CW])
        ub = carve(D0 + CW * 4, CW * 4, F32, [CW])
        vb = carve(D0 + 2 * CW * 4, 4 * KB, F32, [TOK])
        v2b = carve(D0 + 2 * CW * 4 + 4 * KB, 4 * KB, F32, [TOK])
        thb = carve(D0 + 2 * CW * 4 + 8 * KB, 4 * KB, F32, [TOK])
        sb_ = carve(D0 + 2 * CW * 4 + 12 * KB, 4 * KB, F32, [TOK])
        par = carve(D0, 16 * KB, F32, [4, 2, 512])
        xt = carve(D0 + 16 * KB, 6 * KB, F32, [3, 512])
        t1 = carve(D0 + 22 * KB, 4 * KB, F32, [2, 512])
        xa = carve(D0 + 26 * KB, 6 * KB, F32, [3, 512])
        t2 = carve(D0 + 32 * KB, 4 * KB, F32, [2, 512])
        g = [G0]

        def galloc(nbytes, dt, shape=None):
            nb = (nbytes + 63) // 64 * 64
            a = carve(g[0], nb, dt, None)
            n_el = nbytes // (4 if dt == F32 else 2)
            a = a[:, 0:n_el]
            if shape is not None and len(shape) > 1:
                names = " ".join("a%d" % i for i in range(len(shape)))
                kw = {"a%d" % i: s for i, s in enumerate(shape[:-1])}
                a = a.rearrange("p (%s) -> p %s" % (names, names), **kw)
            g[0] += nb
            return a

        gb_sb = galloc(2 * KC * 4, F32)
        cw_sb = galloc(48 * 4, F32, [16, 3])
        sink2 = galloc(16 * 4, F32)
        kb_sb = galloc(12 * 4, F32)
        ident = galloc(128 * 4, F32)
        ones_b = galloc(128 * 2, BF16)
        hTe = galloc(KC * 2 * 2, BF16, [KC, 2])
        bnst = galloc(2 * 8 * 6 * 4, F32, [2, 8, 6])
        mv = galloc(2 * 2 * 4, F32, [2, 2])
        mu_all = galloc(10 * 4, F32)
        rstd_all = galloc(10 * 4, F32)
        arstd_all = galloc(10 * 4, F32)
        nmr_all = galloc(10 * 4, F32)
        namr_all = galloc(10 * 4, F32)
        nmz = galloc(4 * 4, F32)
        veps = galloc(10 * 4, F32)
        mhalf = galloc(4, F32)
        stz = galloc(4 * 8 * 6 * 4, F32, [4, 8, 6])
        mvz = galloc(4 * 2 * 4, F32, [4, 2])
        vez = galloc(4 * 4, F32)
        rz = galloc(4 * 4, F32)
        dD = carve(D0 + 36 * KB, 1536, F32, [3, 128])
        absD = carve(D0 + 36 * KB + 1536, 1536, F32, [3, 128])
        m01 = carve(D0 + 36 * KB + 3072, 1536, F32, [3, 128])
        tmpE = carve(D0 + 36 * KB + 4608, 1536, F32, [3, 128])
        assert g[0] <= ARENA_BYTES, g[0]

        psb = [es.enter_context(nc.psum_tensor("ps%d" % i, [128, 512], F32)) for i in range(8)]

        def sem(name):
            return es.enter_context(nc.semaphore(name))

        eng_sems = {e: sem("s_" + e) for e in Rec.ENGS}
        s_const = [sem("d_const%d" % i) for i in range(4)]
        s_xs = [sem("d_xs%d" % i) for i in range(4)]
        s_w = [sem("d_w%d" % i) for i in range(NW)]
        s_par = [[sem("d_par%d_%d" % (a, i)) for i in range(2)] for a in range(4)]
        s_xt = [sem("d_xt%d" % i) for i in range(3)]
        s_out = [sem("d_out%d" % i) for i in range(32)]

        def dma(eng, out_ap, in_ap, dsem, reads=(), writes=(), toks=(), name=""):
            return R.op(eng, lambda e: e.dma_start(out=out_ap, in_=in_ap), reads=reads, writes=writes,
                        toks=toks, dma_sem=dsem, name=name)

        dma("sp", gb_sb, gbT[:, :], s_const[0], writes=["gb"])
        dma("sp", cw_sb.rearrange("p a b -> p (a b)"), cwT[:, :], s_const[1], writes=["cw"])
        dma("sp", sink2, sinkb[:, :], s_const[2], writes=["sink"])
        dma("sp", kb_sb, kbias_d[:, :], s_const[3], writes=["kb"])

        R.op("pool", lambda e: e.memset(ones_b, 1.0), writes=["ones"])
        R.op("pool", lambda e: e.memset(mhalf, -0.5), writes=["mhalf"])
        R.op("pool", lambda e: e.iota(ident, pattern=[[1, 128]], base=0, channel_multiplier=-1,
                                      allow_small_or_imprecise_dtypes=True), writes=["ident"])
        R.op("pool", lambda e: e.tensor_single_scalar(out=ident, in_=ident, scalar=0.0, op=ALU.is_equal),
             reads=["ident"], writes=["ident"])
        R.op("act", lambda e: e.activation(out=sink2, in_=sink2, func=ACTF.Exp), reads=["sink"], writes=["sink"],
             name="actsink")
        R.op("dve", lambda e: e.tensor_single_scalar(out=sink2, in_=sink2, scalar=2.0, op=ALU.mult),
             reads=["sink"], writes=["sink"])
        R.op("dve", lambda e: e.tensor_single_scalar(out=cw_sb, in_=cw_sb, scalar=0.5, op=ALU.mult),
             reads=["cw"], writes=["cw"])

        wcount = [0]

        def w_dma_in(col):
            slot = wcount[0] % NW
            wcount[0] += 1
            dma("pool", wring[:, slot], w_in_v[:, :, col:col + 128], s_w[slot], writes=[("w", slot)],
                name="w_in %d" % col)
            return slot

        def w_dma_out(r, cq):
            slot = wcount[0] % NW
            wcount[0] += 1
            dma("pool", wring2[:, slot], w_out_v[:, cq * 8:(cq + 1) * 8, r * 512:(r + 1) * 512], s_w[slot],
                reads=["wob"], writes=[("w", slot)], name="w_out %d %d" % (r, cq))
            return slot

        sched = []
        for j in range(4):
            sched.append(("K", j, OFF_K + 128 * j))
        for j in range(4):
            sched.append(("V", j, OFF_V + 128 * j))
        for j in range(4):
            for gg in range(4):
                sched.append(("Q", 4 * j + gg, OFF_Q + 128 * (4 * j + gg)))
            for gg in range(4):
                sched.append(("AZ", 4 * j + gg, OFF_AZ + 128 * (4 * j + gg)))
        for c in range(16):
            sched.append(("H", c, OFF_CH + 128 * c))
            sched.append(("C", c, OFF_CC + 128 * c))
            sched.append(("B", c, OFF_CB + 128 * c))
            sched.append(("Z", c, OFF_CZ + 128 * c))
        wq = [("in", s[2]) for s in sched]
        for gi in range(2):
            for r in range(8):
                for cq in range(4):
                    wq.append(("out", r, cq))
        wq_pos = [0]
        wslots = []

        def issue_next_w():
            if wq_pos[0] >= len(wq):
                return
            it = wq[wq_pos[0]]
            wq_pos[0] += 1
            if it[0] == "in":
                wslots.append(w_dma_in(it[1]))
            else:
                wslots.append(w_dma_out(it[1], it[2]))

        for _ in range(NW):
            issue_next_w()
        wuse = [0]

        def next_wslot():
            s = wslots[wuse[0]]
            wuse[0] += 1
            return s

        tp_bank = [0]

        def p0_load_stats(tc, slot):
            bs = tc % 2
            dma("sp", xs[:, slot], x_in[tc * 128:(tc + 1) * 128, :], s_xs[slot], writes=[("xs", slot)],
                toks=[("RB", 0)], name="x load")
            for q in range(8):
                R.op("dve", lambda e, q=q, slot=slot, bs=bs: e.bn_stats(out=bnst[:, bs, q, :],
                                                                       in_=xs[:, slot, q * 512:(q + 1) * 512]),
                     reads=[("xs", slot)], writes=[("bnst", bs)], toks=[("RB", 0)])
            R.op("dve", lambda e, bs=bs: e.bn_aggr(out=mv[:, bs, :], in_=bnst[:, bs]),
                 reads=[("bnst", bs)], writes=[("mv", bs)])
            R.op("dve", lambda e, bs=bs, tc=tc: e.tensor_single_scalar(out=veps[:, tc:tc + 1], in_=mv[:, bs, 1:2],
                                                                       scalar=LN_EPS, op=ALU.add),
                 reads=[("mv", bs)], writes=[("veps", tc)])
            R.op("dve", lambda e, bs=bs, tc=tc: e.tensor_copy(out=mu_all[:, tc:tc + 1], in_=mv[:, bs, 0:1]),
                 reads=[("mv", bs)], writes=[("mu", tc)])
            R.op("pool", lambda e, tc=tc: e.tensor_tensor(out=rstd_all[:, tc:tc + 1], in0=veps[:, tc:tc + 1],
                                                          in1=mhalf, op=ALU.pow),
                 reads=[("veps", tc), "mhalf"], writes=[("rstd", tc)])
            R.op("dve", lambda e, tc=tc: e.tensor_scalar(out=nmr_all[:, tc:tc + 1], in0=mu_all[:, tc:tc + 1],
                                                         scalar1=rstd_all[:, tc:tc + 1], scalar2=-1.0,
                                                         op0=ALU.mult, op1=ALU.mult),
                 reads=[("mu", tc), ("rstd", tc)], writes=[("nmr", tc)])

        def p0_norm(tc, slot):
            R.op("act", lambda e, slot=slot, tc=tc: e.activation(out=xs[:, slot], in_=xs[:, slot], func=ACTF.Identity,
                                                                 scale=rstd_all[:, tc:tc + 1],
                                                                 bias=nmr_all[:, tc:tc + 1]),
                 reads=[("xs", slot), ("rstd", tc), ("nmr", tc)], writes=[("xs", slot)], toks=[("RB", 0)],
                 name="actnorm")

        def p0_small(tc):
            R.op("dve", lambda e, tc=tc: e.tensor_single_scalar(out=namr_all[:, tc:tc + 1], in_=nmr_all[:, tc:tc + 1],
                                                                scalar=DN_ALPHA, op=ALU.mult),
                 reads=[("nmr", tc)], writes=[("namr", tc)])
            R.op("dve", lambda e, tc=tc: e.tensor_single_scalar(out=arstd_all[:, tc:tc + 1],
                                                                in_=rstd_all[:, tc:tc + 1], scalar=DN_ALPHA,
                                                                op=ALU.mult),
                 reads=[("rstd", tc)], writes=[("arstd", tc)])

        def p0_transpose(tca, tcb, sa, sb2):
            for kq in range(16):
                bank = tp_bank[0] % 4
                tp_bank[0] += 1

                def tr(e, kq=kq, bank=bank):
                    ins = None
                    for k2 in range(2):
                        kc = kq * 2 + k2
                        for ti, sl in ((0, sa), (1, sb2)):
                            c0 = k2 * 256 + ti * 128
                            ins = e.transpose(psb[bank][:, c0:c0 + 128], xs[:, sl, kc * 128:(kc + 1) * 128], ident)
                    return ins
                R.op("pe", tr, reads=[("xs", sa), ("xs", sb2), "ident"], writes=[("ps", bank)], toks=[("RB", 0)])
                for k2 in range(2):
                    kc = kq * 2 + k2
                    if tca == 0:
                        dst = hTh[:, kc, 0:256]
                        wk = ("hTh", kc)
                    else:
                        dst = hT[:, kc, (tca - 1) * 128:(tca + 1) * 128]
                        wk = ("hT", kc, tca - 1)
                    wk2 = ("hT", kc, tcb - 1) if tca != 0 else ("hTh", kc)
                    src = psb[bank][:, k2 * 256:(k2 + 1) * 256]
                    if bank != 3:
                        R.op("act", lambda e, dst=dst, src=src, kc=kc: e.activation(
                            out=dst, in_=src, func=ACTF.Identity, scale=gb_sb[:, kc:kc + 1],
                            bias=gb_sb[:, KC + kc:KC + kc + 1]),
                            reads=[("ps", bank), "gb"], writes=[wk, wk2], toks=[("RD", 0)], name="actevac")
                    else:
                        R.op("dve", lambda e, dst=dst, src=src, kc=kc: e.tensor_scalar(
                            out=dst, in0=src, scalar1=gb_sb[:, kc:kc + 1], scalar2=gb_sb[:, KC + kc:KC + kc + 1],
                            op0=ALU.mult, op1=ALU.add),
                            reads=[("ps", bank), "gb"], writes=[wk, wk2], toks=[("RD", 0)])

        pairs = [(0, 9), (1, 2), (3, 4), (5, 6), (7, 8)]
        p0_load_stats(0, 0)
        p0_load_stats(9, 1)
        p0_norm(0, 0)
        p0_norm(9, 1)
        for pi_, (ta, tb_) in enumerate(pairs):
            sa, sb2 = 2 * (pi_ % 2), 2 * (pi_ % 2) + 1
            if pi_ + 1 < len(pairs):
                na, nb = pairs[pi_ + 1]
                p0_load_stats(na, 2 * ((pi_ + 1) % 2))
                p0_load_stats(nb, 2 * ((pi_ + 1) % 2) + 1)
            p0_transpose(ta, tb_, sa, sb2)
            p0_small(ta)
            p0_small(tb_)
            if pi_ + 1 < len(pairs):
                p0_norm(na, 2 * ((pi_ + 1) % 2))
                p0_norm(nb, 2 * ((pi_ + 1) % 2) + 1)

        R.op("pool", lambda e: e.tensor_copy(out=hTe, in_=hTh[:, :, 127:129]),
             reads=[("hTh", kc) for kc in range(KC)], writes=["hTe"], toks=[("RD", 0)])

        if stop <= 0:
            sched_stop = True
        R.op("pool", lambda e: e.iota(dD.rearrange("p a b -> p (a b)"), pattern=[[-128, 3], [1, 128]], base=128,
                                      channel_multiplier=-1, allow_small_or_imprecise_dtypes=True),
             writes=["dD"], toks=[("RD", 0)])
        R.op("act", lambda e: e.activation(out=absD, in_=dD, func=ACTF.Abs),
             reads=["dD"], writes=["absD"], toks=[("RD", 0)], name="actabs")
        R.op("dve", lambda e: e.tensor_single_scalar(out=m01, in_=absD, scalar=128.0, op=ALU.is_le),
             reads=["absD"], writes=["m01"], toks=[("RD", 0)])
        for h in range(16):
            slope = 2.0 ** (-8.0 * (h + 1) / 16.0)
            j, gg = h // 4, h % 4
            R.op("act", lambda e, s=slope: e.activation(out=tmpE, in_=absD, func=ACTF.Exp, scale=-s),
                 reads=["absD"], writes=["tmpE"], toks=[("RD", 0)], name="actexp")
            R.op("dve", lambda e, j=j, gg=gg: e.tensor_tensor(
                out=emask[:, j * 3:(j + 1) * 3, gg * 128:(gg + 1) * 128], in0=tmpE, in1=m01, op=ALU.mult),
                reads=["tmpE", "m01"], writes=["emask"], toks=[("RD", 0), ("RB", 1)])
        hT_all = [("hT", kc, t) for kc in range(KC) for t in range(NTC)]
        hTh_all = [("hTh", kc) for kc in range(KC)]

        nchunk = [0]

        def inproj(slot, halo):
            par_ = nchunk[0] % 2
            nchunk[0] += 1
            b0, b1 = 2 * par_, 2 * par_ + 1
            if halo == "kv":
                hN = 256
            elif halo == "edge":
                hN = 2
            else:
                hN = 0
            hb = 4 + par_
            hps = psb[hb][:, 0:hN] if hN else None

            def mm(e):
                ins = None
                for kc in range(KC):
                    w = wring[:, slot, kc, :]
                    st, sp_ = (kc == 0), (kc == KC - 1)
                    e.matmul(psb[b0][:, :], w, hT[:, kc, 0:512], start=st, stop=sp_)
                    ins = e.matmul(psb[b1][:, :], w, hT[:, kc, 512:1024], start=st, stop=sp_)
                    if halo == "kv":
                        ins = e.matmul(hps, w, hTh[:, kc, :], start=st, stop=sp_)
                    elif halo == "edge":
                        ins = e.matmul(hps, w, hTe[:, kc, :], start=st, stop=sp_)
                return ins
            reads = [("w", slot)] + hT_all
            writes = [("ps", b0), ("ps", b1)]
            toks = [("RA", 0)]
            if halo == "kv":
                reads += hTh_all
                writes.append(("ps", hb))
                toks.append(("RD", 0))
            elif halo == "edge":
                reads.append("hTe")
                writes.append(("ps", hb))
            R.op("pe", mm, reads=reads, writes=writes, toks=toks, name="inproj")
            issue_next_w()
            return b0, b1, (hps, hb)

        def attention(j):
            NS = 8

            def qk(i):
                banks = (5, 6, 7)

                def f(e):
                    ins = None
                    for b in range(3):
                        kb = i + b
                        ins = e.matmul(psb[banks[b]][:, :], KT[:, j, kb * 128:(kb + 1) * 128],
                                       QT[:, :, i * 128:(i + 1) * 128], start=True, stop=True)
                    return ins
                R.op("pe", f, reads=["KT", ("QT", j)], writes=[("ps", bk) for bk in banks], toks=[("RD", 1)])
                for b in range(3):
                    kb = i + b
                    R.op("act", lambda e, b=b, kb=kb, bk=banks[b]: e.activation(
                        out=ebuf[:, i % 2, b], in_=psb[bk][:, :], func=ACTF.Exp, scale=SCALE,
                        bias=kb_sb[:, kb:kb + 1]),
                        reads=[("ps", banks[b]), "kb"], writes=[("e", i % 2, b)], toks=[("RB", 1)])
                    R.op("dve" if b == 1 else "pool",
                         lambda e, b=b: e.tensor_tensor(out=pT[:, i % 2, b], in0=ebuf[:, i % 2, b],
                                                        in1=emask[:, j * 3 + b], op=ALU.mult),
                         reads=[("e", i % 2, b), "emask"], writes=[("pT", i % 2, b)], toks=[("RB", 1)])

            def pvd(i):
                def f(e):
                    ins = None
                    for b in range(3):
                        kb = i + b
                        e.matmul(psb[3][:, :], Vt[:, kb, j * 128:(j + 1) * 128], pT[:, i % 2, b],
                                 start=(b == 0), stop=(b == 2))
                        ins = e.matmul(psb[4][:, :], ones_b, pT[:, i % 2, b], start=(b == 0), stop=(b == 2))
                    return ins
                R.op("pe", f, reads=["Vt", "ones"] + [("pT", i % 2, b) for b in range(3)],
                     writes=[("ps", 3), ("ps", 4)], toks=[("RD", 1), ("RB", 1)])
                sk = sink2[:, 4 * j:4 * j + 4].unsqueeze(2).broadcast_to([128, 4, 128])
                R.op("dve", lambda e: e.scalar_tensor_tensor(
                    out=densb.rearrange("p (a b) -> p a b", a=4), in0=psb[4][:, :].rearrange("p (a b) -> p a b", a=4),
                    scalar=2.0, in1=sk, op0=ALU.mult, op1=ALU.add),
                    reads=[("ps", 4), "sink"], writes=["densb"], toks=[("RB", 1)])
                R.op("act", lambda e: e.activation(out=pvsb, in_=psb[3][:, :], func=ACTF.Identity),
                     reads=[("ps", 3)], writes=["pvsb"], toks=[("RB", 1)])
                R.op("act", lambda e: e.activation(out=densb, in_=densb, func=ACTF.Ln), reads=["densb"],
                     writes=["densb"], toks=[("RB", 1)])
                R.op("act", lambda e: e.activation(out=densb, in_=densb, func=ACTF.Exp, scale=-1.0), reads=["densb"],
                     writes=["densb"], toks=[("RB", 1)])
                R.op("dve", lambda e: e.tensor_tensor(out=o1, in0=pvsb, in1=densb, op=ALU.mult),
                     reads=["pvsb", "densb"], writes=["o1"], toks=[("RB", 1)])
                R.op("dve", lambda e: e.tensor_tensor(
                    out=ycatT[:, 16 + 4 * j:16 + 4 * j + 4, i * 128:(i + 1) * 128],
                    in0=o1.rearrange("p (a b) -> p a b", a=4), in1=siluz[:, :, i * 128:(i + 1) * 128], op=ALU.mult),
                    reads=["o1", ("siluz", j)], writes=[("ycat", 16 + 4 * j + gg) for gg in range(4)],
                    toks=[("RB", 1), ("RD", 1)])

            def filler(n):
                def f(e):
                    ins = None
                    for _ in range(n):
                        ins = e.matmul(psb[0][:, :], ones_b, emask[:, 0], start=True, stop=True)
                    return ins
                R.op("pe", f, reads=["ones", "emask"], writes=[("ps", 0)], toks=[("RB", 1)])

            for i in range(NS):
                qk(i)
                filler(NFILL)
                if i >= 1:
                    pvd(i - 1)
            filler(NFILL)
            pvd(NS - 1)

        s_cv = sem("d_cv")
        NCV = 16
        RCV = D_MODEL // NCV

        def issue_wout_convert(i):
            dma("pool", wob[i * RCV:(i + 1) * RCV, :], w_out[i * RCV:(i + 1) * RCV, :], s_cv,
                writes=(["wob"] if i == NCV - 1 else []), name="cvt %d" % i)
        cv_next = [0]
        ci = 0
        nlim = {0: 0, 1: 4, 2: 8, 3: 16, 4: 40}.get(stop, len(sched))
        while ci < min(len(sched), nlim):
            kind, idx, col = sched[ci]
            if kind == "K":
                slot = next_wslot()
                b0, b1, (hps, hb) = inproj(slot, "kv")
                j = idx
                R.op("act", lambda e, j=j, b0=b0: e.activation(out=KT[:, j, 128:640], in_=psb[b0][:, :],
                                                               func=ACTF.Identity),
                     reads=[("ps", b0)], writes=["KT"], toks=[("RD", 0)])
                R.op("dve", lambda e, j=j, b1=b1: e.tensor_copy(out=KT[:, j, 640:1152], in_=psb[b1][:, :]),
                     reads=[("ps", b1)], writes=["KT"], toks=[("RD", 0)])
                R.op("act", lambda e, j=j, hps=hps: e.activation(out=KT[:, j, 0:128], in_=hps[:, 0:128],
                                                                 func=ACTF.Identity),
                     reads=[("ps", hb)], writes=["KT"], toks=[("RD", 0)])
                R.op("act", lambda e, j=j, hps=hps: e.activation(out=KT[:, j, 1152:1280], in_=hps[:, 128:256],
                                                                 func=ACTF.Identity),
                     reads=[("ps", hb)], writes=["KT"], toks=[("RD", 0)])
                ci += 1
            elif kind == "V":
                assert idx % 2 == 0
                s0 = next_wslot()
                s1 = next_wslot()
                if s1 == s0 + 1:
                    pieces = [(wring[:, s0:s0 + 2], 256, 0)]
                else:
                    pieces = [(wring[:, s0:s0 + 1], 128, 0), (wring[:, s1:s1 + 1], 128, 128)]
                for tb in range(10):
                    bank = tb % 4
                    if tb == 0:
                        lt = lambda kc: hTh[:, kc, 0:128]
                    elif tb == 9:
                        lt = lambda kc: hTh[:, kc, 128:256]
                    else:
                        lt = lambda kc, tb=tb: hT[:, kc, (tb - 1) * 128:tb * 128]

                    def mmv(e, lt=lt, bank=bank, pieces=pieces):
                        ins = None
                        for (wap, n, off) in pieces:
                            for kc in range(KC):
                                ins = e.matmul(psb[bank][:, off:off + n], lt(kc), wap[:, :, kc, :],
                                               start=(kc == 0), stop=(kc == KC - 1))
                        return ins
                    R.op("pe", mmv, reads=[("w", s0), ("w", s1)] + hT_all + hTh_all, writes=[("ps", bank)],
                         toks=[("RA", 0), ("RD", 0)])
                    dstv = Vt[:, tb, idx * 128:(idx + 2) * 128]
                    if tb % 2 == 0:
                        R.op("act", lambda e, dstv=dstv, bank=bank: e.activation(out=dstv, in_=psb[bank][:, 0:256],
                                                                                 func=ACTF.Identity),
                             reads=[("ps", bank)], writes=["Vt"], toks=[("RD", 0)])
                    else:
                        R.op("dve", lambda e, dstv=dstv, bank=bank: e.tensor_copy(out=dstv, in_=psb[bank][:, 0:256]),
                             reads=[("ps", bank)], writes=["Vt"], toks=[("RD", 0)])
                issue_next_w()
                issue_next_w()
                ci += 2
            elif kind == "Q":
                slot = next_wslot()
                b0, b1, _ = inproj(slot, None)
                gg, j = idx % 4, idx // 4
                R.op("act", lambda e, gg=gg, b0=b0: e.activation(out=QT[:, gg, 0:512], in_=psb[b0][:, :],
                                                                 func=ACTF.Identity),
                     reads=[("ps", b0)], writes=[("QT", j)], toks=[("RD", 1)])
                R.op("dve", lambda e, gg=gg, b1=b1: e.tensor_copy(out=QT[:, gg, 512:1024], in_=psb[b1][:, :]),
                     reads=[("ps", b1)], writes=[("QT", j)], toks=[("RD", 1)])
                ci += 1
            elif kind == "AZ":
                slot = next_wslot()
                b0, b1, _ = inproj(slot, None)
                gg, j = idx % 4, idx // 4
                for hf, bk in ((0, b0), (1, b1)):
                    R.op("act", lambda e, hf=hf, bk=bk: e.activation(out=thz[:, 0], in_=psb[bk][:, :],
                                                                     func=ACTF.Tanh, scale=0.5),
                         reads=[("ps", bk)], writes=[("thz", 0)], toks=[("RB", 1)])
                    R.op("dve", lambda e, hf=hf, bk=bk, gg=gg: e.scalar_tensor_tensor(
                        out=siluz[:, gg, hf * 512:(hf + 1) * 512], in0=thz[:, 0], scalar=1.0, in1=psb[bk][:, :],
                        op0=ALU.add, op1=ALU.mult),
                        reads=[("thz", 0), ("ps", bk)], writes=[("siluz", j)], toks=[("RB", 1), ("RD", 1)])
                ci += 1
                if gg == 3:
                    attention(j)
            elif kind == "H":
                c = idx
                if cv_next[0] < NCV:
                    issue_wout_convert(cv_next[0])
                    cv_next[0] += 1
                slot = next_wslot()
                b0, b1, (hps, hb) = inproj(slot, "edge")
                R.op("act", lambda e, b0=b0: e.activation(out=Hs[:, 1:513], in_=psb[b0][:, :], func=ACTF.Identity),
                     reads=[("ps", b0)], writes=["Hs"], toks=[("RD", 2)])
                R.op("act", lambda e, b1=b1: e.activation(out=Hs[:, 513:1025], in_=psb[b1][:, :], func=ACTF.Identity),
                     reads=[("ps", b1)], writes=["Hs"], toks=[("RD", 2)])
                R.op("act", lambda e, hps=hps: e.activation(out=Hs[:, 0:1], in_=hps[:, 0:1], func=ACTF.Identity),
                     reads=[("ps", hb)], writes=["Hs"], toks=[("RD", 2)])
                R.op("act", lambda e, hps=hps: e.activation(out=Hs[:, 1025:1026], in_=hps[:, 1:2], func=ACTF.Identity),
                     reads=[("ps", hb)], writes=["Hs"], toks=[("RD", 2)])
                slot = next_wslot()
                b0, b1, (hps, hb) = inproj(slot, "edge")
                R.op("dve", lambda e, b0=b0: e.tensor_tensor(out=ub[:, 1:513], in0=psb[b0][:, :], in1=Hs[:, 1:513],
                                                             op=ALU.mult),
                     reads=[("ps", b0), "Hs"], writes=["ub"], toks=[("RD", 2)])
                R.op("dve", lambda e, b1=b1: e.tensor_tensor(out=ub[:, 513:1025], in0=psb[b1][:, :],
                                                             in1=Hs[:, 513:1025], op=ALU.mult),
                     reads=[("ps", b1), "Hs"], writes=["ub"], toks=[("RD", 2)])
                R.op("dve", lambda e, hps=hps: e.scalar_tensor_tensor(out=ub[:, 0:1], in0=hps[:, 0:1],
                                                                      scalar=kb_sb[:, 10:11], in1=Hs[:, 0:1],
                                                                      op0=ALU.mult, op1=ALU.mult),
                     reads=[("ps", hb), "Hs", "kb"], writes=["ub"], toks=[("RD", 2)])
                R.op("dve", lambda e, hps=hps: e.scalar_tensor_tensor(out=ub[:, 1025:1026], in0=hps[:, 1:2],
                                                                      scalar=kb_sb[:, 11:12], in1=Hs[:, 1025:1026],
                                                                      op0=ALU.mult, op1=ALU.mult),
                     reads=[("ps", hb), "Hs", "kb"], writes=["ub"], toks=[("RD", 2)])
                R.op("pool", lambda e, c=c: e.tensor_scalar(out=vb, in0=ub[:, 1:1025], scalar1=cw_sb[:, c, 1:2],
                                                            scalar2=None, op0=ALU.mult),
                     reads=["ub", "cw"], writes=["vb"], toks=[("RD", 2)])
                R.op("dve", lambda e, c=c: e.scalar_tensor_tensor(out=vb, in0=ub[:, 0:1024], scalar=cw_sb[:, c, 0:1],
                                                                  in1=vb, op0=ALU.mult, op1=ALU.add),
                     reads=["ub", "cw", "vb"], writes=["vb"], toks=[("RD", 2)])
                R.op("dve", lambda e, c=c: e.scalar_tensor_tensor(out=vb, in0=ub[:, 2:1026], scalar=cw_sb[:, c, 2:3],
                                                                  in1=vb, op0=ALU.mult, op1=ALU.add),
                     reads=["ub", "cw", "vb"], writes=["vb"], toks=[("RD", 2)])
                slot = next_wslot()
                b0, b1, _ = inproj(slot, None)
                for hf, bk in ((0, b0), (1, b1)):
                    R.op("dve", lambda e, hf=hf, bk=bk: e.tensor_tensor(out=v2b[:, hf * 512:(hf + 1) * 512],
                                                                        in0=psb[bk][:, :],
                                                                        in1=vb[:, hf * 512:(hf + 1) * 512],
                                                                        op=ALU.mult),
                         reads=[("ps", bk), "vb"], writes=[("v2b", hf)], toks=[("RD", 2)])
                slot = next_wslot()
                b0, b1, _ = inproj(slot, None)
                for hf, bk in ((0, b0), (1, b1)):
                    R.op("act", lambda e, hf=hf, bk=bk: e.activation(out=thb[:, hf * 512:(hf + 1) * 512],
                                                                     in_=psb[bk][:, :], func=ACTF.Tanh, scale=0.5),
                         reads=[("ps", bk)], writes=[("thb", hf)], toks=[("RD", 2)])
                    R.op("dve", lambda e, hf=hf, bk=bk: e.scalar_tensor_tensor(
                        out=sb_[:, hf * 512:(hf + 1) * 512], in0=thb[:, hf * 512:(hf + 1) * 512], scalar=1.0,
                        in1=psb[bk][:, :], op0=ALU.add, op1=ALU.mult),
                        reads=[("thb", hf), ("ps", bk)], writes=[("sb", hf)], toks=[("RD", 2)])
                    R.op("pool", lambda e, hf=hf, c=c: e.tensor_tensor(
                        out=ycatT[:, c, hf * 512:(hf + 1) * 512], in0=sb_[:, hf * 512:(hf + 1) * 512],
                        in1=v2b[:, hf * 512:(hf + 1) * 512], op=ALU.mult),
                        reads=[("sb", hf), ("v2b", hf)], writes=[("ycat", c)], toks=[("RD", 2), ("RB", 2)])
                ci += 4
            else:
                raise AssertionError(kind)

        ycat_all = [("ycat", c) for c in range(KC)]
        pcount = [0]
        xcount = [0]
        out_ops = []
        pend_out = []

        def flush_out():
            (dst, src, tl_, r_) = pend_out.pop(0)
            out_ops.append(dma("act", dst, src, s_out[tl_ * 8 + r_], reads=[("z", tl_, r_)], name="out"))

        def final_ln(gi, r):
            pslot = pcount[0] % 2
            pcount[0] += 1
            for pi in (2, 3):
                dma("sp", par[:, pi, pslot], prow[pi, :, r * 512:(r + 1) * 512], s_par[pi][pslot],
                    writes=[("par", pi, pslot)], toks=[("RD", 3)])
            for tl in range(4):
                tslot = tl % 2
                zsl = zbuf[:, tl, r * 512:(r + 1) * 512]
                R.op("act", lambda e, zsl=zsl, tl=tl, tslot=tslot: e.activation(
                    out=t2[:, tslot], in_=zsl, func=ACTF.Identity, scale=rz[:, tl:tl + 1],
                    bias=nmz[:, tl:tl + 1]),
                    reads=[("z", tl, r), ("rz", tl), ("nmz", tl)], writes=[("t2", tslot)], toks=[("RD", 3)])
                R.op("pool" if (gi == 1 and tl % 2 == 0) else "dve", lambda e, tslot=tslot, pslot=pslot: e.tensor_tensor(
                    out=t2[:, tslot], in0=t2[:, tslot], in1=par[:, 2, pslot], op=ALU.mult),
                    reads=[("t2", tslot), ("par", 2, pslot)], writes=[("t2", tslot)], toks=[("RD", 3)])
                R.op("dve", lambda e, zsl=zsl, tslot=tslot, pslot=pslot: e.tensor_tensor(
                    out=zsl, in0=t2[:, tslot], in1=par[:, 3, pslot], op=ALU.add),
                    reads=[("t2", tslot), ("par", 3, pslot)], writes=[("z", tl, r)], toks=[("RD", 3)])
                t0 = (gi * 4 + tl) * 128
                pend_out.append((out[t0:t0 + 128, r * 512:(r + 1) * 512], zsl, tl, r))
                while len(pend_out) > 3:
                    flush_out()

        for gi in range(2 if stop >= 6 else 0):
            for r in range(8):
                if gi == 1 and r < 4:
                    final_ln(0, 2 * r)
                    final_ln(0, 2 * r + 1)
                    if r == 3:
                        while pend_out:
                            flush_out()
                ps_par = r % 2
                banks = [4 * ps_par + tl for tl in range(4)]
                pslot = pcount[0] % 2
                pcount[0] += 1
                for pi in range(2):
                    dma("sp", par[:, pi, pslot], prow[pi, :, r * 512:(r + 1) * 512], s_par[pi][pslot],
                        writes=[("par", pi, pslot)], toks=[("RD", 3)])
                R.op("dve", lambda e, pslot=pslot: e.tensor_single_scalar(out=par[:, 1, pslot], in_=par[:, 1, pslot],
                                                                          scalar=DN_ALPHA, op=ALU.mult),
                     reads=[("par", 1, pslot)], writes=[("par", 1, pslot)], toks=[("RD", 3)])
                for cq in range(4):
                    slot = next_wslot()

                    def mm2(e, slot=slot, cq=cq, banks=banks, gi=gi):
                        ins = None
                        for c8 in range(8):
                            c = cq * 8 + c8
                            for tl in range(4):
                                t0 = (gi * 4 + tl) * 128
                                ins = e.matmul(psb[banks[tl]][:, :], ycatT[:, c, t0:t0 + 128], wring2[:, slot, c8, :],
                                               start=(c == 0), stop=(c == KC - 1))
                        return ins
                    R.op("pe", mm2, reads=[("w", slot)] + ycat_all, writes=[("ps", bk) for bk in banks],
                         toks=[("RB", 2)], name="outproj")
                    issue_next_w()
                for tl in range(4):
                    tcg = 1 + gi * 4 + tl
                    xslot = xcount[0] % 3
                    xcount[0] += 1
                    tslot = xslot % 2
                    dma("sp", xt[:, xslot], x_in[tcg * 128:(tcg + 1) * 128, r * 512:(r + 1) * 512], s_xt[xslot],
                        writes=[("xt", xslot)], toks=[("RD", 3)])
                    R.op("act", lambda e, xslot=xslot, tslot=tslot, tcg=tcg: e.activation(
                        out=t1[:, tslot], in_=xt[:, xslot], func=ACTF.Identity, scale=arstd_all[:, tcg:tcg + 1],
                        bias=namr_all[:, tcg:tcg + 1]),
                        reads=[("xt", xslot), ("arstd", tcg), ("namr", tcg)], writes=[("t1", tslot)],
                        toks=[("RD", 3)])
                    R.op("dve", lambda e, tslot=tslot, pslot=pslot: e.tensor_tensor(
                        out=t1[:, tslot], in0=t1[:, tslot], in1=par[:, 0, pslot], op=ALU.mult),
                        reads=[("t1", tslot), ("par", 0, pslot)], writes=[("t1", tslot)], toks=[("RD", 3)])
                    R.op("dve", lambda e, xslot=xslot, tslot=tslot, pslot=pslot: e.tensor_tensor(
                        out=xa[:, xslot], in0=t1[:, tslot], in1=par[:, 1, pslot], op=ALU.add),
                        reads=[("t1", tslot), ("par", 1, pslot)], writes=[("xa", xslot)], toks=[("RD", 3)])
                    zsl = zbuf[:, tl, r * 512:(r + 1) * 512]
                    R.op("dve", lambda e, zsl=zsl, bk=banks[tl], xslot=xslot: e.tensor_tensor(
                        out=zsl, in0=psb[bk][:, :], in1=xa[:, xslot], op=ALU.add),
                        reads=[("ps", banks[tl]), ("xa", xslot)], writes=[("z", tl, r)], toks=[("RA", 1)])
                    R.op("dve", lambda e, zsl=zsl, tl=tl, r=r: e.bn_stats(out=stz[:, tl, r, :], in_=zsl),
                         reads=[("z", tl, r)], writes=[("stz", tl)])
            for tl in range(4):
                R.op("dve", lambda e, tl=tl: e.bn_aggr(out=mvz[:, tl, :], in_=stz[:, tl]),
                     reads=[("stz", tl)], writes=[("mvz", tl)])
                R.op("dve", lambda e, tl=tl: e.tensor_single_scalar(out=vez[:, tl:tl + 1], in_=mvz[:, tl, 1:2],
                                                                    scalar=LN_EPS, op=ALU.add),
                     reads=[("mvz", tl)], writes=[("vez", tl)])
                R.op("pool", lambda e, tl=tl: e.tensor_tensor(out=rz[:, tl:tl + 1], in0=vez[:, tl:tl + 1], in1=mhalf,
                                                              op=ALU.pow),
                     reads=[("vez", tl), "mhalf"], writes=[("rz", tl)])
                R.op("dve", lambda e, tl=tl: e.tensor_scalar(out=nmz[:, tl:tl + 1], in0=mvz[:, tl, 0:1],
                                                             scalar1=rz[:, tl:tl + 1], scalar2=-1.0,
                                                             op0=ALU.mult, op1=ALU.mult),
                     reads=[("mvz", tl), ("rz", tl)], writes=[("nmz", tl)])
            if gi == 1:
                for r in range(8):
                    final_ln(1, r)
                while pend_out:
                    flush_out()
        R.op("act", None, extra_deps=out_ops, name="fence")

        R.finalize(eng_sems)
        with nc.allow_low_precision("bf16 matmul operands, fp32 PSUM accumulation"):
            with nc.Block() as block:
                @block.sync
                def _(e):
                    R.emit_engine("sp", e)

                @block.scalar
                def _(e):
                    R.emit_engine("act", e)

                @block.gpsimd
                def _(e):
                    R.emit_engine("pool", e)

                @block.vector
                def _(e):
                    R.emit_engine("dve", e)

                @block.tensor
                def _(e):
                    R.emit_engine("pe", e)
    return nc


_PROG = {}


def _get_prog():
    if "nc" not in _PROG:
        _PROG["nc"] = build_program()
    return _PROG["nc"]


def make_in_maps(x, emb_ln_g, emb_ln_b, w_in, conv_w, sink, w_out, ln_g, ln_b):
    x = np.asarray(x, dtype=np.float32)
    x2 = x.reshape(SEQ, D_MODEL)
    xp = np.zeros((SEQ + 256, D_MODEL), np.float32)
    xp[128:128 + SEQ] = x2
    w_in2 = np.ascontiguousarray(np.asarray(w_in, np.float32).reshape(D_MODEL, PROJ))
    w_out2 = np.ascontiguousarray(np.asarray(w_out, np.float32).reshape(D_MODEL, D_MODEL))
    eg = np.asarray(emb_ln_g, np.float32).reshape(D_MODEL)
    eb = np.asarray(emb_ln_b, np.float32).reshape(D_MODEL)
    lg = np.asarray(ln_g, np.float32).reshape(D_MODEL)
    lb = np.asarray(ln_b, np.float32).reshape(D_MODEL)
    gbT = np.ascontiguousarray(np.concatenate([eg.reshape(KC, 128).T, eb.reshape(KC, 128).T], axis=1))
    cw = np.asarray(conv_w, np.float32).reshape(3, 16, 128)
    cwT = np.ascontiguousarray(cw.transpose(2, 1, 0).reshape(128, 48))
    sinkb = np.ascontiguousarray(np.broadcast_to(np.asarray(sink, np.float32).reshape(1, 16), (128, 16)))
    prow = np.ascontiguousarray(np.broadcast_to(np.stack([eg, eb, lg, lb])[:, None, :], (4, 128, D_MODEL)))
    in_maps = []
    for c in range(NCORES):
        kb = np.zeros((128, 12), np.float32)
        kb[:, 10] = 1.0
        kb[:, 11] = 1.0
        if c == 0:
            kb[:, 0] = NEGBIG
            kb[:, 10] = 0.0
        if c == NCORES - 1:
            kb[:, 9] = NEGBIG
            kb[:, 11] = 0.0
        in_maps.append({
            "x_in": np.ascontiguousarray(xp[c * TOK:c * TOK + TOK + 256]),
            "w_in": w_in2, "w_out": w_out2, "gbT": gbT, "cwT": cwT, "sinkb": sinkb, "kbias": kb, "prow": prow,
        })
    return in_maps


def kernel(x, emb_ln_g, emb_ln_b, w_in, conv_w, sink, w_out, ln_g, ln_b):
    in_maps = make_in_maps(x, emb_ln_g, emb_ln_b, w_in, conv_w, sink, w_out, ln_g, ln_b)
    nc = _get_prog()
    res = run_bass_kernel_spmd(nc, in_maps, core_ids=list(range(NCORES)))
    outs = [np.asarray(r["out"], dtype=np.float32) for r in res.results]
    return np.concatenate(outs, axis=0).reshape(1, SEQ, D_MODEL)
```

```python
import contextlib
import numpy as np
import concourse.bass as bass
import concourse.mybir as mybir
from concourse.bass_utils import run_bass_kernel_spmd

F32 = mybir.dt.float32
BF16 = mybir.dt.bfloat16
ALU = mybir.AluOpType
ACTF = mybir.ActivationFunctionType

NCORES = 8
D_MODEL = 4096
SEQ = 8192
TOK = SEQ // NCORES
NTC = TOK // 128
KC = D_MODEL // 128
CONV_W = 2048
HD = 128
OFF_CB, OFF_CC, OFF_CH, OFF_CZ = 0, 2048, 4096, 6144
OFF_Q, OFF_K, OFF_V, OFF_AZ = 8192, 10240, 10752, 11264
PROJ = 13312
DN_ALPHA = (2.0 * 1) ** 0.25
LN_EPS = 1e-5
SCALE = HD ** -0.5
NEGBIG = -30000.0
NW = 3
NFILL = 6
import os as _os
SKIP = set(_os.environ.get("KSKIP", "").split(","))


class Op:
    __slots__ = ("eng", "emit", "deps", "sig", "tick", "csem", "is_dma", "uid", "name")


class Rec:
    ENGS = ("sp", "act", "pool", "dve", "pe")

    def __init__(self):
        self.ops = {e: [] for e in self.ENGS}
        self.keyw = {}
        self.keyr = {}
        self.tok = {}
        self.uid = 0
        self.dma_counts = {}

    def op(self, eng, emit, reads=(), writes=(), toks=(), dma_sem=None, name="", extra_deps=()):
        if SKIP and ((eng in SKIP) or (name and name.split(" ")[0] in SKIP)):
            emit = None if dma_sem is None else emit
            if dma_sem is not None:
                return None
        ps_r = [k for k in reads if isinstance(k, tuple) and k[0] == "ps"]
        if ps_r:
            reads = [k for k in reads if not (isinstance(k, tuple) and k[0] == "ps")]
            writes = list(writes) + [k for k in ps_r if k not in writes]
        o = Op()
        o.eng, o.emit, o.deps, o.sig, o.tick = eng, emit, set(), False, None
        o.is_dma = dma_sem is not None
        o.csem = dma_sem
        o.uid = self.uid
        o.name = name
        self.uid += 1
        ek = ("dma", o.uid) if o.is_dma else eng
        for k in reads:
            for w in self.keyw.get(k, {}).values():
                o.deps.add(w)
        for k in writes:
            for w in self.keyw.get(k, {}).values():
                o.deps.add(w)
            for r in self.keyr.get(k, {}).values():
                o.deps.add(r)
        for (tname, ph) in toks:
            st = self.tok.setdefault(tname, [ph, {}, {}])
            assert ph >= st[0], (tname, ph, st[0])
            if ph > st[0]:
                st[0], st[2], st[1] = ph, st[1], {}
            for p in st[2].values():
                o.deps.add(p)
        for d in extra_deps:
            o.deps.add(d)
        o.deps.discard(o)
        if eng == "pe":
            o.deps = {d for d in o.deps if d.is_dma or d.eng != "pe"}
        for d in o.deps:
            d.sig = True
        for k in reads:
            self.keyr.setdefault(k, {})[ek] = o
        for k in writes:
            if self.keyr.get(k):
                self.keyw[k] = {ek: o}
                self.keyr[k] = {}
            else:
                self.keyw.setdefault(k, {})[ek] = o
        for (tname, ph) in toks:
            self.tok[tname][1][ek] = o
        if o.is_dma:
            c = self.dma_counts.get(id(dma_sem), 0) + 16
            self.dma_counts[id(dma_sem)] = c
            o.tick = c
            o.sig = True
        self.ops[eng].append(o)
        return o

    def finalize(self, eng_sems):
        for e in self.ENGS:
            n = 0
            for o in self.ops[e]:
                if o.is_dma:
                    continue
                if o.sig:
                    n += 1
                    o.tick = n
                    o.csem = eng_sems[e]

    def emit_engine(self, e, handle):
        waited = {}
        for o in self.ops[e]:
            for d in sorted(o.deps, key=lambda t: t.uid):
                assert d.tick is not None, (d.name, o.name)
                k = id(d.csem)
                if waited.get(k, 0) >= d.tick:
                    continue
                handle.wait_ge(d.csem, d.tick)
                waited[k] = d.tick
            if o.emit is None:
                if o.sig and not o.is_dma:
                    handle.nop().then_inc(o.csem, 1)
                continue
            ins = o.emit(handle)
            if o.is_dma:
                ins.then_inc(o.csem, 16)
            elif o.sig:
                ins.then_inc(o.csem, 1)


def build_program(stop=99):
    nc = bass.Bass("TRN2", target_bir_lowering=False)
    x_in = nc.dram_tensor("x_in", [TOK + 256, D_MODEL], F32, kind="ExternalInput").ap()
    w_in = nc.dram_tensor("w_in", [D_MODEL, PROJ], F32, kind="ExternalInput").ap()
    w_out = nc.dram_tensor("w_out", [D_MODEL, D_MODEL], F32, kind="ExternalInput").ap()
    gbT = nc.dram_tensor("gbT", [128, 2 * KC], F32, kind="ExternalInput").ap()
    cwT = nc.dram_tensor("cwT", [128, 16 * 3], F32, kind="ExternalInput").ap()
    sinkb = nc.dram_tensor("sinkb", [128, 16], F32, kind="ExternalInput").ap()
    kbias_d = nc.dram_tensor("kbias", [128, 12], F32, kind="ExternalInput").ap()
    prow = nc.dram_tensor("prow", [4, 128, D_MODEL], F32, kind="ExternalInput").ap()
    out = nc.dram_tensor("out", [TOK, D_MODEL], F32, kind="ExternalOutput").ap()

    wob = nc.dram_tensor("wob", [D_MODEL, D_MODEL], BF16).ap()
    w_in_v = w_in.rearrange("(kc p) f -> p kc f", p=128)
    w_out_v = wob.rearrange("(c p) d -> p c d", p=128)

    R = Rec()
    es = contextlib.ExitStack()
    with es:
        KB = 1024
        ARENA_BYTES = 205 * KB
        arena = es.enter_context(nc.sbuf_tensor("arena", [128, ARENA_BYTES // 2], BF16))

        def carve(off_b, nbytes, dt, shape=None):
            a = arena[:, off_b // 2:(off_b + nbytes) // 2]
            if dt == F32:
                a = a.bitcast(F32)
            if shape is not None and len(shape) > 1:
                names = " ".join("a%d" % i for i in range(len(shape)))
                kw = {"a%d" % i: s for i, s in enumerate(shape[:-1])}
                a = a.rearrange("p (%s) -> p %s" % (names, names), **kw)
            return a

        A0 = 0
        B0 = 64 * KB
        C0 = 128 * KB
        D0 = C0 + NW * 8 * KB
        G0 = D0 + 44 * KB
        hT = carve(A0, 64 * KB, BF16, [KC, TOK])
        zbuf = carve(A0, 64 * KB, F32, [4, D_MODEL])
        ycatT = carve(B0, 64 * KB, BF16, [KC, TOK])
        bt_hi = carve(B0, 1536, BF16, [3, 256])
        bt_lo = carve(B0 + 1536, 1536, BF16, [3, 256])
        identb = carve(B0 + 3072, 2048, BF16, [8, 128])
        btf = carve(B0 + 5120, 3072, F32, [3, 256])
        pT = carve(B0 + 12 * KB, 6 * KB, BF16, [2, 3, 512])
        densb = carve(B0 + 24 * KB, 2 * KB, F32, [512])
        o1 = carve(B0 + 26 * KB, 2 * KB, F32, [512])
        pvsb = carve(B0 + 28 * KB, 2 * KB, F32, [512])
        thz = carve(B0 + 30 * KB, 2 * KB, F32, [1, 512])
        xs = carve(B0, 64 * KB, F32, [4, D_MODEL])
        wring = carve(C0, NW * 8 * KB, BF16, [NW, KC, 128])
        wring2 = carve(C0, NW * 8 * KB, BF16, [NW, 8, 512])
        hTh = carve(D0, 16 * KB, BF16, [KC, 256])
        siluz = carve(D0, 16 * KB, F32, [4, TOK])
        KT = carve(D0 + 16 * KB, 10 * KB, BF16, [4, 10 * 128])
        Vt = carve(D0 + 26 * KB, 10 * KB, BF16, [10, 512])
        QT = carve(D0 + 36 * KB, 8 * KB, BF16, [4, TOK])
        CW = 1032
        Hs = carve(D0, CW * 4, F32, [CW])
        ub = carve(D0 + CW * 4, CW * 4, F32, [CW])
        vb = carve(D0 + 2 * CW * 4, 4 * KB, F32, [TOK])
        v2b = carve(D0 + 2 * CW * 4 + 4 * KB, 4 * KB, F32, [TOK])
        thb = carve(D0 + 2 * CW * 4 + 8 * KB, 4 * KB, F32, [TOK])
        sb_ = carve(D0 + 2 * CW * 4 + 12 * KB, 4 * KB, F32, [TOK])
        par = carve(D0, 16 * KB, F32, [4, 2, 512])
        xt = carve(D0 + 16 * KB, 6 * KB, F32, [3, 512])
        t1 = carve(D0 + 22 * KB, 4 * KB, F32, [2, 512])
        xa = carve(D0 + 26 * KB, 6 * KB, F32, [3, 512])
        t2 = carve(D0 + 32 * KB, 4 * KB, F32, [2, 512])
        g = [G0]

        def galloc(nbytes, dt, shape=None):
            nb = (nbytes + 63) // 64 * 64
            a = carve(g[0], nb, dt, None)
            n_el = nbytes // (4 if dt == F32 else 2)
            a = a[:, 0:n_el]
            if shape is not None and len(shape) > 1:
                names = " ".join("a%d" % i for i in range(len(shape)))
                kw = {"a%d" % i: s for i, s in enumerate(shape[:-1])}
                a = a.rearrange("p (%s) -> p %s" % (names, names), **kw)
            g[0] += nb
            return a

        gb_sb = galloc(2 * KC * 4, F32)
        cw_sb = galloc(48 * 4, F32, [16, 3])
        sink2 = galloc(16 * 4, F32)
        kb_sb = galloc(12 * 4, F32)
        ident = galloc(128 * 4, F32)
        ones_b = galloc(128 * 2, BF16)
        hTe = galloc(KC * 2 * 2, BF16, [KC, 2])
        bnst = galloc(2 * 8 * 6 * 4, F32, [2, 8, 6])
        mv = galloc(2 * 2 * 4, F32, [2, 2])
        mu_all = galloc(10 * 4, F32)
        rstd_all = galloc(10 * 4, F32)
        arstd_all = galloc(10 * 4, F32)
        nmr_all = galloc(10 * 4, F32)
        namr_all = galloc(10 * 4, F32)
        nmz = galloc(4 * 4, F32)
        veps = galloc(10 * 4, F32)
        mhalf = galloc(4, F32)
        stz = galloc(4 * 8 * 6 * 4, F32, [4, 8, 6])
        mvz = galloc(4 * 2 * 4, F32, [4, 2])
        vez = galloc(4 * 4, F32)
        rz = galloc(4 * 4, F32)
        dD = carve(D0 + 36 * KB, 1536, F32, [3, 128])
        absD = carve(D0 + 36 * KB + 1536, 1536, F32, [3, 128])
        m01 = carve(D0 + 36 * KB + 3072, 1536, F32, [3, 128])
        tmpE = carve(D0 + 36 * KB + 4608, 1536, F32, [3, 128])
        assert g[0] <= ARENA_BYTES, g[0]

        psb = [es.enter_context(nc.psum_tensor("ps%d" % i, [128, 512], F32)) for i in range(8)]

        def sem(name):
            return es.enter_context(nc.semaphore(name))

        eng_sems = {e: sem("s_" + e) for e in Rec.ENGS}
        s_const = [sem("d_const%d" % i) for i in range(4)]
        s_xs = [sem("d_xs%d" % i) for i in range(4)]
        s_w = [sem("d_w%d" % i) for i in range(NW)]
        s_par = [[sem("d_par%d_%d" % (a, i)) for i in range(2)] for a in range(4)]
        s_xt = [sem("d_xt%d" % i) for i in range(3)]
        s_out = [sem("d_out%d" % i) for i in range(32)]

        def dma(eng, out_ap, in_ap, dsem, reads=(), writes=(), toks=(), name=""):
            return R.op(eng, lambda e: e.dma_start(out=out_ap, in_=in_ap), reads=reads, writes=writes,
                        toks=toks, dma_sem=dsem, name=name)

        dma("sp", gb_sb, gbT[:, :], s_const[0], writes=["gb"])
        dma("sp", cw_sb.rearrange("p a b -> p (a b)"), cwT[:, :], s_const[1], writes=["cw"])
        dma("sp", sink2, sinkb[:, :], s_const[2], writes=["sink"])
        dma("sp", kb_sb, kbias_d[:, :], s_const[3], writes=["kb"])

        R.op("pool", lambda e: e.memset(ones_b, 1.0), writes=["ones"])
        R.op("pool", lambda e: e.memset(mhalf, -0.5), writes=["mhalf"])
        R.op("pool", lambda e: e.iota(ident, pattern=[[1, 128]], base=0, channel_multiplier=-1,
                                      allow_small_or_imprecise_dtypes=True), writes=["ident"])
        R.op("pool", lambda e: e.tensor_single_scalar(out=ident, in_=ident, scalar=0.0, op=ALU.is_equal),
             reads=["ident"], writes=["ident"])
        R.op("act", lambda e: e.activation(out=sink2, in_=sink2, func=ACTF.Exp), reads=["sink"], writes=["sink"],
             name="actsink")
        R.op("dve", lambda e: e.tensor_single_scalar(out=sink2, in_=sink2, scalar=2.0, op=ALU.mult),
             reads=["sink"], writes=["sink"])
        R.op("dve", lambda e: e.tensor_single_scalar(out=cw_sb, in_=cw_sb, scalar=0.5, op=ALU.mult),
             reads=["cw"], writes=["cw"])

        wcount = [0]

        def w_dma_in(col):
            slot = wcount[0] % NW
            wcount[0] += 1
            dma("pool", wring[:, slot], w_in_v[:, :, col:col + 128], s_w[slot], writes=[("w", slot)],
                name="w_in %d" % col)
            return slot

        def w_dma_out(r, cq):
            slot = wcount[0] % NW
            wcount[0] += 1
            dma("pool", wring2[:, slot], w_out_v[:, cq * 8:(cq + 1) * 8, r * 512:(r + 1) * 512], s_w[slot],
                reads=["wob"], writes=[("w", slot)], name="w_out %d %d" % (r, cq))
            return slot

        sched = []
        for j in range(4):
            sched.append(("K", j, OFF_K + 128 * j))
        for j in range(4):
            sched.append(("V", j, OFF_V + 128 * j))
        for j in range(4):
            for gg in range(4):
                sched.append(("Q", 4 * j + gg, OFF_Q + 128 * (4 * j + gg)))
            for gg in range(4):
                sched.append(("AZ", 4 * j + gg, OFF_AZ + 128 * (4 * j + gg)))
        for c in range(16):
            sched.append(("H", c, OFF_CH + 128 * c))
            sched.append(("C", c, OFF_CC + 128 * c))
            sched.append(("B", c, OFF_CB + 128 * c))
            sched.append(("Z", c, OFF_CZ + 128 * c))
        wq = [("in", s[2]) for s in sched]
        for gi in range(2):
            for r in range(8):
                for cq in range(4):
                    wq.append(("out", r, cq))
        wq_pos = [0]
        wslots = []

        def issue_next_w():
            if wq_pos[0] >= len(wq):
                return
            it = wq[wq_pos[0]]
            wq_pos[0] += 1
            if it[0] == "in":
                wslots.append(w_dma_in(it[1]))
            else:
                wslots.append(w_dma_out(it[1], it[2]))

        for _ in range(NW):
            issue_next_w()
        wuse = [0]

        def next_wslot():
            s = wslots[wuse[0]]
            wuse[0] += 1
            return s

        tp_bank = [0]

        def p0_load_stats(tc, slot):
            bs = tc % 2
            dma("sp", xs[:, slot], x_in[tc * 128:(tc + 1) * 128, :], s_xs[slot], writes=[("xs", slot)],
                toks=[("RB", 0)], name="x load")
            for q in range(8):
                R.op("dve", lambda e, q=q, slot=slot, bs=bs: e.bn_stats(out=bnst[:, bs, q, :],
                                                                       in_=xs[:, slot, q * 512:(q + 1) * 512]),
                     reads=[("xs", slot)], writes=[("bnst", bs)], toks=[("RB", 0)])
            R.op("dve", lambda e, bs=bs: e.bn_aggr(out=mv[:, bs, :], in_=bnst[:, bs]),
                 reads=[("bnst", bs)], writes=[("mv", bs)])
            R.op("dve", lambda e, bs=bs, tc=tc: e.tensor_single_scalar(out=veps[:, tc:tc + 1], in_=mv[:, bs, 1:2],
                                                                       scalar=LN_EPS, op=ALU.add),
                 reads=[("mv", bs)], writes=[("veps", tc)])
            R.op("dve", lambda e, bs=bs, tc=tc: e.tensor_copy(out=mu_all[:, tc:tc + 1], in_=mv[:, bs, 0:1]),
                 reads=[("mv", bs)], writes=[("mu", tc)])
            R.op("pool", lambda e, tc=tc: e.tensor_tensor(out=rstd_all[:, tc:tc + 1], in0=veps[:, tc:tc + 1],
                                                          in1=mhalf, op=ALU.pow),
                 reads=[("veps", tc), "mhalf"], writes=[("rstd", tc)])
            R.op("dve", lambda e, tc=tc: e.tensor_scalar(out=nmr_all[:, tc:tc + 1], in0=mu_all[:, tc:tc + 1],
                                                         scalar1=rstd_all[:, tc:tc + 1], scalar2=-1.0,
                                                         op0=ALU.mult, op1=ALU.mult),
                 reads=[("mu", tc), ("rstd", tc)], writes=[("nmr", tc)])

        def p0_norm(tc, slot):
            R.op("act", lambda e, slot=slot, tc=tc: e.activation(out=xs[:, slot], in_=xs[:, slot], func=ACTF.Identity,
                                                                 scale=rstd_all[:, tc:tc + 1],
                                                                 bias=nmr_all[:, tc:tc + 1]),
                 reads=[("xs", slot), ("rstd", tc), ("nmr", tc)], writes=[("xs", slot)], toks=[("RB", 0)],
                 name="actnorm")

        def p0_small(tc):
            R.op("dve", lambda e, tc=tc: e.tensor_single_scalar(out=namr_all[:, tc:tc + 1], in_=nmr_all[:, tc:tc + 1],
                                                                scalar=DN_ALPHA, op=ALU.mult),
                 reads=[("nmr", tc)], writes=[("namr", tc)])
            R.op("dve", lambda e, tc=tc: e.tensor_single_scalar(out=arstd_all[:, tc:tc + 1],
                                                                in_=rstd_all[:, tc:tc + 1], scalar=DN_ALPHA,
                                                                op=ALU.mult),
                 reads=[("rstd", tc)], writes=[("arstd", tc)])

        def p0_transpose(tca, tcb, sa, sb2):
            for kq in range(16):
                bank = tp_bank[0] % 4
                tp_bank[0] += 1

                def tr(e, kq=kq, bank=bank):
                    ins = None
                    for k2 in range(2):
                        kc = kq * 2 + k2
                        for ti, sl in ((0, sa), (1, sb2)):
                            c0 = k2 * 256 + ti * 128
                            ins = e.transpose(psb[bank][:, c0:c0 + 128], xs[:, sl, kc * 128:(kc + 1) * 128], ident)
                    return ins
                R.op("pe", tr, reads=[("xs", sa), ("xs", sb2), "ident"], writes=[("ps", bank)], toks=[("RB", 0)])
                for k2 in range(2):
                    kc = kq * 2 + k2
                    if tca == 0:
                        dst = hTh[:, kc, 0:256]
                        wk = ("hTh", kc)
                    else:
                        dst = hT[:, kc, (tca - 1) * 128:(tca + 1) * 128]
                        wk = ("hT", kc, tca - 1)
                    wk2 = ("hT", kc, tcb - 1) if tca != 0 else ("hTh", kc)
                    src = psb[bank][:, k2 * 256:(k2 + 1) * 256]
                    if bank != 3:
                        R.op("act", lambda e, dst=dst, src=src, kc=kc: e.activation(
                            out=dst, in_=src, func=ACTF.Identity, scale=gb_sb[:, kc:kc + 1],
                            bias=gb_sb[:, KC + kc:KC + kc + 1]),
                            reads=[("ps", bank), "gb"], writes=[wk, wk2], toks=[("RD", 0)], name="actevac")
                    else:
                        R.op("dve", lambda e, dst=dst, src=src, kc=kc: e.tensor_scalar(
                            out=dst, in0=src, scalar1=gb_sb[:, kc:kc + 1], scalar2=gb_sb[:, KC + kc:KC + kc + 1],
                            op0=ALU.mult, op1=ALU.add),
                            reads=[("ps", bank), "gb"], writes=[wk, wk2], toks=[("RD", 0)])

        pairs = [(0, 9), (1, 2), (3, 4), (5, 6), (7, 8)]
        p0_load_stats(0, 0)
        p0_load_stats(9, 1)
        p0_norm(0, 0)
        p0_norm(9, 1)
        for pi_, (ta, tb_) in enumerate(pairs):
            sa, sb2 = 2 * (pi_ % 2), 2 * (pi_ % 2) + 1
            if pi_ + 1 < len(pairs):
                na, nb = pairs[pi_ + 1]
                p0_load_stats(na, 2 * ((pi_ + 1) % 2))
                p0_load_stats(nb, 2 * ((pi_ + 1) % 2) + 1)
            p0_transpose(ta, tb_, sa, sb2)
            p0_small(ta)
            p0_small(tb_)
            if pi_ + 1 < len(pairs):
                p0_norm(na, 2 * ((pi_ + 1) % 2))
                p0_norm(nb, 2 * ((pi_ + 1) % 2) + 1)

        R.op("pool", lambda e: e.tensor_copy(out=hTe, in_=hTh[:, :, 127:129]),
             reads=[("hTh", kc) for kc in range(KC)], writes=["hTe"], toks=[("RD", 0)])

        if stop <= 0:
            sched_stop = True
        R.op("pool", lambda e: e.iota(dD.rearrange("p a b -> p (a b)"), pattern=[[-128, 3], [1, 128]], base=128,
                                      channel_multiplier=-1, allow_small_or_imprecise_dtypes=True),
             writes=["dD"], toks=[("RD", 0)])
        R.op("act", lambda e: e.activation(out=absD, in_=dD, func=ACTF.Abs),
             reads=["dD"], writes=["absD"], toks=[("RD", 0)], name="actabs")
        R.op("dve", lambda e: e.tensor_single_scalar(out=m01, in_=absD, scalar=128.0, op=ALU.is_le),
             reads=["absD"], writes=["m01"], toks=[("RD", 0)])
        BIGB = 340000.0
        R.op("dve", lambda e: e.tensor_scalar(out=tmpE, in0=m01, scalar1=BIGB, scalar2=-BIGB, op0=ALU.mult,
                                              op1=ALU.add),
             reads=["m01"], writes=["tmpE"], toks=[("RD", 0)])
        for gg in range(2):
            slope = 2.0 ** (-8.0 * (gg + 1) / 16.0)
            dst = btf[:, :, gg * 128:(gg + 1) * 128]
            R.op("dve", lambda e, dst=dst, sl=slope: e.tensor_single_scalar(out=dst, in_=absD, scalar=-sl / SCALE,
                                                                            op=ALU.mult),
                 reads=["absD"], writes=["btf"], toks=[("RD", 0), ("RB", 1)])
            R.op("dve", lambda e, dst=dst: e.tensor_tensor(out=dst, in0=dst, in1=m01, op=ALU.mult),
                 reads=["btf", "m01"], writes=["btf"], toks=[("RD", 0), ("RB", 1)])
            R.op("dve", lambda e, dst=dst: e.tensor_tensor(out=dst, in0=dst, in1=tmpE, op=ALU.add),
                 reads=["btf", "tmpE"], writes=["btf"], toks=[("RD", 0), ("RB", 1)])
        R.op("dve", lambda e: e.tensor_copy(out=bt_hi, in_=btf), reads=["btf"], writes=["bt_hi"], toks=[("RB", 1)])
        R.op("dve", lambda e: e.tensor_tensor(out=bt_lo, in0=btf, in1=bt_hi, op=ALU.subtract),
             reads=["btf", "bt_hi"], writes=["bt_lo"], toks=[("RB", 1)])
        for k in range(8):
            R.op("dve", lambda e, k=k: e.tensor_single_scalar(out=identb[:, k, :], in_=ident, scalar=2.0 ** (-k),
                                                              op=ALU.mult),
                 reads=["ident"], writes=["identb"], toks=[("RB", 1)])
        hT_all = [("hT", kc, t) for kc in range(KC) for t in range(NTC)]
        hTh_all = [("hTh", kc) for kc in range(KC)]

        nchunk = [0]

        def inproj(slot, halo):
            par_ = nchunk[0] % 2
            nchunk[0] += 1
            b0, b1 = 2 * par_, 2 * par_ + 1
            if halo == "kv":
                hN = 256
            elif halo == "edge":
                hN = 2
            else:
                hN = 0
            hb = 4 + par_
            hps = psb[hb][:, 0:hN] if hN else None

            def mm(e):
                ins = None
                for kc in range(KC):
                    w = wring[:, slot, kc, :]
                    st, sp_ = (kc == 0), (kc == KC - 1)
                    e.matmul(psb[b0][:, :], w, hT[:, kc, 0:512], start=st, stop=sp_)
                    ins = e.matmul(psb[b1][:, :], w, hT[:, kc, 512:1024], start=st, stop=sp_)
                    if halo == "kv":
                        ins = e.matmul(hps, w, hTh[:, kc, :], start=st, stop=sp_)
                    elif halo == "edge":
                        ins = e.matmul(hps, w, hTe[:, kc, :], start=st, stop=sp_)
                return ins
            reads = [("w", slot)] + hT_all
            writes = [("ps", b0), ("ps", b1)]
            toks = [("RA", 0)]
            if halo == "kv":
                reads += hTh_all
                writes.append(("ps", hb))
                toks.append(("RD", 0))
            elif halo == "edge":
                reads.append("hTe")
                writes.append(("ps", hb))
            R.op("pe", mm, reads=reads, writes=writes, toks=toks, name="inproj")
            issue_next_w()
            return b0, b1, (hps, hb)

        def attention(j):
            NS = 8

            def qk(i):
                banks = (5, 6, 7)

                def f(e):
                    ins = None
                    for b in range(3):
                        kb = i + b
                        e.matmul(psb[banks[b]][:, :], KT[:, j, kb * 128:(kb + 1) * 128],
                                 QT[:, :, i * 128:(i + 1) * 128], start=True, stop=False)
                        for gp in range(2):
                            k = 2 * j + gp
                            o_ = psb[banks[b]][:, gp * 256:(gp + 1) * 256]
                            e.matmul(o_, identb[:, k, :], bt_hi[:, b, :], start=False, stop=False)
                            ins = e.matmul(o_, identb[:, k, :], bt_lo[:, b, :], start=False, stop=(gp == 1))
                    return ins
                R.op("pe", f, reads=["KT", ("QT", j), "identb", "bt_hi", "bt_lo"],
                     writes=[("ps", bk) for bk in banks], toks=[("RD", 1), ("RB", 1)])
                for b in range(3):
                    kb = i + b
                    R.op("act", lambda e, b=b, kb=kb, bk=banks[b]: e.activation(
                        out=pT[:, i % 2, b], in_=psb[bk][:, :], func=ACTF.Exp, scale=SCALE,
                        bias=kb_sb[:, kb:kb + 1]),
                        reads=[("ps", banks[b]), "kb"], writes=[("pT", i % 2, b)], toks=[("RB", 1)])

            def pvd(i):
                def f(e):
                    ins = None
                    for b in range(3):
                        kb = i + b
                        e.matmul(psb[3][:, :], Vt[:, kb, j * 128:(j + 1) * 128], pT[:, i % 2, b],
                                 start=(b == 0), stop=(b == 2))
                        ins = e.matmul(psb[4][:, :], ones_b, pT[:, i % 2, b], start=(b == 0), stop=(b == 2))
                    return ins
                R.op("pe", f, reads=["Vt", "ones"] + [("pT", i % 2, b) for b in range(3)],
                     writes=[("ps", 3), ("ps", 4)], toks=[("RD", 1), ("RB", 1)])
                sk = sink2[:, 4 * j:4 * j + 4].unsqueeze(2).broadcast_to([128, 4, 128])
                R.op("dve", lambda e: e.scalar_tensor_tensor(
                    out=densb.rearrange("p (a b) -> p a b", a=4), in0=psb[4][:, :].rearrange("p (a b) -> p a b", a=4),
                    scalar=2.0, in1=sk, op0=ALU.mult, op1=ALU.add),
                    reads=[("ps", 4), "sink"], writes=["densb"], toks=[("RB", 1)])
                R.op("act", lambda e: e.activation(out=pvsb, in_=psb[3][:, :], func=ACTF.Identity),
                     reads=[("ps", 3)], writes=["pvsb"], toks=[("RB", 1)])
                R.op("act", lambda e: e.activation(out=densb, in_=densb, func=ACTF.Ln), reads=["densb"],
                     writes=["densb"], toks=[("RB", 1)])
                R.op("act", lambda e: e.activation(out=densb, in_=densb, func=ACTF.Exp, scale=-1.0), reads=["densb"],
                     writes=["densb"], toks=[("RB", 1)])
                R.op("dve", lambda e: e.tensor_tensor(out=o1, in0=pvsb, in1=densb, op=ALU.mult),
                     reads=["pvsb", "densb"], writes=["o1"], toks=[("RB", 1)])
                R.op("dve", lambda e: e.tensor_tensor(
                    out=ycatT[:, 16 + 4 * j:16 + 4 * j + 4, i * 128:(i + 1) * 128],
                    in0=o1.rearrange("p (a b) -> p a b", a=4), in1=siluz[:, :, i * 128:(i + 1) * 128], op=ALU.mult),
                    reads=["o1", ("siluz", j)], writes=[("ycat", 16 + 4 * j + gg) for gg in range(4)],
                    toks=[("RB", 1), ("RD", 1)])

            def filler(n):
                def f(e):
                    ins = None
                    for _ in range(n):
                        ins = e.matmul(psb[0][:, :], ones_b, pT[:, 0, 0], start=True, stop=True)
                    return ins
                R.op("pe", f, reads=["ones"], writes=[("ps", 0)], toks=[("RB", 1)])

            for i in range(NS):
                qk(i)
                if i >= 1:
                    pvd(i - 1)
            pvd(NS - 1)

        s_cv = sem("d_cv")
        NCV = 16
        RCV = D_MODEL // NCV

        def issue_wout_convert(i):
            dma("pool", wob[i * RCV:(i + 1) * RCV, :], w_out[i * RCV:(i + 1) * RCV, :], s_cv,
                writes=(["wob"] if i == NCV - 1 else []), name="cvt %d" % i)
        cv_next = [0]
        ci = 0
        nlim = {0: 0, 1: 4, 2: 8, 3: 16, 4: 40}.get(stop, len(sched))
        while ci < min(len(sched), nlim):
            kind, idx, col = sched[ci]
            if kind == "K":
                slot = next_wslot()
                b0, b1, (hps, hb) = inproj(slot, "kv")
                j = idx
                R.op("act", lambda e, j=j, b0=b0: e.activation(out=KT[:, j, 128:640], in_=psb[b0][:, :],
                                                               func=ACTF.Identity),
                     reads=[("ps", b0)], writes=["KT"], toks=[("RD", 0)])
                R.op("dve", lambda e, j=j, b1=b1: e.tensor_copy(out=KT[:, j, 640:1152], in_=psb[b1][:, :]),
                     reads=[("ps", b1)], writes=["KT"], toks=[("RD", 0)])
                R.op("act", lambda e, j=j, hps=hps: e.activation(out=KT[:, j, 0:128], in_=hps[:, 0:128],
                                                                 func=ACTF.Identity),
                     reads=[("ps", hb)], writes=["KT"], toks=[("RD", 0)])
                R.op("act", lambda e, j=j, hps=hps: e.activation(out=KT[:, j, 1152:1280], in_=hps[:, 128:256],
                                                                 func=ACTF.Identity),
                     reads=[("ps", hb)], writes=["KT"], toks=[("RD", 0)])
                ci += 1
            elif kind == "V":
                assert idx % 2 == 0
                s0 = next_wslot()
                s1 = next_wslot()
                if s1 == s0 + 1:
                    pieces = [(wring[:, s0:s0 + 2], 256, 0)]
                else:
                    pieces = [(wring[:, s0:s0 + 1], 128, 0), (wring[:, s1:s1 + 1], 128, 128)]
                for tb in range(10):
                    bank = tb % 4
                    if tb == 0:
                        lt = lambda kc: hTh[:, kc, 0:128]
                    elif tb == 9:
                        lt = lambda kc: hTh[:, kc, 128:256]
                    else:
                        lt = lambda kc, tb=tb: hT[:, kc, (tb - 1) * 128:tb * 128]

                    def mmv(e, lt=lt, bank=bank, pieces=pieces):
                        ins = None
                        for (wap, n, off) in pieces:
                            for kc in range(KC):
                                ins = e.matmul(psb[bank][:, off:off + n], lt(kc), wap[:, :, kc, :],
                                               start=(kc == 0), stop=(kc == KC - 1))
                        return ins
                    R.op("pe", mmv, reads=[("w", s0), ("w", s1)] + hT_all + hTh_all, writes=[("ps", bank)],
                         toks=[("RA", 0), ("RD", 0)])
                    dstv = Vt[:, tb, idx * 128:(idx + 2) * 128]
                    if tb % 2 == 0:
                        R.op("act", lambda e, dstv=dstv, bank=bank: e.activation(out=dstv, in_=psb[bank][:, 0:256],
                                                                                 func=ACTF.Identity),
                             reads=[("ps", bank)], writes=["Vt"], toks=[("RD", 0)])
                    else:
                        R.op("dve", lambda e, dstv=dstv, bank=bank: e.tensor_copy(out=dstv, in_=psb[bank][:, 0:256]),
                             reads=[("ps", bank)], writes=["Vt"], toks=[("RD", 0)])
                issue_next_w()
                issue_next_w()
                ci += 2
            elif kind == "Q":
                slot = next_wslot()
                b0, b1, _ = inproj(slot, None)
                gg, j = idx % 4, idx // 4
                R.op("act", lambda e, gg=gg, b0=b0: e.activation(out=QT[:, gg, 0:512], in_=psb[b0][:, :],
                                                                 func=ACTF.Identity),
                     reads=[("ps", b0)], writes=[("QT", j)], toks=[("RD", 1)])
                R.op("dve", lambda e, gg=gg, b1=b1: e.tensor_copy(out=QT[:, gg, 512:1024], in_=psb[b1][:, :]),
                     reads=[("ps", b1)], writes=[("QT", j)], toks=[("RD", 1)])
                ci += 1
            elif kind == "AZ":
                slot = next_wslot()
                b0, b1, _ = inproj(slot, None)
                gg, j = idx % 4, idx // 4
                for hf, bk in ((0, b0), (1, b1)):
                    R.op("act", lambda e, hf=hf, bk=bk: e.activation(out=thz[:, 0], in_=psb[bk][:, :],
                                                                     func=ACTF.Tanh, scale=0.5),
                         reads=[("ps", bk)], writes=[("thz", 0)], toks=[("RB", 1)])
                    R.op("dve", lambda e, hf=hf, bk=bk, gg=gg: e.scalar_tensor_tensor(
                        out=siluz[:, gg, hf * 512:(hf + 1) * 512], in0=thz[:, 0], scalar=1.0, in1=psb[bk][:, :],
                        op0=ALU.add, op1=ALU.mult),
                        reads=[("thz", 0), ("ps", bk)], writes=[("siluz", j)], toks=[("RB", 1), ("RD", 1)])
                ci += 1
                if gg == 3:
                    attention(j)
            elif kind == "H":
                c = idx
                if cv_next[0] < NCV:
                    issue_wout_convert(cv_next[0])
                    cv_next[0] += 1
                slot = next_wslot()
                b0, b1, (hps, hb) = inproj(slot, "edge")
                R.op("act", lambda e, b0=b0: e.activation(out=Hs[:, 1:513], in_=psb[b0][:, :], func=ACTF.Identity),
                     reads=[("ps", b0)], writes=["Hs"], toks=[("RD", 2)])
                R.op("act", lambda e, b1=b1: e.activation(out=Hs[:, 513:1025], in_=psb[b1][:, :], func=ACTF.Identity),
                     reads=[("ps", b1)], writes=["Hs"], toks=[("RD", 2)])
                R.op("act", lambda e, hps=hps: e.activation(out=Hs[:, 0:1], in_=hps[:, 0:1], func=ACTF.Identity),
                     reads=[("ps", hb)], writes=["Hs"], toks=[("RD", 2)])
                R.op("act", lambda e, hps=hps: e.activation(out=Hs[:, 1025:1026], in_=hps[:, 1:2], func=ACTF.Identity),
                     reads=[("ps", hb)], writes=["Hs"], toks=[("RD", 2)])
                slot = next_wslot()
                b0, b1, (hps, hb) = inproj(slot, "edge")
                R.op("dve", lambda e, b0=b0: e.tensor_tensor(out=ub[:, 1:513], in0=psb[b0][:, :], in1=Hs[:, 1:513],
                                                             op=ALU.mult),
                     reads=[("ps", b0), "Hs"], writes=["ub"], toks=[("RD", 2)])
                R.op("dve", lambda e, b1=b1: e.tensor_tensor(out=ub[:, 513:1025], in0=psb[b1][:, :],
                                                             in1=Hs[:, 513:1025], op=ALU.mult),
                     reads=[("ps", b1), "Hs"], writes=["ub"], toks=[("RD", 2)])
                R.op("dve", lambda e, hps=hps: e.scalar_tensor_tensor(out=ub[:, 0:1], in0=hps[:, 0:1],
                                                                      scalar=kb_sb[:, 10:11], in1=Hs[:, 0:1],
                                                                      op0=ALU.mult, op1=ALU.mult),
                     reads=[("ps", hb), "Hs", "kb"], writes=["ub"], toks=[("RD", 2)])
                R.op("dve", lambda e, hps=hps: e.scalar_tensor_tensor(out=ub[:, 1025:1026], in0=hps[:, 1:2],
                                                                      scalar=kb_sb[:, 11:12], in1=Hs[:, 1025:1026],
                                                                      op0=ALU.mult, op1=ALU.mult),
                     reads=[("ps", hb), "Hs", "kb"], writes=["ub"], toks=[("RD", 2)])
                R.op("pool", lambda e, c=c: e.tensor_scalar(out=vb, in0=ub[:, 1:1025], scalar1=cw_sb[:, c, 1:2],
                                                            scalar2=None, op0=ALU.mult),
                     reads=["ub", "cw"], writes=["vb"], toks=[("RD", 2)])
                R.op("dve", lambda e, c=c: e.scalar_tensor_tensor(out=vb, in0=ub[:, 0:1024], scalar=cw_sb[:, c, 0:1],
                                                                  in1=vb, op0=ALU.mult, op1=ALU.add),
                     reads=["ub", "cw", "vb"], writes=["vb"], toks=[("RD", 2)])
                R.op("dve", lambda e, c=c: e.scalar_tensor_tensor(out=vb, in0=ub[:, 2:1026], scalar=cw_sb[:, c, 2:3],
                                                                  in1=vb, op0=ALU.mult, op1=ALU.add),
                     reads=["ub", "cw", "vb"], writes=["vb"], toks=[("RD", 2)])
                slot = next_wslot()
                b0, b1, _ = inproj(slot, None)
                for hf, bk in ((0, b0), (1, b1)):
                    R.op("dve", lambda e, hf=hf, bk=bk: e.tensor_tensor(out=v2b[:, hf * 512:(hf + 1) * 512],
                                                                        in0=psb[bk][:, :],
                                                                        in1=vb[:, hf * 512:(hf + 1) * 512],
                                                                        op=ALU.mult),
                         reads=[("ps", bk), "vb"], writes=[("v2b", hf)], toks=[("RD", 2)])
                slot = next_wslot()
                b0, b1, _ = inproj(slot, None)
                for hf, bk in ((0, b0), (1, b1)):
                    R.op("act", lambda e, hf=hf, bk=bk: e.activation(out=thb[:, hf * 512:(hf + 1) * 512],
                                                                     in_=psb[bk][:, :], func=ACTF.Tanh, scale=0.5),
                         reads=[("ps", bk)], writes=[("thb", hf)], toks=[("RD", 2)])
                    R.op("dve", lambda e, hf=hf, bk=bk: e.scalar_tensor_tensor(
                        out=sb_[:, hf * 512:(hf + 1) * 512], in0=thb[:, hf * 512:(hf + 1) * 512], scalar=1.0,
                        in1=psb[bk][:, :], op0=ALU.add, op1=ALU.mult),
                        reads=[("thb", hf), ("ps", bk)], writes=[("sb", hf)], toks=[("RD", 2)])
                    R.op("pool", lambda e, hf=hf, c=c: e.tensor_tensor(
                        out=ycatT[:, c, hf * 512:(hf + 1) * 512], in0=sb_[:, hf * 512:(hf + 1) * 512],
                        in1=v2b[:, hf * 512:(hf + 1) * 512], op=ALU.mult),
                        reads=[("sb", hf), ("v2b", hf)], writes=[("ycat", c)], toks=[("RD", 2), ("RB", 2)])
                ci += 4
            else:
                raise AssertionError(kind)

        ycat_all = [("ycat", c) for c in range(KC)]
        pcount = [0]
        xcount = [0]
        out_ops = []
        pend_out = []

        def flush_out():
            (dst, src, tl_, r_) = pend_out.pop(0)
            out_ops.append(dma("act", dst, src, s_out[tl_ * 8 + r_], reads=[("z", tl_, r_)], name="out"))

        def final_ln(gi, r):
            pslot = pcount[0] % 2
            pcount[0] += 1
            for pi in (2, 3):
                dma("sp", par[:, pi, pslot], prow[pi, :, r * 512:(r + 1) * 512], s_par[pi][pslot],
                    writes=[("par", pi, pslot)], toks=[("RD", 3)])
            for tl in range(4):
                tslot = tl % 2
                zsl = zbuf[:, tl, r * 512:(r + 1) * 512]
                R.op("act", lambda e, zsl=zsl, tl=tl, tslot=tslot: e.activation(
                    out=t2[:, tslot], in_=zsl, func=ACTF.Identity, scale=rz[:, tl:tl + 1],
                    bias=nmz[:, tl:tl + 1]),
                    reads=[("z", tl, r), ("rz", tl), ("nmz", tl)], writes=[("t2", tslot)], toks=[("RD", 3)])
                R.op("pool" if (gi == 1 and tl % 2 == 0) else "dve", lambda e, tslot=tslot, pslot=pslot: e.tensor_tensor(
                    out=t2[:, tslot], in0=t2[:, tslot], in1=par[:, 2, pslot], op=ALU.mult),
                    reads=[("t2", tslot), ("par", 2, pslot)], writes=[("t2", tslot)], toks=[("RD", 3)])
                R.op("dve", lambda e, zsl=zsl, tslot=tslot, pslot=pslot: e.tensor_tensor(
                    out=zsl, in0=t2[:, tslot], in1=par[:, 3, pslot], op=ALU.add),
                    reads=[("t2", tslot), ("par", 3, pslot)], writes=[("z", tl, r)], toks=[("RD", 3)])
                t0 = (gi * 4 + tl) * 128
                pend_out.append((out[t0:t0 + 128, r * 512:(r + 1) * 512], zsl, tl, r))
                while len(pend_out) > 3:
                    flush_out()

        for gi in range(2 if stop >= 6 else 0):
            for r in range(8):
                if gi == 1 and r < 4:
                    final_ln(0, 2 * r)
                    final_ln(0, 2 * r + 1)
                    if r == 3:
                        while pend_out:
                            flush_out()
                ps_par = r % 2
                banks = [4 * ps_par + tl for tl in range(4)]
                pslot = pcount[0] % 2
                pcount[0] += 1
                for pi in range(2):
                    dma("sp", par[:, pi, pslot], prow[pi, :, r * 512:(r + 1) * 512], s_par[pi][pslot],
                        writes=[("par", pi, pslot)], toks=[("RD", 3)])
                R.op("dve", lambda e, pslot=pslot: e.tensor_single_scalar(out=par[:, 1, pslot], in_=par[:, 1, pslot],
                                                                          scalar=DN_ALPHA, op=ALU.mult),
                     reads=[("par", 1, pslot)], writes=[("par", 1, pslot)], toks=[("RD", 3)])
                for cq in range(4):
                    slot = next_wslot()

                    def mm2(e, slot=slot, cq=cq, banks=banks, gi=gi):
                        ins = None
                        for c8 in range(8):
                            c = cq * 8 + c8
                            for tl in range(4):
                                t0 = (gi * 4 + tl) * 128
                                ins = e.matmul(psb[banks[tl]][:, :], ycatT[:, c, t0:t0 + 128], wring2[:, slot, c8, :],
                                               start=(c == 0), stop=(c == KC - 1))
                        return ins
                    R.op("pe", mm2, reads=[("w", slot)] + ycat_all, writes=[("ps", bk) for bk in banks],
                         toks=[("RB", 2)], name="outproj")
                    issue_next_w()
                for tl in range(4):
                    tcg = 1 + gi * 4 + tl
                    xslot = xcount[0] % 3
                    xcount[0] += 1
                    tslot = xslot % 2
                    dma("sp", xt[:, xslot], x_in[tcg * 128:(tcg + 1) * 128, r * 512:(r + 1) * 512], s_xt[xslot],
                        writes=[("xt", xslot)], toks=[("RD", 3)])
                    R.op("act", lambda e, xslot=xslot, tslot=tslot, tcg=tcg: e.activation(
                        out=t1[:, tslot], in_=xt[:, xslot], func=ACTF.Identity, scale=arstd_all[:, tcg:tcg + 1],
                        bias=namr_all[:, tcg:tcg + 1]),
                        reads=[("xt", xslot), ("arstd", tcg), ("namr", tcg)], writes=[("t1", tslot)],
                        toks=[("RD", 3)])
                    R.op("dve", lambda e, tslot=tslot, pslot=pslot: e.tensor_tensor(
                        out=t1[:, tslot], in0=t1[:, tslot], in1=par[:, 0, pslot], op=ALU.mult),
                        reads=[("t1", tslot), ("par", 0, pslot)], writes=[("t1", tslot)], toks=[("RD", 3)])
                    R.op("dve", lambda e, xslot=xslot, tslot=tslot, pslot=pslot: e.tensor_tensor(
                        out=xa[:, xslot], in0=t1[:, tslot], in1=par[:, 1, pslot], op=ALU.add),
                        reads=[("t1", tslot), ("par", 1, pslot)], writes=[("xa", xslot)], toks=[("RD", 3)])
                    zsl = zbuf[:, tl, r * 512:(r + 1) * 512]
                    R.op("dve", lambda e, zsl=zsl, bk=banks[tl], xslot=xslot: e.tensor_tensor(
                        out=zsl, in0=psb[bk][:, :], in1=xa[:, xslot], op=ALU.add),
                        reads=[("ps", banks[tl]), ("xa", xslot)], writes=[("z", tl, r)], toks=[("RA", 1)])
                    R.op("dve", lambda e, zsl=zsl, tl=tl, r=r: e.bn_stats(out=stz[:, tl, r, :], in_=zsl),
                         reads=[("z", tl, r)], writes=[("stz", tl)])
            for tl in range(4):
                R.op("dve", lambda e, tl=tl: e.bn_aggr(out=mvz[:, tl, :], in_=stz[:, tl]),
                     reads=[("stz", tl)], writes=[("mvz", tl)])
                R.op("dve", lambda e, tl=tl: e.tensor_single_scalar(out=vez[:, tl:tl + 1], in_=mvz[:, tl, 1:2],
                                                                    scalar=LN_EPS, op=ALU.add),
                     reads=[("mvz", tl)], writes=[("vez", tl)])
                R.op("pool", lambda e, tl=tl: e.tensor_tensor(out=rz[:, tl:tl + 1], in0=vez[:, tl:tl + 1], in1=mhalf,
                                                              op=ALU.pow),
                     reads=[("vez", tl), "mhalf"], writes=[("rz", tl)])
                R.op("dve", lambda e, tl=tl: e.tensor_scalar(out=nmz[:, tl:tl + 1], in0=mvz[:, tl, 0:1],
                                                             scalar1=rz[:, tl:tl + 1], scalar2=-1.0,
                                                             op0=ALU.mult, op1=ALU.mult),
                     reads=[("mvz", tl), ("rz", tl)], writes=[("nmz", tl)])
            if gi == 1:
                for r in range(8):
                    final_ln(1, r)
                while pend_out:
                    flush_out()
        R.op("act", None, extra_deps=out_ops, name="fence")

        R.finalize(eng_sems)
        with nc.allow_low_precision("bf16 matmul operands, fp32 PSUM accumulation"):
            with nc.Block() as block:
                @block.sync
                def _(e):
                    R.emit_engine("sp", e)

                @block.scalar
                def _(e):
                    R.emit_engine("act", e)

                @block.gpsimd
                def _(e):
                    R.emit_engine("pool", e)

                @block.vector
                def _(e):
                    R.emit_engine("dve", e)

                @block.tensor
                def _(e):
                    R.emit_engine("pe", e)
    return nc


_PROG = {}


def _get_prog():
    if "nc" not in _PROG:
        _PROG["nc"] = build_program()
    return _PROG["nc"]


def make_in_maps(x, emb_ln_g, emb_ln_b, w_in, conv_w, sink, w_out, ln_g, ln_b):
    x = np.asarray(x, dtype=np.float32)
    x2 = x.reshape(SEQ, D_MODEL)
    xp = np.zeros((SEQ + 256, D_MODEL), np.float32)
    xp[128:128 + SEQ] = x2
    w_in2 = np.ascontiguousarray(np.asarray(w_in, np.float32).reshape(D_MODEL, PROJ))
    w_out2 = np.ascontiguousarray(np.asarray(w_out, np.float32).reshape(D_MODEL, D_MODEL))
    eg = np.asarray(emb_ln_g, np.float32).reshape(D_MODEL)
    eb = np.asarray(emb_ln_b, np.float32).reshape(D_MODEL)
    lg = np.asarray(ln_g, np.float32).reshape(D_MODEL)
    lb = np.asarray(ln_b, np.float32).reshape(D_MODEL)
    gbT = np.ascontiguousarray(np.concatenate([eg.reshape(KC, 128).T, eb.reshape(KC, 128).T], axis=1))
    cw = np.asarray(conv_w, np.float32).reshape(3, 16, 128)
    cwT = np.ascontiguousarray(cw.transpose(2, 1, 0).reshape(128, 48))
    sinkb = np.ascontiguousarray(np.broadcast_to(np.asarray(sink, np.float32).reshape(1, 16), (128, 16)))
    prow = np.ascontiguousarray(np.broadcast_to(np.stack([eg, eb, lg, lb])[:, None, :], (4, 128, D_MODEL)))
    in_maps = []
    for c in range(NCORES):
        kb = np.zeros((128, 12), np.float32)
        kb[:, 10] = 1.0
        kb[:, 11] = 1.0
        if c == 0:
            kb[:, 0] = NEGBIG
            kb[:, 10] = 0.0
        if c == NCORES - 1:
            kb[:, 9] = NEGBIG
            kb[:, 11] = 0.0
        in_maps.append({
            "x_in": np.ascontiguousarray(xp[c * TOK:c * TOK + TOK + 256]),
            "w_in": w_in2, "w_out": w_out2, "gbT": gbT, "cwT": cwT, "sinkb": sinkb, "kbias": kb, "prow": prow,
        })
    return in_maps


def kernel(x, emb_ln_g, emb_ln_b, w_in, conv_w, sink, w_out, ln_g, ln_b):
    in_maps = make_in_maps(x, emb_ln_g, emb_ln_b, w_in, conv_w, sink, w_out, ln_g, ln_b)
    nc = _get_prog()
    res = run_bass_kernel_spmd(nc, in_maps, core_ids=list(range(NCORES)))
    outs = [np.asarray(r["out"], dtype=np.float32) for r in res.results]
    return np.concatenate(outs, axis=0).reshape(1, SEQ, D_MODEL)
```

```python
import contextlib
import numpy as np
import concourse.bass as bass
import concourse.mybir as mybir
from concourse.bass_utils import run_bass_kernel_spmd

F32 = mybir.dt.float32
BF16 = mybir.dt.bfloat16
ALU = mybir.AluOpType
ACTF = mybir.ActivationFunctionType

NCORES = 8
D_MODEL = 4096
SEQ = 8192
TOK = SEQ // NCORES
NTC = TOK // 128
KC = D_MODEL // 128
CONV_W = 2048
HD = 128
OFF_CB, OFF_CC, OFF_CH, OFF_CZ = 0, 2048, 4096, 6144
OFF_Q, OFF_K, OFF_V, OFF_AZ = 8192, 10240, 10752, 11264
PROJ = 13312
DN_ALPHA = (2.0 * 1) ** 0.25
LN_EPS = 1e-5
SCALE = HD ** -0.5
NEGBIG = -30000.0
NW = 3
NFILL = 6
import os as _os
SKIP = set(_os.environ.get("KSKIP", "").split(","))


class Op:
    __slots__ = ("eng", "emit", "deps", "sig", "tick", "csem", "is_dma", "uid", "name")


class Rec:
    ENGS = ("sp", "act", "pool", "dve", "pe")

    def __init__(self):
        self.ops = {e: [] for e in self.ENGS}
        self.keyw = {}
        self.keyr = {}
        self.tok = {}
        self.uid = 0
        self.dma_counts = {}

    def op(self, eng, emit, reads=(), writes=(), toks=(), dma_sem=None, name="", extra_deps=()):
        if SKIP and ((eng in SKIP) or (name and name.split(" ")[0] in SKIP)):
            emit = None if dma_sem is None else emit
            if dma_sem is not None:
                return None
        ps_r = [k for k in reads if isinstance(k, tuple) and k[0] == "ps"]
        if ps_r:
            reads = [k for k in reads if not (isinstance(k, tuple) and k[0] == "ps")]
            writes = list(writes) + [k for k in ps_r if k not in writes]
        o = Op()
        o.eng, o.emit, o.deps, o.sig, o.tick = eng, emit, set(), False, None
        o.is_dma = dma_sem is not None
        o.csem = dma_sem
        o.uid = self.uid
        o.name = name
        self.uid += 1
        ek = ("dma", o.uid) if o.is_dma else eng
        for k in reads:
            for w in self.keyw.get(k, {}).values():
                o.deps.add(w)
        for k in writes:
            for w in self.keyw.get(k, {}).values():
                o.deps.add(w)
            for r in self.keyr.get(k, {}).values():
                o.deps.add(r)
        for (tname, ph) in toks:
            st = self.tok.setdefault(tname, [ph, {}, {}])
            assert ph >= st[0], (tname, ph, st[0])
            if ph > st[0]:
                st[0], st[2], st[1] = ph, st[1], {}
            for p in st[2].values():
                o.deps.add(p)
        for d in extra_deps:
            o.deps.add(d)
        o.deps.discard(o)
        if eng == "pe":
            o.deps = {d for d in o.deps if d.is_dma or d.eng != "pe"}
        for d in o.deps:
            d.sig = True
        for k in reads:
            self.keyr.setdefault(k, {})[ek] = o
        for k in writes:
            if self.keyr.get(k):
                self.keyw[k] = {ek: o}
                self.keyr[k] = {}
            else:
                self.keyw.setdefault(k, {})[ek] = o
        for (tname, ph) in toks:
            self.tok[tname][1][ek] = o
        if o.is_dma:
            c = self.dma_counts.get(id(dma_sem), 0) + 16
            self.dma_counts[id(dma_sem)] = c
            o.tick = c
            o.sig = True
        self.ops[eng].append(o)
        return o

    def finalize(self, eng_sems):
        for e in self.ENGS:
            n = 0
            for o in self.ops[e]:
                if o.is_dma:
                    continue
                if o.sig:
                    n += 1
                    o.tick = n
                    o.csem = eng_sems[e]

    def emit_engine(self, e, handle):
        waited = {}
        for o in self.ops[e]:
            for d in sorted(o.deps, key=lambda t: t.uid):
                assert d.tick is not None, (d.name, o.name)
                k = id(d.csem)
                if waited.get(k, 0) >= d.tick:
                    continue
                handle.wait_ge(d.csem, d.tick)
                waited[k] = d.tick
            if o.emit is None:
                if o.sig and not o.is_dma:
                    handle.nop().then_inc(o.csem, 1)
                continue
            ins = o.emit(handle)
            if o.is_dma:
                ins.then_inc(o.csem, 16)
            elif o.sig:
                ins.then_inc(o.csem, 1)


def build_program(stop=99):
    nc = bass.Bass("TRN2", target_bir_lowering=False)
    x_in = nc.dram_tensor("x_in", [TOK + 256, D_MODEL], F32, kind="ExternalInput").ap()
    w_in = nc.dram_tensor("w_in", [D_MODEL, PROJ], F32, kind="ExternalInput").ap()
    w_out = nc.dram_tensor("w_out", [D_MODEL, D_MODEL], F32, kind="ExternalInput").ap()
    gbT = nc.dram_tensor("gbT", [128, 2 * KC], F32, kind="ExternalInput").ap()
    cwT = nc.dram_tensor("cwT", [128, 16 * 3], F32, kind="ExternalInput").ap()
    sinkb = nc.dram_tensor("sinkb", [128, 16], F32, kind="ExternalInput").ap()
    kbias_d = nc.dram_tensor("kbias", [128, 12], F32, kind="ExternalInput").ap()
    prow = nc.dram_tensor("prow", [4, 128, D_MODEL], F32, kind="ExternalInput").ap()
    out = nc.dram_tensor("out", [TOK, D_MODEL], F32, kind="ExternalOutput").ap()

    wob = nc.dram_tensor("wob", [D_MODEL, D_MODEL], BF16).ap()
    w_in_v = w_in.rearrange("(kc p) f -> p kc f", p=128)
    w_out_v = wob.rearrange("(c p) d -> p c d", p=128)

    R = Rec()
    es = contextlib.ExitStack()
    with es:
        KB = 1024
        ARENA_BYTES = 205 * KB
        arena = es.enter_context(nc.sbuf_tensor("arena", [128, ARENA_BYTES // 2], BF16))

        def carve(off_b, nbytes, dt, shape=None):
            a = arena[:, off_b // 2:(off_b + nbytes) // 2]
            if dt == F32:
                a = a.bitcast(F32)
            if shape is not None and len(shape) > 1:
                names = " ".join("a%d" % i for i in range(len(shape)))
                kw = {"a%d" % i: s for i, s in enumerate(shape[:-1])}
                a = a.rearrange("p (%s) -> p %s" % (names, names), **kw)
            return a

        A0 = 0
        B0 = 64 * KB
        C0 = 128 * KB
        D0 = C0 + NW * 8 * KB
        G0 = D0 + 44 * KB
        hT = carve(A0, 64 * KB, BF16, [KC, TOK])
        zbuf = carve(A0, 64 * KB, F32, [4, D_MODEL])
        ycatT = carve(B0, 64 * KB, BF16, [KC, TOK])
        bt_hi = carve(B0, 1536, BF16, [3, 256])
        bt_lo = carve(B0 + 1536, 1536, BF16, [3, 256])
        identb = carve(B0 + 3072, 2048, BF16, [8, 128])
        btf = carve(B0 + 5120, 3072, F32, [3, 256])
        pT = carve(B0 + 12 * KB, 6 * KB, BF16, [2, 3, 512])
        densb = carve(B0 + 24 * KB, 2 * KB, F32, [512])
        o1 = carve(B0 + 26 * KB, 2 * KB, F32, [512])
        pvsb = carve(B0 + 28 * KB, 2 * KB, F32, [512])
        thz = carve(B0 + 30 * KB, 2 * KB, F32, [1, 512])
        xs = carve(B0, 64 * KB, F32, [4, D_MODEL])
        wring = carve(C0, NW * 8 * KB, BF16, [NW, KC, 128])
        wring2 = carve(C0, NW * 8 * KB, BF16, [NW, 8, 512])
        hTh = carve(D0, 16 * KB, BF16, [KC, 256])
        siluz = carve(D0, 16 * KB, F32, [4, TOK])
        KT = carve(D0 + 16 * KB, 10 * KB, BF16, [4, 10 * 128])
        Vt = carve(D0 + 26 * KB, 10 * KB, BF16, [10, 512])
        QT = carve(D0 + 36 * KB, 8 * KB, BF16, [4, TOK])
        CW = 1032
        Hs = carve(D0, CW * 4, F32, [CW])
        ub = carve(D0 + CW * 4, CW * 4, F32, [CW])
        vb = carve(D0 + 2 * CW * 4, 4 * KB, F32, [TOK])
        v2b = carve(D0 + 2 * CW * 4 + 4 * KB, 4 * KB, F32, [TOK])
        thb = carve(D0 + 2 * CW * 4 + 8 * KB, 4 * KB, F32, [TOK])
        sb_ = carve(D0 + 2 * CW * 4 + 12 * KB, 4 * KB, F32, [TOK])
        par = carve(D0, 16 * KB, F32, [4, 2, 512])
        xt = carve(D0 + 16 * KB, 6 * KB, F32, [3, 512])
        t1 = carve(D0 + 22 * KB, 4 * KB, F32, [2, 512])
        xa = carve(D0 + 26 * KB, 6 * KB, F32, [3, 512])
        t2 = carve(D0 + 32 * KB, 8 * KB, F32, [4, 512])
        g = [G0]

        def galloc(nbytes, dt, shape=None):
            nb = (nbytes + 63) // 64 * 64
            a = carve(g[0], nb, dt, None)
            n_el = nbytes // (4 if dt == F32 else 2)
            a = a[:, 0:n_el]
            if shape is not None and len(shape) > 1:
                names = " ".join("a%d" % i for i in range(len(shape)))
                kw = {"a%d" % i: s for i, s in enumerate(shape[:-1])}
                a = a.rearrange("p (%s) -> p %s" % (names, names), **kw)
            g[0] += nb
            return a

        gb_sb = galloc(2 * KC * 4, F32)
        cw_sb = galloc(48 * 4, F32, [16, 3])
        sink2 = galloc(16 * 4, F32)
        kb_sb = galloc(12 * 4, F32)
        ident = galloc(128 * 4, F32)
        ones_b = galloc(128 * 2, BF16)
        hTe = galloc(KC * 2 * 2, BF16, [KC, 2])
        bnst = galloc(2 * 8 * 6 * 4, F32, [2, 8, 6])
        mv = galloc(2 * 2 * 4, F32, [2, 2])
        mu_all = galloc(10 * 4, F32)
        rstd_all = galloc(10 * 4, F32)
        arstd_all = galloc(10 * 4, F32)
        nmr_all = galloc(10 * 4, F32)
        namr_all = galloc(10 * 4, F32)
        nmz = galloc(4 * 4, F32)
        veps = galloc(10 * 4, F32)
        mhalf = galloc(4, F32)
        stz = galloc(4 * 8 * 6 * 4, F32, [4, 8, 6])
        mvz = galloc(4 * 2 * 4, F32, [4, 2])
        vez = galloc(4 * 4, F32)
        rz = galloc(4 * 4, F32)
        dD = carve(D0 + 36 * KB, 1536, F32, [3, 128])
        absD = carve(D0 + 36 * KB + 1536, 1536, F32, [3, 128])
        m01 = carve(D0 + 36 * KB + 3072, 1536, F32, [3, 128])
        tmpE = carve(D0 + 36 * KB + 4608, 1536, F32, [3, 128])
        assert g[0] <= ARENA_BYTES, g[0]

        psb = [es.enter_context(nc.psum_tensor("ps%d" % i, [128, 512], F32)) for i in range(8)]

        def sem(name):
            return es.enter_context(nc.semaphore(name))

        eng_sems = {e: sem("s_" + e) for e in Rec.ENGS}
        s_const = [sem("d_const%d" % i) for i in range(4)]
        s_xs = [sem("d_xs%d" % i) for i in range(4)]
        s_w = [sem("d_w%d" % i) for i in range(NW)]
        s_par = [[sem("d_par%d_%d" % (a, i)) for i in range(2)] for a in range(4)]
        s_xt = [sem("d_xt%d" % i) for i in range(3)]
        s_out = [sem("d_out%d" % i) for i in range(32)]

        def dma(eng, out_ap, in_ap, dsem, reads=(), writes=(), toks=(), name=""):
            return R.op(eng, lambda e: e.dma_start(out=out_ap, in_=in_ap), reads=reads, writes=writes,
                        toks=toks, dma_sem=dsem, name=name)

        dma("sp", gb_sb, gbT[:, :], s_const[0], writes=["gb"])
        dma("sp", cw_sb.rearrange("p a b -> p (a b)"), cwT[:, :], s_const[1], writes=["cw"])
        dma("sp", sink2, sinkb[:, :], s_const[2], writes=["sink"])
        dma("sp", kb_sb, kbias_d[:, :], s_const[3], writes=["kb"])

        R.op("pool", lambda e: e.memset(ones_b, 1.0), writes=["ones"])
        R.op("pool", lambda e: e.memset(mhalf, -0.5), writes=["mhalf"])
        R.op("pool", lambda e: e.iota(ident, pattern=[[1, 128]], base=0, channel_multiplier=-1,
                                      allow_small_or_imprecise_dtypes=True), writes=["ident"])
        R.op("pool", lambda e: e.tensor_single_scalar(out=ident, in_=ident, scalar=0.0, op=ALU.is_equal),
             reads=["ident"], writes=["ident"])
        R.op("act", lambda e: e.activation(out=sink2, in_=sink2, func=ACTF.Exp), reads=["sink"], writes=["sink"],
             name="actsink")
        R.op("dve", lambda e: e.tensor_single_scalar(out=sink2, in_=sink2, scalar=2.0, op=ALU.mult),
             reads=["sink"], writes=["sink"])
        R.op("dve", lambda e: e.tensor_single_scalar(out=cw_sb, in_=cw_sb, scalar=0.5, op=ALU.mult),
             reads=["cw"], writes=["cw"])

        wcount = [0]

        def w_dma_in(col):
            slot = wcount[0] % NW
            wcount[0] += 1
            dma("pool", wring[:, slot], w_in_v[:, :, col:col + 128], s_w[slot], writes=[("w", slot)],
                name="w_in %d" % col)
            return slot

        def w_dma_out(r, cq):
            slot = wcount[0] % NW
            wcount[0] += 1
            dma("pool", wring2[:, slot], w_out_v[:, cq * 8:(cq + 1) * 8, r * 512:(r + 1) * 512], s_w[slot],
                reads=[("wob", r * 4 + cq)], writes=[("w", slot)], name="w_out %d %d" % (r, cq))
            return slot

        sched = []
        for j in range(4):
            sched.append(("K", j, OFF_K + 128 * j))
        for j in range(4):
            sched.append(("V", j, OFF_V + 128 * j))
        for j in range(4):
            for gg in range(4):
                sched.append(("Q", 4 * j + gg, OFF_Q + 128 * (4 * j + gg)))
            for gg in range(4):
                sched.append(("AZ", 4 * j + gg, OFF_AZ + 128 * (4 * j + gg)))
        for c in range(16):
            sched.append(("H", c, OFF_CH + 128 * c))
            sched.append(("C", c, OFF_CC + 128 * c))
            sched.append(("B", c, OFF_CB + 128 * c))
            sched.append(("Z", c, OFF_CZ + 128 * c))
        wq = [("in", s[2]) for s in sched]
        for gi in range(2):
            for r in range(8):
                for cq in range(4):
                    wq.append(("out", r, cq))
        wq_pos = [0]
        wslots = []

        def issue_next_w():
            if wq_pos[0] >= len(wq):
                return
            it = wq[wq_pos[0]]
            wq_pos[0] += 1
            if it[0] == "in":
                wslots.append(w_dma_in(it[1]))
            else:
                wslots.append(w_dma_out(it[1], it[2]))

        for _ in range(NW):
            issue_next_w()
        wuse = [0]

        def next_wslot():
            s = wslots[wuse[0]]
            wuse[0] += 1
            return s

        tp_bank = [0]

        def p0_load_stats(tc, slot):
            bs = tc % 2
            dma("sp", xs[:, slot], x_in[tc * 128:(tc + 1) * 128, :], s_xs[slot], writes=[("xs", slot)],
                toks=[("RB", 0)], name="x load")
            for q in range(8):
                R.op("dve", lambda e, q=q, slot=slot, bs=bs: e.bn_stats(out=bnst[:, bs, q, :],
                                                                       in_=xs[:, slot, q * 512:(q + 1) * 512]),
                     reads=[("xs", slot)], writes=[("bnst", bs)], toks=[("RB", 0)])
            R.op("dve", lambda e, bs=bs: e.bn_aggr(out=mv[:, bs, :], in_=bnst[:, bs]),
                 reads=[("bnst", bs)], writes=[("mv", bs)])
            R.op("dve", lambda e, bs=bs, tc=tc: e.tensor_single_scalar(out=veps[:, tc:tc + 1], in_=mv[:, bs, 1:2],
                                                                       scalar=LN_EPS, op=ALU.add),
                 reads=[("mv", bs)], writes=[("veps", tc)])
            R.op("dve", lambda e, bs=bs, tc=tc: e.tensor_copy(out=mu_all[:, tc:tc + 1], in_=mv[:, bs, 0:1]),
                 reads=[("mv", bs)], writes=[("mu", tc)])
            R.op("pool", lambda e, tc=tc: e.tensor_tensor(out=rstd_all[:, tc:tc + 1], in0=veps[:, tc:tc + 1],
                                                          in1=mhalf, op=ALU.pow),
                 reads=[("veps", tc), "mhalf"], writes=[("rstd", tc)])
            R.op("dve", lambda e, tc=tc: e.tensor_scalar(out=nmr_all[:, tc:tc + 1], in0=mu_all[:, tc:tc + 1],
                                                         scalar1=rstd_all[:, tc:tc + 1], scalar2=-1.0,
                                                         op0=ALU.mult, op1=ALU.mult),
                 reads=[("mu", tc), ("rstd", tc)], writes=[("nmr", tc)])

        def p0_norm(tc, slot):
            R.op("act", lambda e, slot=slot, tc=tc: e.activation(out=xs[:, slot], in_=xs[:, slot], func=ACTF.Identity,
                                                                 scale=rstd_all[:, tc:tc + 1],
                                                                 bias=nmr_all[:, tc:tc + 1]),
                 reads=[("xs", slot), ("rstd", tc), ("nmr", tc)], writes=[("xs", slot)], toks=[("RB", 0)],
                 name="actnorm")

        def p0_small(tc):
            R.op("dve", lambda e, tc=tc: e.tensor_single_scalar(out=namr_all[:, tc:tc + 1], in_=nmr_all[:, tc:tc + 1],
                                                                scalar=DN_ALPHA, op=ALU.mult),
                 reads=[("nmr", tc)], writes=[("namr", tc)])
            R.op("dve", lambda e, tc=tc: e.tensor_single_scalar(out=arstd_all[:, tc:tc + 1],
                                                                in_=rstd_all[:, tc:tc + 1], scalar=DN_ALPHA,
                                                                op=ALU.mult),
                 reads=[("rstd", tc)], writes=[("arstd", tc)])

        def p0_transpose(tca, tcb, sa, sb2):
            for kq in range(16):
                bank = tp_bank[0] % 4
                tp_bank[0] += 1

                def tr(e, kq=kq, bank=bank):
                    ins = None
                    for k2 in range(2):
                        kc = kq * 2 + k2
                        for ti, sl in ((0, sa), (1, sb2)):
                            c0 = k2 * 256 + ti * 128
                            ins = e.transpose(psb[bank][:, c0:c0 + 128], xs[:, sl, kc * 128:(kc + 1) * 128], ident)
                    return ins
                R.op("pe", tr, reads=[("xs", sa), ("xs", sb2), "ident"], writes=[("ps", bank)], toks=[("RB", 0)])
                for k2 in range(2):
                    kc = kq * 2 + k2
                    if tca == 0:
                        dst = hTh[:, kc, 0:256]
                        wk = ("hTh", kc)
                    else:
                        dst = hT[:, kc, (tca - 1) * 128:(tca + 1) * 128]
                        wk = ("hT", kc, tca - 1)
                    wk2 = ("hT", kc, tcb - 1) if tca != 0 else ("hTh", kc)
                    src = psb[bank][:, k2 * 256:(k2 + 1) * 256]
                    if bank != 3:
                        R.op("act", lambda e, dst=dst, src=src, kc=kc: e.activation(
                            out=dst, in_=src, func=ACTF.Identity, scale=gb_sb[:, kc:kc + 1],
                            bias=gb_sb[:, KC + kc:KC + kc + 1]),
                            reads=[("ps", bank), "gb"], writes=[wk, wk2], toks=[("RD", 0)], name="actevac")
                    else:
                        R.op("dve", lambda e, dst=dst, src=src, kc=kc: e.tensor_scalar(
                            out=dst, in0=src, scalar1=gb_sb[:, kc:kc + 1], scalar2=gb_sb[:, KC + kc:KC + kc + 1],
                            op0=ALU.mult, op1=ALU.add),
                            reads=[("ps", bank), "gb"], writes=[wk, wk2], toks=[("RD", 0)])

        pairs = [(0, 9), (1, 2), (3, 4), (5, 6), (7, 8)]
        p0_load_stats(0, 0)
        p0_load_stats(9, 1)
        p0_norm(0, 0)
        p0_norm(9, 1)
        for pi_, (ta, tb_) in enumerate(pairs):
            sa, sb2 = 2 * (pi_ % 2), 2 * (pi_ % 2) + 1
            if pi_ + 1 < len(pairs):
                na, nb = pairs[pi_ + 1]
                p0_load_stats(na, 2 * ((pi_ + 1) % 2))
                p0_load_stats(nb, 2 * ((pi_ + 1) % 2) + 1)
            p0_transpose(ta, tb_, sa, sb2)
            p0_small(ta)
            p0_small(tb_)
            if pi_ + 1 < len(pairs):
                p0_norm(na, 2 * ((pi_ + 1) % 2))
                p0_norm(nb, 2 * ((pi_ + 1) % 2) + 1)

        R.op("pool", lambda e: e.tensor_copy(out=hTe, in_=hTh[:, :, 127:129]),
             reads=[("hTh", kc) for kc in range(KC)], writes=["hTe"], toks=[("RD", 0)])

        if stop <= 0:
            sched_stop = True
        R.op("pool", lambda e: e.iota(dD.rearrange("p a b -> p (a b)"), pattern=[[-128, 3], [1, 128]], base=128,
                                      channel_multiplier=-1, allow_small_or_imprecise_dtypes=True),
             writes=["dD"], toks=[("RD", 0)])
        R.op("act", lambda e: e.activation(out=absD, in_=dD, func=ACTF.Abs),
             reads=["dD"], writes=["absD"], toks=[("RD", 0)], name="actabs")
        R.op("dve", lambda e: e.tensor_single_scalar(out=m01, in_=absD, scalar=128.0, op=ALU.is_le),
             reads=["absD"], writes=["m01"], toks=[("RD", 0)])
        BIGB = 340000.0
        R.op("dve", lambda e: e.tensor_scalar(out=tmpE, in0=m01, scalar1=BIGB, scalar2=-BIGB, op0=ALU.mult,
                                              op1=ALU.add),
             reads=["m01"], writes=["tmpE"], toks=[("RD", 0)])
        for gg in range(2):
            slope = 2.0 ** (-8.0 * (gg + 1) / 16.0)
            dst = btf[:, :, gg * 128:(gg + 1) * 128]
            R.op("dve", lambda e, dst=dst, sl=slope: e.tensor_single_scalar(out=dst, in_=absD, scalar=-sl / SCALE,
                                                                            op=ALU.mult),
                 reads=["absD"], writes=["btf"], toks=[("RD", 0), ("RB", 1)])
            R.op("dve", lambda e, dst=dst: e.tensor_tensor(out=dst, in0=dst, in1=m01, op=ALU.mult),
                 reads=["btf", "m01"], writes=["btf"], toks=[("RD", 0), ("RB", 1)])
            R.op("dve", lambda e, dst=dst: e.tensor_tensor(out=dst, in0=dst, in1=tmpE, op=ALU.add),
                 reads=["btf", "tmpE"], writes=["btf"], toks=[("RD", 0), ("RB", 1)])
        R.op("dve", lambda e: e.tensor_copy(out=bt_hi, in_=btf), reads=["btf"], writes=["bt_hi"], toks=[("RB", 1)])
        R.op("dve", lambda e: e.tensor_tensor(out=bt_lo, in0=btf, in1=bt_hi, op=ALU.subtract),
             reads=["btf", "bt_hi"], writes=["bt_lo"], toks=[("RB", 1)])
        for k in range(8):
            R.op("dve", lambda e, k=k: e.tensor_single_scalar(out=identb[:, k, :], in_=ident, scalar=2.0 ** (-k),
                                                              op=ALU.mult),
                 reads=["ident"], writes=["identb"], toks=[("RB", 1)])
        hT_all = [("hT", kc, t) for kc in range(KC) for t in range(NTC)]
        hTh_all = [("hTh", kc) for kc in range(KC)]

        nchunk = [0]

        def inproj(slot, halo):
            par_ = nchunk[0] % 2
            nchunk[0] += 1
            b0, b1 = 2 * par_, 2 * par_ + 1
            if halo == "kv":
                hN = 256
            elif halo == "edge":
                hN = 2
            else:
                hN = 0
            hb = 4 + par_
            hps = psb[hb][:, 0:hN] if hN else None

            def mm(e):
                ins = None
                for kc in range(KC):
                    w = wring[:, slot, kc, :]
                    st, sp_ = (kc == 0), (kc == KC - 1)
                    e.matmul(psb[b0][:, :], w, hT[:, kc, 0:512], start=st, stop=sp_)
                    ins = e.matmul(psb[b1][:, :], w, hT[:, kc, 512:1024], start=st, stop=sp_)
                    if halo == "kv":
                        ins = e.matmul(hps, w, hTh[:, kc, :], start=st, stop=sp_)
                    elif halo == "edge":
                        ins = e.matmul(hps, w, hTe[:, kc, :], start=st, stop=sp_)
                return ins
            reads = [("w", slot)] + hT_all
            writes = [("ps", b0), ("ps", b1)]
            toks = [("RA", 0)]
            if halo == "kv":
                reads += hTh_all
                writes.append(("ps", hb))
                toks.append(("RD", 0))
            elif halo == "edge":
                reads.append("hTe")
                writes.append(("ps", hb))
            R.op("pe", mm, reads=reads, writes=writes, toks=toks, name="inproj")
            issue_next_w()
            return b0, b1, (hps, hb)

        def attention(j):
            NS = 8

            def qk(i):
                banks = (5, 6, 7)

                def f(e):
                    ins = None
                    for b in range(3):
                        kb = i + b
                        e.matmul(psb[banks[b]][:, :], KT[:, j, kb * 128:(kb + 1) * 128],
                                 QT[:, :, i * 128:(i + 1) * 128], start=True, stop=False)
                        for gp in range(2):
                            k = 2 * j + gp
                            o_ = psb[banks[b]][:, gp * 256:(gp + 1) * 256]
                            e.matmul(o_, identb[:, k, :], bt_hi[:, b, :], start=False, stop=False)
                            ins = e.matmul(o_, identb[:, k, :], bt_lo[:, b, :], start=False, stop=(gp == 1))
                    return ins
                R.op("pe", f, reads=["KT", ("QT", j), "identb", "bt_hi", "bt_lo"],
                     writes=[("ps", bk) for bk in banks], toks=[("RD", 1), ("RB", 1)])
                for b in range(3):
                    kb = i + b
                    R.op("act", lambda e, b=b, kb=kb, bk=banks[b]: e.activation(
                        out=pT[:, i % 2, b], in_=psb[bk][:, :], func=ACTF.Exp, scale=SCALE,
                        bias=kb_sb[:, kb:kb + 1]),
                        reads=[("ps", banks[b]), "kb"], writes=[("pT", i % 2, b)], toks=[("RB", 1)])

            def pvd(i):
                def f(e):
                    ins = None
                    for b in range(3):
                        kb = i + b
                        e.matmul(psb[3][:, :], Vt[:, kb, j * 128:(j + 1) * 128], pT[:, i % 2, b],
                                 start=(b == 0), stop=(b == 2))
                        ins = e.matmul(psb[4][:, :], ones_b, pT[:, i % 2, b], start=(b == 0), stop=(b == 2))
                    return ins
                R.op("pe", f, reads=["Vt", "ones"] + [("pT", i % 2, b) for b in range(3)],
                     writes=[("ps", 3), ("ps", 4)], toks=[("RD", 1), ("RB", 1)])
                sk = sink2[:, 4 * j:4 * j + 4].unsqueeze(2).broadcast_to([128, 4, 128])
                R.op("dve", lambda e: e.scalar_tensor_tensor(
                    out=densb.rearrange("p (a b) -> p a b", a=4), in0=psb[4][:, :].rearrange("p (a b) -> p a b", a=4),
                    scalar=2.0, in1=sk, op0=ALU.mult, op1=ALU.add),
                    reads=[("ps", 4), "sink"], writes=["densb"], toks=[("RB", 1)])
                R.op("act", lambda e: e.activation(out=pvsb, in_=psb[3][:, :], func=ACTF.Identity),
                     reads=[("ps", 3)], writes=["pvsb"], toks=[("RB", 1)])
                R.op("act", lambda e: e.activation(out=densb, in_=densb, func=ACTF.Ln), reads=["densb"],
                     writes=["densb"], toks=[("RB", 1)])
                R.op("act", lambda e: e.activation(out=densb, in_=densb, func=ACTF.Exp, scale=-1.0), reads=["densb"],
                     writes=["densb"], toks=[("RB", 1)])
                R.op("dve", lambda e: e.tensor_tensor(out=o1, in0=pvsb, in1=densb, op=ALU.mult),
                     reads=["pvsb", "densb"], writes=["o1"], toks=[("RB", 1)])
                R.op("dve", lambda e: e.tensor_tensor(
                    out=ycatT[:, 16 + 4 * j:16 + 4 * j + 4, i * 128:(i + 1) * 128],
                    in0=o1.rearrange("p (a b) -> p a b", a=4), in1=siluz[:, :, i * 128:(i + 1) * 128], op=ALU.mult),
                    reads=["o1", ("siluz", j)], writes=[("ycat", 16 + 4 * j + gg) for gg in range(4)],
                    toks=[("RB", 1), ("RD", 1)])

            def filler(n):
                def f(e):
                    ins = None
                    for _ in range(n):
                        ins = e.matmul(psb[0][:, :], ones_b, pT[:, 0, 0], start=True, stop=True)
                    return ins
                R.op("pe", f, reads=["ones"], writes=[("ps", 0)], toks=[("RB", 1)])

            for i in range(NS):
                qk(i)
                if i >= 1:
                    pvd(i - 1)
            pvd(NS - 1)

        s_cvin = [sem("d_cvin%d" % i) for i in range(2)]
        s_cvout = [sem("d_cvout%d" % i) for i in range(2)]
        cvb = carve(D0 + 26 * KB, 16 * KB, BF16, [2, 8, 512])
        w_out_f = w_out.rearrange("(c p) d -> p c d", p=128)
        NCV = 32

        def issue_wout_convert(n):
            r_, cq_ = n // 4, n % 4
            sl = n % 2
            dma("pool", cvb[:, sl], w_out_f[:, cq_ * 8:(cq_ + 1) * 8, r_ * 512:(r_ + 1) * 512], s_cvin[sl],
                writes=[("cvb", sl)], toks=[("RD", 2)], name="cvin %d" % n)
            dma("sp", w_out_v[:, cq_ * 8:(cq_ + 1) * 8, r_ * 512:(r_ + 1) * 512], cvb[:, sl], s_cvout[sl],
                reads=[("cvb", sl)], writes=[("wob", n)], toks=[("RD", 2)], name="cvout %d" % n)
        cv_next = [0]
        ci = 0
        nlim = {0: 0, 1: 4, 2: 8, 3: 16, 4: 40}.get(stop, len(sched))
        while ci < min(len(sched), nlim):
            kind, idx, col = sched[ci]
            if kind == "K":
                slot = next_wslot()
                b0, b1, (hps, hb) = inproj(slot, "kv")
                j = idx
                R.op("act", lambda e, j=j, b0=b0: e.activation(out=KT[:, j, 128:640], in_=psb[b0][:, :],
                                                               func=ACTF.Identity),
                     reads=[("ps", b0)], writes=["KT"], toks=[("RD", 0)])
                R.op("dve", lambda e, j=j, b1=b1: e.tensor_copy(out=KT[:, j, 640:1152], in_=psb[b1][:, :]),
                     reads=[("ps", b1)], writes=["KT"], toks=[("RD", 0)])
                R.op("act", lambda e, j=j, hps=hps: e.activation(out=KT[:, j, 0:128], in_=hps[:, 0:128],
                                                                 func=ACTF.Identity),
                     reads=[("ps", hb)], writes=["KT"], toks=[("RD", 0)])
                R.op("act", lambda e, j=j, hps=hps: e.activation(out=KT[:, j, 1152:1280], in_=hps[:, 128:256],
                                                                 func=ACTF.Identity),
                     reads=[("ps", hb)], writes=["KT"], toks=[("RD", 0)])
                ci += 1
            elif kind == "V":
                assert idx % 2 == 0
                s0 = next_wslot()
                s1 = next_wslot()
                if s1 == s0 + 1:
                    pieces = [(wring[:, s0:s0 + 2], 256, 0)]
                else:
                    pieces = [(wring[:, s0:s0 + 1], 128, 0), (wring[:, s1:s1 + 1], 128, 128)]
                for tb in range(10):
                    bank = tb % 4
                    if tb == 0:
                        lt = lambda kc: hTh[:, kc, 0:128]
                    elif tb == 9:
                        lt = lambda kc: hTh[:, kc, 128:256]
                    else:
                        lt = lambda kc, tb=tb: hT[:, kc, (tb - 1) * 128:tb * 128]

                    def mmv(e, lt=lt, bank=bank, pieces=pieces):
                        ins = None
                        for (wap, n, off) in pieces:
                            for kc in range(KC):
                                ins = e.matmul(psb[bank][:, off:off + n], lt(kc), wap[:, :, kc, :],
                                               start=(kc == 0), stop=(kc == KC - 1))
                        return ins
                    R.op("pe", mmv, reads=[("w", s0), ("w", s1)] + hT_all + hTh_all, writes=[("ps", bank)],
                         toks=[("RA", 0), ("RD", 0)])
                    dstv = Vt[:, tb, idx * 128:(idx + 2) * 128]
                    if tb % 2 == 0:
                        R.op("act", lambda e, dstv=dstv, bank=bank: e.activation(out=dstv, in_=psb[bank][:, 0:256],
                                                                                 func=ACTF.Identity),
                             reads=[("ps", bank)], writes=["Vt"], toks=[("RD", 0)])
                    else:
                        R.op("dve", lambda e, dstv=dstv, bank=bank: e.tensor_copy(out=dstv, in_=psb[bank][:, 0:256]),
                             reads=[("ps", bank)], writes=["Vt"], toks=[("RD", 0)])
                issue_next_w()
                issue_next_w()
                ci += 2
            elif kind == "Q":
                slot = next_wslot()
                b0, b1, _ = inproj(slot, None)
                gg, j = idx % 4, idx // 4
                R.op("act", lambda e, gg=gg, b0=b0: e.activation(out=QT[:, gg, 0:512], in_=psb[b0][:, :],
                                                                 func=ACTF.Identity),
                     reads=[("ps", b0)], writes=[("QT", j)], toks=[("RD", 1)])
                R.op("dve", lambda e, gg=gg, b1=b1: e.tensor_copy(out=QT[:, gg, 512:1024], in_=psb[b1][:, :]),
                     reads=[("ps", b1)], writes=[("QT", j)], toks=[("RD", 1)])
                ci += 1
            elif kind == "AZ":
                slot = next_wslot()
                b0, b1, _ = inproj(slot, None)
                gg, j = idx % 4, idx // 4
                for hf, bk in ((0, b0), (1, b1)):
                    R.op("act", lambda e, hf=hf, bk=bk: e.activation(out=thz[:, 0], in_=psb[bk][:, :],
                                                                     func=ACTF.Tanh, scale=0.5),
                         reads=[("ps", bk)], writes=[("thz", 0)], toks=[("RB", 1)])
                    R.op("dve", lambda e, hf=hf, bk=bk, gg=gg: e.scalar_tensor_tensor(
                        out=siluz[:, gg, hf * 512:(hf + 1) * 512], in0=thz[:, 0], scalar=1.0, in1=psb[bk][:, :],
                        op0=ALU.add, op1=ALU.mult),
                        reads=[("thz", 0), ("ps", bk)], writes=[("siluz", j)], toks=[("RB", 1), ("RD", 1)])
                ci += 1
                if gg == 3:
                    attention(j)
            elif kind == "H":
                c = idx
                slot = next_wslot()
                b0, b1, (hps, hb) = inproj(slot, "edge")
                if nchunk[0] % 2 == 0 and cv_next[0] < 32:
                    issue_wout_convert(cv_next[0])
                    cv_next[0] += 1
                R.op("act", lambda e, b0=b0: e.activation(out=Hs[:, 1:513], in_=psb[b0][:, :], func=ACTF.Identity),
                     reads=[("ps", b0)], writes=["Hs"], toks=[("RD", 2)])
                R.op("act", lambda e, b1=b1: e.activation(out=Hs[:, 513:1025], in_=psb[b1][:, :], func=ACTF.Identity),
                     reads=[("ps", b1)], writes=["Hs"], toks=[("RD", 2)])
                R.op("act", lambda e, hps=hps: e.activation(out=Hs[:, 0:1], in_=hps[:, 0:1], func=ACTF.Identity),
                     reads=[("ps", hb)], writes=["Hs"], toks=[("RD", 2)])
                R.op("act", lambda e, hps=hps: e.activation(out=Hs[:, 1025:1026], in_=hps[:, 1:2], func=ACTF.Identity),
                     reads=[("ps", hb)], writes=["Hs"], toks=[("RD", 2)])
                slot = next_wslot()
                b0, b1, (hps, hb) = inproj(slot, "edge")
                if nchunk[0] % 2 == 0 and cv_next[0] < 32:
                    issue_wout_convert(cv_next[0])
                    cv_next[0] += 1
                R.op("dve", lambda e, b0=b0: e.tensor_tensor(out=ub[:, 1:513], in0=psb[b0][:, :], in1=Hs[:, 1:513],
                                                             op=ALU.mult),
                     reads=[("ps", b0), "Hs"], writes=["ub"], toks=[("RD", 2)])
                R.op("dve", lambda e, b1=b1: e.tensor_tensor(out=ub[:, 513:1025], in0=psb[b1][:, :],
                                                             in1=Hs[:, 513:1025], op=ALU.mult),
                     reads=[("ps", b1), "Hs"], writes=["ub"], toks=[("RD", 2)])
                R.op("dve", lambda e, hps=hps: e.scalar_tensor_tensor(out=ub[:, 0:1], in0=hps[:, 0:1],
                                                                      scalar=kb_sb[:, 10:11], in1=Hs[:, 0:1],
                                                                      op0=ALU.mult, op1=ALU.mult),
                     reads=[("ps", hb), "Hs", "kb"], writes=["ub"], toks=[("RD", 2)])
                R.op("dve", lambda e, hps=hps: e.scalar_tensor_tensor(out=ub[:, 1025:1026], in0=hps[:, 1:2],
                                                                      scalar=kb_sb[:, 11:12], in1=Hs[:, 1025:1026],
                                                                      op0=ALU.mult, op1=ALU.mult),
                     reads=[("ps", hb), "Hs", "kb"], writes=["ub"], toks=[("RD", 2)])
                R.op("pool", lambda e, c=c: e.tensor_scalar(out=vb, in0=ub[:, 1:1025], scalar1=cw_sb[:, c, 1:2],
                                                            scalar2=None, op0=ALU.mult),
                     reads=["ub", "cw"], writes=["vb"], toks=[("RD", 2)])
                R.op("dve", lambda e, c=c: e.scalar_tensor_tensor(out=vb, in0=ub[:, 0:1024], scalar=cw_sb[:, c, 0:1],
                                                                  in1=vb, op0=ALU.mult, op1=ALU.add),
                     reads=["ub", "cw", "vb"], writes=["vb"], toks=[("RD", 2)])
                R.op("dve", lambda e, c=c: e.scalar_tensor_tensor(out=vb, in0=ub[:, 2:1026], scalar=cw_sb[:, c, 2:3],
                                                                  in1=vb, op0=ALU.mult, op1=ALU.add),
                     reads=["ub", "cw", "vb"], writes=["vb"], toks=[("RD", 2)])
                slot = next_wslot()
                b0, b1, _ = inproj(slot, None)
                if nchunk[0] % 2 == 0 and cv_next[0] < 32:
                    issue_wout_convert(cv_next[0])
                    cv_next[0] += 1
                for hf, bk in ((0, b0), (1, b1)):
                    R.op("dve", lambda e, hf=hf, bk=bk: e.tensor_tensor(out=v2b[:, hf * 512:(hf + 1) * 512],
                                                                        in0=psb[bk][:, :],
                                                                        in1=vb[:, hf * 512:(hf + 1) * 512],
                                                                        op=ALU.mult),
                         reads=[("ps", bk), "vb"], writes=[("v2b", hf)], toks=[("RD", 2)])
                slot = next_wslot()
                b0, b1, _ = inproj(slot, None)
                if nchunk[0] % 2 == 0 and cv_next[0] < 32:
                    issue_wout_convert(cv_next[0])
                    cv_next[0] += 1
                for hf, bk in ((0, b0), (1, b1)):
                    R.op("act", lambda e, hf=hf, bk=bk: e.activation(out=thb[:, hf * 512:(hf + 1) * 512],
                                                                     in_=psb[bk][:, :], func=ACTF.Tanh, scale=0.5),
                         reads=[("ps", bk)], writes=[("thb", hf)], toks=[("RD", 2)])
                    R.op("dve", lambda e, hf=hf, bk=bk: e.scalar_tensor_tensor(
                        out=sb_[:, hf * 512:(hf + 1) * 512], in0=thb[:, hf * 512:(hf + 1) * 512], scalar=1.0,
                        in1=psb[bk][:, :], op0=ALU.add, op1=ALU.mult),
                        reads=[("thb", hf), ("ps", bk)], writes=[("sb", hf)], toks=[("RD", 2)])
                    R.op("pool", lambda e, hf=hf, c=c: e.tensor_tensor(
                        out=ycatT[:, c, hf * 512:(hf + 1) * 512], in0=sb_[:, hf * 512:(hf + 1) * 512],
                        in1=v2b[:, hf * 512:(hf + 1) * 512], op=ALU.mult),
                        reads=[("sb", hf), ("v2b", hf)], writes=[("ycat", c)], toks=[("RD", 2), ("RB", 2)])
                ci += 4
            else:
                raise AssertionError(kind)

        ycat_all = [("ycat", c) for c in range(KC)]
        pcount = [0]
        xcount = [0]
        out_ops = []
        pend_out = []

        def flush_out():
            (dst, src, tl_, r_) = pend_out.pop(0)
            out_ops.append(dma("act", dst, src, s_out[tl_ * 8 + r_], reads=[("z", tl_, r_)], name="out"))

        def final_ln(gi, r):
            pslot = pcount[0] % 2
            pcount[0] += 1
            for pi in (2, 3):
                dma("sp", par[:, pi, pslot], prow[pi, :, r * 512:(r + 1) * 512], s_par[pi][pslot],
                    writes=[("par", pi, pslot)], toks=[("RD", 3)])
            for tl in range(4):
                tslot = tl
                zsl = zbuf[:, tl, r * 512:(r + 1) * 512]
                R.op("act", lambda e, zsl=zsl, tl=tl, tslot=tslot: e.activation(
                    out=t2[:, tslot], in_=zsl, func=ACTF.Identity, scale=rz[:, tl:tl + 1],
                    bias=nmz[:, tl:tl + 1]),
                    reads=[("z", tl, r), ("rz", tl), ("nmz", tl)], writes=[("t2", tslot)], toks=[("RD", 3)])
                R.op("pool" if (gi == 1 and tl % 2 == 0) else "dve", lambda e, tslot=tslot, pslot=pslot: e.tensor_tensor(
                    out=t2[:, tslot], in0=t2[:, tslot], in1=par[:, 2, pslot], op=ALU.mult),
                    reads=[("t2", tslot), ("par", 2, pslot)], writes=[("t2", tslot)], toks=[("RD", 3)])
                R.op("dve", lambda e, zsl=zsl, tslot=tslot, pslot=pslot: e.tensor_tensor(
                    out=zsl, in0=t2[:, tslot], in1=par[:, 3, pslot], op=ALU.add),
                    reads=[("t2", tslot), ("par", 3, pslot)], writes=[("z", tl, r)], toks=[("RD", 3)])
                t0 = (gi * 4 + tl) * 128
                pend_out.append((out[t0:t0 + 128, r * 512:(r + 1) * 512], zsl, tl, r))
                while len(pend_out) > 3:
                    flush_out()

        for gi in range(2 if stop >= 6 else 0):
            for r in range(8):
                if gi == 1 and r < 4:
                    final_ln(0, 2 * r)
                    final_ln(0, 2 * r + 1)
                    if r == 3:
                        while pend_out:
                            flush_out()
                ps_par = r % 2
                banks = [4 * ps_par + tl for tl in range(4)]
                pslot = pcount[0] % 2
                pcount[0] += 1
                for pi in range(2):
                    dma("sp", par[:, pi, pslot], prow[pi, :, r * 512:(r + 1) * 512], s_par[pi][pslot],
                        writes=[("par", pi, pslot)], toks=[("RD", 3)])
                R.op("dve", lambda e, pslot=pslot: e.tensor_single_scalar(out=par[:, 1, pslot], in_=par[:, 1, pslot],
                                                                          scalar=DN_ALPHA, op=ALU.mult),
                     reads=[("par", 1, pslot)], writes=[("par", 1, pslot)], toks=[("RD", 3)])
                for cq in range(4):
                    slot = next_wslot()

                    def mm2(e, slot=slot, cq=cq, banks=banks, gi=gi):
                        ins = None
                        for c8 in range(8):
                            c = cq * 8 + c8
                            for tl in range(4):
                                t0 = (gi * 4 + tl) * 128
                                ins = e.matmul(psb[banks[tl]][:, :], ycatT[:, c, t0:t0 + 128], wring2[:, slot, c8, :],
                                               start=(c == 0), stop=(c == KC - 1))
                        return ins
                    R.op("pe", mm2, reads=[("w", slot)] + ycat_all, writes=[("ps", bk) for bk in banks],
                         toks=[("RB", 2)], name="outproj")
                    issue_next_w()
                for tl in range(4):
                    tcg = 1 + gi * 4 + tl
                    xslot = xcount[0] % 3
                    xcount[0] += 1
                    tslot = xslot % 2
                    dma("sp", xt[:, xslot], x_in[tcg * 128:(tcg + 1) * 128, r * 512:(r + 1) * 512], s_xt[xslot],
                        writes=[("xt", xslot)], toks=[("RD", 3)])
                    R.op("act", lambda e, xslot=xslot, tslot=tslot, tcg=tcg: e.activation(
                        out=t1[:, tslot], in_=xt[:, xslot], func=ACTF.Identity, scale=arstd_all[:, tcg:tcg + 1],
                        bias=namr_all[:, tcg:tcg + 1]),
                        reads=[("xt", xslot), ("arstd", tcg), ("namr", tcg)], writes=[("t1", tslot)],
                        toks=[("RD", 3)])
                    R.op("dve", lambda e, tslot=tslot, pslot=pslot: e.tensor_tensor(
                        out=t1[:, tslot], in0=t1[:, tslot], in1=par[:, 0, pslot], op=ALU.mult),
                        reads=[("t1", tslot), ("par", 0, pslot)], writes=[("t1", tslot)], toks=[("RD", 3)])
                    R.op("dve", lambda e, xslot=xslot, tslot=tslot, pslot=pslot: e.tensor_tensor(
                        out=xa[:, xslot], in0=t1[:, tslot], in1=par[:, 1, pslot], op=ALU.add),
                        reads=[("t1", tslot), ("par", 1, pslot)], writes=[("xa", xslot)], toks=[("RD", 3)])
                    zsl = zbuf[:, tl, r * 512:(r + 1) * 512]
                    R.op("dve", lambda e, zsl=zsl, bk=banks[tl], xslot=xslot: e.tensor_tensor(
                        out=zsl, in0=psb[bk][:, :], in1=xa[:, xslot], op=ALU.add),
                        reads=[("ps", banks[tl]), ("xa", xslot)], writes=[("z", tl, r)], toks=[("RA", 1)])
                    R.op("dve", lambda e, zsl=zsl, tl=tl, r=r: e.bn_stats(out=stz[:, tl, r, :], in_=zsl),
                         reads=[("z", tl, r)], writes=[("stz", tl)])
            for tl in range(4):
                R.op("dve", lambda e, tl=tl: e.bn_aggr(out=mvz[:, tl, :], in_=stz[:, tl]),
                     reads=[("stz", tl)], writes=[("mvz", tl)])
            for tl in range(4):
                R.op("dve", lambda e, tl=tl: e.tensor_single_scalar(out=vez[:, tl:tl + 1], in_=mvz[:, tl, 1:2],
                                                                    scalar=LN_EPS, op=ALU.add),
                     reads=[("mvz", tl)], writes=[("vez", tl)])
            for tl in range(4):
                R.op("pool", lambda e, tl=tl: e.tensor_tensor(out=rz[:, tl:tl + 1], in0=vez[:, tl:tl + 1], in1=mhalf,
                                                              op=ALU.pow),
                     reads=[("vez", tl), "mhalf"], writes=[("rz", tl)])
            for tl in range(4):
                R.op("dve", lambda e, tl=tl: e.tensor_scalar(out=nmz[:, tl:tl + 1], in0=mvz[:, tl, 0:1],
                                                             scalar1=rz[:, tl:tl + 1], scalar2=-1.0,
                                                             op0=ALU.mult, op1=ALU.mult),
                     reads=[("mvz", tl), ("rz", tl)], writes=[("nmz", tl)])
            if gi == 1:
                for r in range(8):
                    final_ln(1, r)
                while pend_out:
                    flush_out()
        R.op("act", None, extra_deps=out_ops, name="fence")

        R.finalize(eng_sems)
        with nc.allow_low_precision("bf16 matmul operands, fp32 PSUM accumulation"):
            with nc.Block() as block:
                @block.sync
                def _(e):
                    R.emit_engine("sp", e)

                @block.scalar
                def _(e):
                    R.emit_engine("act", e)

                @block.gpsimd
                def _(e):
                    R.emit_engine("pool", e)

                @block.vector
                def _(e):
                    R.emit_engine("dve", e)

                @block.tensor
                def _(e):
                    R.emit_engine("pe", e)
    return nc


_PROG = {}


def _get_prog():
    if "nc" not in _PROG:
        _PROG["nc"] = build_program()
    return _PROG["nc"]


def make_in_maps(x, emb_ln_g, emb_ln_b, w_in, conv_w, sink, w_out, ln_g, ln_b):
    x = np.asarray(x, dtype=np.float32)
    x2 = x.reshape(SEQ, D_MODEL)
    xp = np.zeros((SEQ + 256, D_MODEL), np.float32)
    xp[128:128 + SEQ] = x2
    w_in2 = np.ascontiguousarray(np.asarray(w_in, np.float32).reshape(D_MODEL, PROJ))
    w_out2 = np.ascontiguousarray(np.asarray(w_out, np.float32).reshape(D_MODEL, D_MODEL))
    eg = np.asarray(emb_ln_g, np.float32).reshape(D_MODEL)
    eb = np.asarray(emb_ln_b, np.float32).reshape(D_MODEL)
    lg = np.asarray(ln_g, np.float32).reshape(D_MODEL)
    lb = np.asarray(ln_b, np.float32).reshape(D_MODEL)
    gbT = np.ascontiguousarray(np.concatenate([eg.reshape(KC, 128).T, eb.reshape(KC, 128).T], axis=1))
    cw = np.asarray(conv_w, np.float32).reshape(3, 16, 128)
    cwT = np.ascontiguousarray(cw.transpose(2, 1, 0).reshape(128, 48))
    sinkb = np.ascontiguousarray(np.broadcast_to(np.asarray(sink, np.float32).reshape(1, 16), (128, 16)))
    prow = np.ascontiguousarray(np.broadcast_to(np.stack([eg, eb, lg, lb])[:, None, :], (4, 128, D_MODEL)))
    in_maps = []
    for c in range(NCORES):
        kb = np.zeros((128, 12), np.float32)
        kb[:, 10] = 1.0
        kb[:, 11] = 1.0
        if c == 0:
            kb[:, 0] = NEGBIG
            kb[:, 10] = 0.0
        if c == NCORES - 1:
            kb[:, 9] = NEGBIG
            kb[:, 11] = 0.0
        in_maps.append({
            "x_in": np.ascontiguousarray(xp[c * TOK:c * TOK + TOK + 256]),
            "w_in": w_in2, "w_out": w_out2, "gbT": gbT, "cwT": cwT, "sinkb": sinkb, "kbias": kb, "prow": prow,
        })
    return in_maps


def kernel(x, emb_ln_g, emb_ln_b, w_in, conv_w, sink, w_out, ln_g, ln_b):
    in_maps = make_in_maps(x, emb_ln_g, emb_ln_b, w_in, conv_w, sink, w_out, ln_g, ln_b)
    nc = _get_prog()
    res = run_bass_kernel_spmd(nc, in_maps, core_ids=list(range(NCORES)))
    outs = [np.asarray(r["out"], dtype=np.float32) for r in res.results]
    return np.concatenate(outs, axis=0).reshape(1, SEQ, D_MODEL)
```

```python
import contextlib
import numpy as np
import concourse.bass as bass
import concourse.mybir as mybir
from concourse.bass_utils import run_bass_kernel_spmd

F32 = mybir.dt.float32
BF16 = mybir.dt.bfloat16
ALU = mybir.AluOpType
ACTF = mybir.ActivationFunctionType

NCORES = 8
D_MODEL = 4096
SEQ = 8192
TOK = SEQ // NCORES
NTC = TOK // 128
KC = D_MODEL // 128
CONV_W = 2048
HD = 128
OFF_CB, OFF_CC, OFF_CH, OFF_CZ = 0, 2048, 4096, 6144
OFF_Q, OFF_K, OFF_V, OFF_AZ = 8192, 10240, 10752, 11264
PROJ = 13312
DN_ALPHA = (2.0 * 1) ** 0.25
LN_EPS = 1e-5
SCALE = HD ** -0.5
NEGBIG = -30000.0
NW = 3
NFILL = 6
import os as _os
SKIP = set(_os.environ.get("KSKIP", "").split(","))


class Op:
    __slots__ = ("eng", "emit", "deps", "sig", "tick", "csem", "is_dma", "uid", "name")


class Rec:
    ENGS = ("sp", "act", "pool", "dve", "pe")

    def __init__(self):
        self.ops = {e: [] for e in self.ENGS}
        self.keyw = {}
        self.keyr = {}
        self.tok = {}
        self.uid = 0
        self.dma_counts = {}

    def op(self, eng, emit, reads=(), writes=(), toks=(), dma_sem=None, name="", extra_deps=()):
        if SKIP and ((eng in SKIP) or (name and name.split(" ")[0] in SKIP)):
            emit = None if dma_sem is None else emit
            if dma_sem is not None:
                return None
        ps_r = [k for k in reads if isinstance(k, tuple) and k[0] == "ps"]
        if ps_r:
            reads = [k for k in reads if not (isinstance(k, tuple) and k[0] == "ps")]
            writes = list(writes) + [k for k in ps_r if k not in writes]
        o = Op()
        o.eng, o.emit, o.deps, o.sig, o.tick = eng, emit, set(), False, None
        o.is_dma = dma_sem is not None
        o.csem = dma_sem
        o.uid = self.uid
        o.name = name
        self.uid += 1
        ek = ("dma", o.uid) if o.is_dma else eng
        for k in reads:
            for w in self.keyw.get(k, {}).values():
                o.deps.add(w)
        for k in writes:
            for w in self.keyw.get(k, {}).values():
                o.deps.add(w)
            for r in self.keyr.get(k, {}).values():
                o.deps.add(r)
        for (tname, ph) in toks:
            st = self.tok.setdefault(tname, [ph, {}, {}])
            assert ph >= st[0], (tname, ph, st[0])
            if ph > st[0]:
                st[0], st[2], st[1] = ph, st[1], {}
            for p in st[2].values():
                o.deps.add(p)
        for d in extra_deps:
            o.deps.add(d)
        o.deps.discard(o)
        if eng == "pe":
            o.deps = {d for d in o.deps if d.is_dma or d.eng != "pe"}
        for d in o.deps:
            d.sig = True
        for k in reads:
            self.keyr.setdefault(k, {})[ek] = o
        for k in writes:
            if self.keyr.get(k):
                self.keyw[k] = {ek: o}
                self.keyr[k] = {}
            else:
                self.keyw.setdefault(k, {})[ek] = o
        for (tname, ph) in toks:
            self.tok[tname][1][ek] = o
        if o.is_dma:
            c = self.dma_counts.get(id(dma_sem), 0) + 16
            self.dma_counts[id(dma_sem)] = c
            o.tick = c
            o.sig = True
        self.ops[eng].append(o)
        return o

    def finalize(self, eng_sems):
        for e in self.ENGS:
            n = 0
            for o in self.ops[e]:
                if o.is_dma:
                    continue
                if o.sig:
                    n += 1
                    o.tick = n
                    o.csem = eng_sems[e]

    def emit_engine(self, e, handle):
        waited = {}
        for o in self.ops[e]:
            for d in sorted(o.deps, key=lambda t: t.uid):
                assert d.tick is not None, (d.name, o.name)
                k = id(d.csem)
                if waited.get(k, 0) >= d.tick:
                    continue
                handle.wait_ge(d.csem, d.tick)
                waited[k] = d.tick
            if o.emit is None:
                if o.sig and not o.is_dma:
                    handle.nop().then_inc(o.csem, 1)
                continue
            ins = o.emit(handle)
            if o.is_dma:
                ins.then_inc(o.csem, 16)
            elif o.sig:
                ins.then_inc(o.csem, 1)


def build_program(stop=99):
    nc = bass.Bass("TRN2", target_bir_lowering=False)
    x_in = nc.dram_tensor("x_in", [TOK + 256, D_MODEL], F32, kind="ExternalInput").ap()
    w_in = nc.dram_tensor("w_in", [D_MODEL, PROJ], F32, kind="ExternalInput").ap()
    w_out = nc.dram_tensor("w_out", [D_MODEL, D_MODEL], F32, kind="ExternalInput").ap()
    gbT = nc.dram_tensor("gbT", [128, 2 * KC], F32, kind="ExternalInput").ap()
    cwT = nc.dram_tensor("cwT", [128, 16 * 3], F32, kind="ExternalInput").ap()
    sinkb = nc.dram_tensor("sinkb", [128, 16], F32, kind="ExternalInput").ap()
    kbias_d = nc.dram_tensor("kbias", [128, 12], F32, kind="ExternalInput").ap()
    prow = nc.dram_tensor("prow", [4, 128, D_MODEL], F32, kind="ExternalInput").ap()
    out = nc.dram_tensor("out", [TOK, D_MODEL], F32, kind="ExternalOutput").ap()

    wob = nc.dram_tensor("wob", [D_MODEL, D_MODEL], BF16).ap()
    w_in_v = w_in.rearrange("(kc p) f -> p kc f", p=128)
    w_out_v = wob.rearrange("(c p) d -> p c d", p=128)

    R = Rec()
    es = contextlib.ExitStack()
    with es:
        KB = 1024
        ARENA_BYTES = 205 * KB
        arena = es.enter_context(nc.sbuf_tensor("arena", [128, ARENA_BYTES // 2], BF16))

        def carve(off_b, nbytes, dt, shape=None):
            a = arena[:, off_b // 2:(off_b + nbytes) // 2]
            if dt == F32:
                a = a.bitcast(F32)
            if shape is not None and len(shape) > 1:
                names = " ".join("a%d" % i for i in range(len(shape)))
                kw = {"a%d" % i: s for i, s in enumerate(shape[:-1])}
                a = a.rearrange("p (%s) -> p %s" % (names, names), **kw)
            return a

        A0 = 0
        B0 = 64 * KB
        C0 = 128 * KB
        D0 = C0 + NW * 8 * KB
        G0 = D0 + 44 * KB
        hT = carve(A0, 64 * KB, BF16, [KC, TOK])
        zbuf = carve(A0, 64 * KB, F32, [4, D_MODEL])
        ycatT = carve(B0, 64 * KB, BF16, [KC, TOK])
        bt_hi = carve(B0, 1536, BF16, [3, 256])
        bt_lo = carve(B0 + 1536, 1536, BF16, [3, 256])
        identb = carve(B0 + 3072, 2048, BF16, [8, 128])
        btf = carve(B0 + 5120, 3072, F32, [3, 256])
        pT = carve(B0 + 12 * KB, 6 * KB, BF16, [2, 3, 512])
        densb = carve(B0 + 24 * KB, 2 * KB, F32, [512])
        o1 = carve(B0 + 26 * KB, 2 * KB, F32, [512])
        pvsb = carve(B0 + 28 * KB, 2 * KB, F32, [512])
        thz = carve(B0 + 30 * KB, 2 * KB, F32, [1, 512])
        xs = carve(B0, 64 * KB, F32, [4, D_MODEL])
        wring = carve(C0, NW * 8 * KB, BF16, [NW, KC, 128])
        wring2 = carve(C0, NW * 8 * KB, BF16, [NW, 8, 512])
        hTh = carve(D0, 16 * KB, BF16, [KC, 256])
        siluz = carve(D0, 16 * KB, F32, [4, TOK])
        KT = carve(D0 + 16 * KB, 10 * KB, BF16, [4, 10 * 128])
        Vt = carve(D0 + 26 * KB, 10 * KB, BF16, [10, 512])
        QT = carve(D0 + 36 * KB, 8 * KB, BF16, [4, TOK])
        CW = 1032
        Hs = carve(D0, CW * 4, F32, [CW])
        ub = carve(D0 + CW * 4, CW * 4, F32, [CW])
        vb = carve(D0 + 2 * CW * 4, 4 * KB, F32, [TOK])
        v2b = carve(D0 + 2 * CW * 4 + 4 * KB, 4 * KB, F32, [TOK])
        thb = carve(D0 + 2 * CW * 4 + 8 * KB, 4 * KB, F32, [TOK])
        sb_ = carve(D0 + 2 * CW * 4 + 12 * KB, 4 * KB, F32, [TOK])
        par = carve(D0, 16 * KB, F32, [4, 2, 512])
        xt = carve(D0 + 16 * KB, 6 * KB, F32, [3, 512])
        t1 = carve(D0 + 22 * KB, 4 * KB, F32, [2, 512])
        xa = carve(D0 + 26 * KB, 6 * KB, F32, [3, 512])
        t2 = carve(D0 + 32 * KB, 8 * KB, F32, [4, 512])
        g = [G0]

        def galloc(nbytes, dt, shape=None):
            nb = (nbytes + 63) // 64 * 64
            a = carve(g[0], nb, dt, None)
            n_el = nbytes // (4 if dt == F32 else 2)
            a = a[:, 0:n_el]
            if shape is not None and len(shape) > 1:
                names = " ".join("a%d" % i for i in range(len(shape)))
                kw = {"a%d" % i: s for i, s in enumerate(shape[:-1])}
                a = a.rearrange("p (%s) -> p %s" % (names, names), **kw)
            g[0] += nb
            return a

        gb_sb = galloc(2 * KC * 4, F32)
        cw_sb = galloc(48 * 4, F32, [16, 3])
        sink2 = galloc(16 * 4, F32)
        kb_sb = galloc(12 * 4, F32)
        ident = galloc(128 * 4, F32)
        ones_b = galloc(128 * 2, BF16)
        hTe = galloc(KC * 2 * 2, BF16, [KC, 2])
        bnst = galloc(2 * 8 * 6 * 4, F32, [2, 8, 6])
        mv = galloc(2 * 2 * 4, F32, [2, 2])
        mu_all = galloc(10 * 4, F32)
        rstd_all = galloc(10 * 4, F32)
        arstd_all = galloc(10 * 4, F32)
        nmr_all = galloc(10 * 4, F32)
        namr_all = galloc(10 * 4, F32)
        nmz = galloc(4 * 4, F32)
        veps = galloc(10 * 4, F32)
        mhalf = galloc(4, F32)
        stz = galloc(4 * 8 * 6 * 4, F32, [4, 8, 6])
        mvz = galloc(4 * 2 * 4, F32, [4, 2])
        vez = galloc(4 * 4, F32)
        rz = galloc(4 * 4, F32)
        dD = carve(D0 + 36 * KB, 1536, F32, [3, 128])
        absD = carve(D0 + 36 * KB + 1536, 1536, F32, [3, 128])
        m01 = carve(D0 + 36 * KB + 3072, 1536, F32, [3, 128])
        tmpE = carve(D0 + 36 * KB + 4608, 1536, F32, [3, 128])
        assert g[0] <= ARENA_BYTES, g[0]

        psb = [es.enter_context(nc.psum_tensor("ps%d" % i, [128, 512], F32)) for i in range(8)]

        def sem(name):
            return es.enter_context(nc.semaphore(name))

        eng_sems = {e: sem("s_" + e) for e in Rec.ENGS}
        s_const = [sem("d_const%d" % i) for i in range(4)]
        s_xs = [sem("d_xs%d" % i) for i in range(4)]
        s_w = [sem("d_w%d" % i) for i in range(NW)]
        s_par = [[sem("d_par%d_%d" % (a, i)) for i in range(2)] for a in range(4)]
        s_xt = [sem("d_xt%d" % i) for i in range(3)]
        s_out = [sem("d_out%d" % i) for i in range(32)]

        def dma(eng, out_ap, in_ap, dsem, reads=(), writes=(), toks=(), name=""):
            return R.op(eng, lambda e: e.dma_start(out=out_ap, in_=in_ap), reads=reads, writes=writes,
                        toks=toks, dma_sem=dsem, name=name)

        dma("sp", gb_sb, gbT[:, :], s_const[0], writes=["gb"])
        dma("sp", cw_sb.rearrange("p a b -> p (a b)"), cwT[:, :], s_const[1], writes=["cw"])
        dma("sp", sink2, sinkb[:, :], s_const[2], writes=["sink"])
        dma("sp", kb_sb, kbias_d[:, :], s_const[3], writes=["kb"])

        R.op("pool", lambda e: e.memset(ones_b, 1.0), writes=["ones"])
        R.op("pool", lambda e: e.memset(mhalf, -0.5), writes=["mhalf"])
        R.op("pool", lambda e: e.iota(ident, pattern=[[1, 128]], base=0, channel_multiplier=-1,
                                      allow_small_or_imprecise_dtypes=True), writes=["ident"])
        R.op("pool", lambda e: e.tensor_single_scalar(out=ident, in_=ident, scalar=0.0, op=ALU.is_equal),
             reads=["ident"], writes=["ident"])
        R.op("act", lambda e: e.activation(out=sink2, in_=sink2, func=ACTF.Exp), reads=["sink"], writes=["sink"],
             name="actsink")
        R.op("dve", lambda e: e.tensor_single_scalar(out=sink2, in_=sink2, scalar=2.0, op=ALU.mult),
             reads=["sink"], writes=["sink"])
        R.op("dve", lambda e: e.tensor_single_scalar(out=cw_sb, in_=cw_sb, scalar=0.5, op=ALU.mult),
             reads=["cw"], writes=["cw"])

        wcount = [0]

        def w_dma_in(col):
            slot = wcount[0] % NW
            wcount[0] += 1
            dma("pool", wring[:, slot], w_in_v[:, :, col:col + 128], s_w[slot], writes=[("w", slot)],
                name="w_in %d" % col)
            return slot

        def w_dma_out(r, cq):
            slot = wcount[0] % NW
            wcount[0] += 1
            dma("pool", wring2[:, slot], w_out_v[:, cq * 8:(cq + 1) * 8, r * 512:(r + 1) * 512], s_w[slot],
                reads=[("wob", r * 4 + cq)], writes=[("w", slot)], name="w_out %d %d" % (r, cq))
            return slot

        sched = []
        for j in range(4):
            sched.append(("K", j, OFF_K + 128 * j))
        for j in range(4):
            sched.append(("V", j, OFF_V + 128 * j))
        for j in range(4):
            for gg in range(4):
                sched.append(("Q", 4 * j + gg, OFF_Q + 128 * (4 * j + gg)))
            for gg in range(4):
                sched.append(("AZ", 4 * j + gg, OFF_AZ + 128 * (4 * j + gg)))
        for c in range(16):
            sched.append(("H", c, OFF_CH + 128 * c))
            sched.append(("C", c, OFF_CC + 128 * c))
            sched.append(("B", c, OFF_CB + 128 * c))
            sched.append(("Z", c, OFF_CZ + 128 * c))
        wq = [("in", s[2]) for s in sched]
        for gi in range(2):
            for r in range(8):
                for cq in range(4):
                    wq.append(("out", r, cq))
        wq_pos = [0]
        wslots = []

        def issue_next_w():
            if wq_pos[0] >= len(wq):
                return
            it = wq[wq_pos[0]]
            wq_pos[0] += 1
            if it[0] == "in":
                wslots.append(w_dma_in(it[1]))
            else:
                wslots.append(w_dma_out(it[1], it[2]))

        for _ in range(NW):
            issue_next_w()
        wuse = [0]

        def next_wslot():
            s = wslots[wuse[0]]
            wuse[0] += 1
            return s

        tp_bank = [0]

        def p0_load_stats(tc, slot):
            bs = tc % 2
            dma("sp", xs[:, slot], x_in[tc * 128:(tc + 1) * 128, :], s_xs[slot], writes=[("xs", slot)],
                toks=[("RB", 0)], name="x load")
            for q in range(8):
                R.op("dve", lambda e, q=q, slot=slot, bs=bs: e.bn_stats(out=bnst[:, bs, q, :],
                                                                       in_=xs[:, slot, q * 512:(q + 1) * 512]),
                     reads=[("xs", slot)], writes=[("bnst", bs)], toks=[("RB", 0)])
            R.op("dve", lambda e, bs=bs: e.bn_aggr(out=mv[:, bs, :], in_=bnst[:, bs]),
                 reads=[("bnst", bs)], writes=[("mv", bs)])
            R.op("dve", lambda e, bs=bs, tc=tc: e.tensor_single_scalar(out=veps[:, tc:tc + 1], in_=mv[:, bs, 1:2],
                                                                       scalar=LN_EPS, op=ALU.add),
                 reads=[("mv", bs)], writes=[("veps", tc)])
            R.op("dve", lambda e, bs=bs, tc=tc: e.tensor_copy(out=mu_all[:, tc:tc + 1], in_=mv[:, bs, 0:1]),
                 reads=[("mv", bs)], writes=[("mu", tc)])
            R.op("pool", lambda e, tc=tc: e.tensor_tensor(out=rstd_all[:, tc:tc + 1], in0=veps[:, tc:tc + 1],
                                                          in1=mhalf, op=ALU.pow),
                 reads=[("veps", tc), "mhalf"], writes=[("rstd", tc)])
            R.op("dve", lambda e, tc=tc: e.tensor_scalar(out=nmr_all[:, tc:tc + 1], in0=mu_all[:, tc:tc + 1],
                                                         scalar1=rstd_all[:, tc:tc + 1], scalar2=-1.0,
                                                         op0=ALU.mult, op1=ALU.mult),
                 reads=[("mu", tc), ("rstd", tc)], writes=[("nmr", tc)])

        def p0_norm(tc, slot):
            R.op("act", lambda e, slot=slot, tc=tc: e.activation(out=xs[:, slot], in_=xs[:, slot], func=ACTF.Identity,
                                                                 scale=rstd_all[:, tc:tc + 1],
                                                                 bias=nmr_all[:, tc:tc + 1]),
                 reads=[("xs", slot), ("rstd", tc), ("nmr", tc)], writes=[("xs", slot)], toks=[("RB", 0)],
                 name="actnorm")

        def p0_small(tc):
            R.op("dve", lambda e, tc=tc: e.tensor_single_scalar(out=namr_all[:, tc:tc + 1], in_=nmr_all[:, tc:tc + 1],
                                                                scalar=DN_ALPHA, op=ALU.mult),
                 reads=[("nmr", tc)], writes=[("namr", tc)])
            R.op("dve", lambda e, tc=tc: e.tensor_single_scalar(out=arstd_all[:, tc:tc + 1],
                                                                in_=rstd_all[:, tc:tc + 1], scalar=DN_ALPHA,
                                                                op=ALU.mult),
                 reads=[("rstd", tc)], writes=[("arstd", tc)])

        def p0_transpose(tca, tcb, sa, sb2):
            for kq in range(16):
                bank = tp_bank[0] % 4
                tp_bank[0] += 1

                def tr(e, kq=kq, bank=bank):
                    ins = None
                    for k2 in range(2):
                        kc = kq * 2 + k2
                        for ti, sl in ((0, sa), (1, sb2)):
                            c0 = k2 * 256 + ti * 128
                            ins = e.transpose(psb[bank][:, c0:c0 + 128], xs[:, sl, kc * 128:(kc + 1) * 128], ident)
                    return ins
                R.op("pe", tr, reads=[("xs", sa), ("xs", sb2), "ident"], writes=[("ps", bank)], toks=[("RB", 0)])
                for k2 in range(2):
                    kc = kq * 2 + k2
                    if tca == 0:
                        dst = hTh[:, kc, 0:256]
                        wk = ("hTh", kc)
                    else:
                        dst = hT[:, kc, (tca - 1) * 128:(tca + 1) * 128]
                        wk = ("hT", kc, tca - 1)
                    wk2 = ("hT", kc, tcb - 1) if tca != 0 else ("hTh", kc)
                    src = psb[bank][:, k2 * 256:(k2 + 1) * 256]
                    if bank != 3:
                        R.op("act", lambda e, dst=dst, src=src, kc=kc: e.activation(
                            out=dst, in_=src, func=ACTF.Identity, scale=gb_sb[:, kc:kc + 1],
                            bias=gb_sb[:, KC + kc:KC + kc + 1]),
                            reads=[("ps", bank), "gb"], writes=[wk, wk2], toks=[("RD", 0)], name="actevac")
                    else:
                        R.op("dve", lambda e, dst=dst, src=src, kc=kc: e.tensor_scalar(
                            out=dst, in0=src, scalar1=gb_sb[:, kc:kc + 1], scalar2=gb_sb[:, KC + kc:KC + kc + 1],
                            op0=ALU.mult, op1=ALU.add),
                            reads=[("ps", bank), "gb"], writes=[wk, wk2], toks=[("RD", 0)])

        pairs = [(0, 9), (1, 2), (3, 4), (5, 6), (7, 8)]
        p0_load_stats(0, 0)
        p0_load_stats(9, 1)
        p0_norm(0, 0)
        p0_norm(9, 1)
        for pi_, (ta, tb_) in enumerate(pairs):
            sa, sb2 = 2 * (pi_ % 2), 2 * (pi_ % 2) + 1
            if pi_ + 1 < len(pairs):
                na, nb = pairs[pi_ + 1]
                p0_load_stats(na, 2 * ((pi_ + 1) % 2))
                p0_load_stats(nb, 2 * ((pi_ + 1) % 2) + 1)
            p0_transpose(ta, tb_, sa, sb2)
            p0_small(ta)
            p0_small(tb_)
            if pi_ + 1 < len(pairs):
                p0_norm(na, 2 * ((pi_ + 1) % 2))
                p0_norm(nb, 2 * ((pi_ + 1) % 2) + 1)

        R.op("pool", lambda e: e.tensor_copy(out=hTe, in_=hTh[:, :, 127:129]),
             reads=[("hTh", kc) for kc in range(KC)], writes=["hTe"], toks=[("RD", 0)])

        if stop <= 0:
            sched_stop = True
        R.op("pool", lambda e: e.iota(dD.rearrange("p a b -> p (a b)"), pattern=[[-128, 3], [1, 128]], base=128,
                                      channel_multiplier=-1, allow_small_or_imprecise_dtypes=True),
             writes=["dD"], toks=[("RD", 0)])
        R.op("act", lambda e: e.activation(out=absD, in_=dD, func=ACTF.Abs),
             reads=["dD"], writes=["absD"], toks=[("RD", 0)], name="actabs")
        R.op("dve", lambda e: e.tensor_single_scalar(out=m01, in_=absD, scalar=128.0, op=ALU.is_le),
             reads=["absD"], writes=["m01"], toks=[("RD", 0)])
        BIGB = 340000.0
        R.op("dve", lambda e: e.tensor_scalar(out=tmpE, in0=m01, scalar1=BIGB, scalar2=-BIGB, op0=ALU.mult,
                                              op1=ALU.add),
             reads=["m01"], writes=["tmpE"], toks=[("RD", 0)])
        for gg in range(2):
            slope = 2.0 ** (-8.0 * (gg + 1) / 16.0)
            dst = btf[:, :, gg * 128:(gg + 1) * 128]
            R.op("dve", lambda e, dst=dst, sl=slope: e.tensor_single_scalar(out=dst, in_=absD, scalar=-sl / SCALE,
                                                                            op=ALU.mult),
                 reads=["absD"], writes=["btf"], toks=[("RD", 0), ("RB", 1)])
            R.op("dve", lambda e, dst=dst: e.tensor_tensor(out=dst, in0=dst, in1=m01, op=ALU.mult),
                 reads=["btf", "m01"], writes=["btf"], toks=[("RD", 0), ("RB", 1)])
            R.op("dve", lambda e, dst=dst: e.tensor_tensor(out=dst, in0=dst, in1=tmpE, op=ALU.add),
                 reads=["btf", "tmpE"], writes=["btf"], toks=[("RD", 0), ("RB", 1)])
        R.op("dve", lambda e: e.tensor_copy(out=bt_hi, in_=btf), reads=["btf"], writes=["bt_hi"], toks=[("RB", 1)])
        R.op("dve", lambda e: e.tensor_tensor(out=bt_lo, in0=btf, in1=bt_hi, op=ALU.subtract),
             reads=["btf", "bt_hi"], writes=["bt_lo"], toks=[("RB", 1)])
        for k in range(8):
            R.op("dve", lambda e, k=k: e.tensor_single_scalar(out=identb[:, k, :], in_=ident, scalar=2.0 ** (-k),
                                                              op=ALU.mult),
                 reads=["ident"], writes=["identb"], toks=[("RB", 1)])
        hT_all = [("hT", kc, t) for kc in range(KC) for t in range(NTC)]
        hTh_all = [("hTh", kc) for kc in range(KC)]

        nchunk = [0]

        def inproj(slot, halo):
            par_ = nchunk[0] % 2
            nchunk[0] += 1
            b0, b1 = 2 * par_, 2 * par_ + 1
            if halo == "kv":
                hN = 256
            elif halo == "edge":
                hN = 2
            else:
                hN = 0
            hb = 4 + par_
            hps = psb[hb][:, 0:hN] if hN else None

            def mm(e):
                ins = None
                for kc in range(KC):
                    w = wring[:, slot, kc, :]
                    st, sp_ = (kc == 0), (kc == KC - 1)
                    e.matmul(psb[b0][:, :], w, hT[:, kc, 0:512], start=st, stop=sp_)
                    ins = e.matmul(psb[b1][:, :], w, hT[:, kc, 512:1024], start=st, stop=sp_)
                    if halo == "kv":
                        ins = e.matmul(hps, w, hTh[:, kc, :], start=st, stop=sp_)
                    elif halo == "edge":
                        ins = e.matmul(hps, w, hTe[:, kc, :], start=st, stop=sp_)
                return ins
            reads = [("w", slot)] + hT_all
            writes = [("ps", b0), ("ps", b1)]
            toks = [("RA", 0)]
            if halo == "kv":
                reads += hTh_all
                writes.append(("ps", hb))
                toks.append(("RD", 0))
            elif halo == "edge":
                reads.append("hTe")
                writes.append(("ps", hb))
            R.op("pe", mm, reads=reads, writes=writes, toks=toks, name="inproj")
            issue_next_w()
            return b0, b1, (hps, hb)

        def attention(j):
            NS = 8

            def qk(i):
                banks = (5, 6, 7)

                def f(e):
                    ins = None
                    for b in range(3):
                        kb = i + b
                        e.matmul(psb[banks[b]][:, :], KT[:, j, kb * 128:(kb + 1) * 128],
                                 QT[:, :, i * 128:(i + 1) * 128], start=True, stop=False)
                        for gp in range(2):
                            k = 2 * j + gp
                            o_ = psb[banks[b]][:, gp * 256:(gp + 1) * 256]
                            e.matmul(o_, identb[:, k, :], bt_hi[:, b, :], start=False, stop=False)
                            ins = e.matmul(o_, identb[:, k, :], bt_lo[:, b, :], start=False, stop=(gp == 1))
                    return ins
                R.op("pe", f, reads=["KT", ("QT", j), "identb", "bt_hi", "bt_lo"],
                     writes=[("ps", bk) for bk in banks], toks=[("RD", 1), ("RB", 1)])
                for b in range(3):
                    kb = i + b
                    R.op("act", lambda e, b=b, kb=kb, bk=banks[b]: e.activation(
                        out=pT[:, i % 2, b], in_=psb[bk][:, :], func=ACTF.Exp, scale=SCALE,
                        bias=kb_sb[:, kb:kb + 1]),
                        reads=[("ps", banks[b]), "kb"], writes=[("pT", i % 2, b)], toks=[("RB", 1)])

            def pvd(i):
                def f(e):
                    ins = None
                    for b in range(3):
                        kb = i + b
                        e.matmul(psb[3][:, :], Vt[:, kb, j * 128:(j + 1) * 128], pT[:, i % 2, b],
                                 start=(b == 0), stop=(b == 2))
                        ins = e.matmul(psb[4][:, :], ones_b, pT[:, i % 2, b], start=(b == 0), stop=(b == 2))
                    return ins
                R.op("pe", f, reads=["Vt", "ones"] + [("pT", i % 2, b) for b in range(3)],
                     writes=[("ps", 3), ("ps", 4)], toks=[("RD", 1), ("RB", 1)])
                sk = sink2[:, 4 * j:4 * j + 4].unsqueeze(2).broadcast_to([128, 4, 128])
                R.op("dve", lambda e: e.scalar_tensor_tensor(
                    out=densb.rearrange("p (a b) -> p a b", a=4), in0=psb[4][:, :].rearrange("p (a b) -> p a b", a=4),
                    scalar=2.0, in1=sk, op0=ALU.mult, op1=ALU.add),
                    reads=[("ps", 4), "sink"], writes=["densb"], toks=[("RB", 1)])
                R.op("act", lambda e: e.activation(out=pvsb, in_=psb[3][:, :], func=ACTF.Identity),
                     reads=[("ps", 3)], writes=["pvsb"], toks=[("RB", 1)])
                R.op("act", lambda e: e.activation(out=densb, in_=densb, func=ACTF.Ln), reads=["densb"],
                     writes=["densb"], toks=[("RB", 1)])
                R.op("act", lambda e: e.activation(out=densb, in_=densb, func=ACTF.Exp, scale=-1.0), reads=["densb"],
                     writes=["densb"], toks=[("RB", 1)])
                R.op("dve", lambda e: e.tensor_tensor(out=o1, in0=pvsb, in1=densb, op=ALU.mult),
                     reads=["pvsb", "densb"], writes=["o1"], toks=[("RB", 1)])
                R.op("dve", lambda e: e.tensor_tensor(
                    out=ycatT[:, 16 + 4 * j:16 + 4 * j + 4, i * 128:(i + 1) * 128],
                    in0=o1.rearrange("p (a b) -> p a b", a=4), in1=siluz[:, :, i * 128:(i + 1) * 128], op=ALU.mult),
                    reads=["o1", ("siluz", j)], writes=[("ycat", 16 + 4 * j + gg) for gg in range(4)],
                    toks=[("RB", 1), ("RD", 1)])

            def filler(n):
                def f(e):
                    ins = None
                    for _ in range(n):
                        ins = e.matmul(psb[0][:, :], ones_b, pT[:, 0, 0], start=True, stop=True)
                    return ins
                R.op("pe", f, reads=["ones"], writes=[("ps", 0)], toks=[("RB", 1)])

            for i in range(NS):
                qk(i)
                if i >= 1:
                    pvd(i - 1)
            pvd(NS - 1)

        s_cvin = [sem("d_cvin%d" % i) for i in range(2)]
        s_cvout = [sem("d_cvout%d" % i) for i in range(2)]
        cvb = carve(D0 + 26 * KB, 16 * KB, BF16, [2, 8, 512])
        w_out_f = w_out.rearrange("(c p) d -> p c d", p=128)
        NCV = 32

        def issue_wout_convert(n):
            r_, cq_ = n // 4, n % 4
            sl = n % 2
            dma("pool", cvb[:, sl], w_out_f[:, cq_ * 8:(cq_ + 1) * 8, r_ * 512:(r_ + 1) * 512], s_cvin[sl],
                writes=[("cvb", sl)], toks=[("RD", 2)], name="cvin %d" % n)
            dma("sp", w_out_v[:, cq_ * 8:(cq_ + 1) * 8, r_ * 512:(r_ + 1) * 512], cvb[:, sl], s_cvout[sl],
                reads=[("cvb", sl)], writes=[("wob", n)], toks=[("RD", 2)], name="cvout %d" % n)
        cv_next = [0]
        ci = 0
        nlim = {0: 0, 1: 4, 2: 8, 3: 16, 4: 40}.get(stop, len(sched))
        while ci < min(len(sched), nlim):
            kind, idx, col = sched[ci]
            if kind == "K":
                slot = next_wslot()
                b0, b1, (hps, hb) = inproj(slot, "kv")
                j = idx
                R.op("act", lambda e, j=j, b0=b0: e.activation(out=KT[:, j, 128:640], in_=psb[b0][:, :],
                                                               func=ACTF.Identity),
                     reads=[("ps", b0)], writes=["KT"], toks=[("RD", 0)])
                R.op("dve", lambda e, j=j, b1=b1: e.tensor_copy(out=KT[:, j, 640:1152], in_=psb[b1][:, :]),
                     reads=[("ps", b1)], writes=["KT"], toks=[("RD", 0)])
                R.op("act", lambda e, j=j, hps=hps: e.activation(out=KT[:, j, 0:128], in_=hps[:, 0:128],
                                                                 func=ACTF.Identity),
                     reads=[("ps", hb)], writes=["KT"], toks=[("RD", 0)])
                R.op("act", lambda e, j=j, hps=hps: e.activation(out=KT[:, j, 1152:1280], in_=hps[:, 128:256],
                                                                 func=ACTF.Identity),
                     reads=[("ps", hb)], writes=["KT"], toks=[("RD", 0)])
                ci += 1
            elif kind == "V":
                assert idx % 2 == 0
                s0 = next_wslot()
                s1 = next_wslot()
                if s1 == s0 + 1:
                    pieces = [(wring[:, s0:s0 + 2], 256, 0)]
                else:
                    pieces = [(wring[:, s0:s0 + 1], 128, 0), (wring[:, s1:s1 + 1], 128, 128)]
                for tb in range(10):
                    bank = tb % 4
                    if tb == 0:
                        lt = lambda kc: hTh[:, kc, 0:128]
                    elif tb == 9:
                        lt = lambda kc: hTh[:, kc, 128:256]
                    else:
                        lt = lambda kc, tb=tb: hT[:, kc, (tb - 1) * 128:tb * 128]

                    def mmv(e, lt=lt, bank=bank, pieces=pieces):
                        ins = None
                        for (wap, n, off) in pieces:
                            for kc in range(KC):
                                ins = e.matmul(psb[bank][:, off:off + n], lt(kc), wap[:, :, kc, :],
                                               start=(kc == 0), stop=(kc == KC - 1))
                        return ins
                    R.op("pe", mmv, reads=[("w", s0), ("w", s1)] + hT_all + hTh_all, writes=[("ps", bank)],
                         toks=[("RA", 0), ("RD", 0)])
                    dstv = Vt[:, tb, idx * 128:(idx + 2) * 128]
                    if tb % 2 == 0:
                        R.op("act", lambda e, dstv=dstv, bank=bank: e.activation(out=dstv, in_=psb[bank][:, 0:256],
                                                                                 func=ACTF.Identity),
                             reads=[("ps", bank)], writes=["Vt"], toks=[("RD", 0)])
                    else:
                        R.op("dve", lambda e, dstv=dstv, bank=bank: e.tensor_copy(out=dstv, in_=psb[bank][:, 0:256]),
                             reads=[("ps", bank)], writes=["Vt"], toks=[("RD", 0)])
                issue_next_w()
                issue_next_w()
                ci += 2
            elif kind == "Q":
                slot = next_wslot()
                b0, b1, _ = inproj(slot, None)
                gg, j = idx % 4, idx // 4
                R.op("act", lambda e, gg=gg, b0=b0: e.activation(out=QT[:, gg, 0:512], in_=psb[b0][:, :],
                                                                 func=ACTF.Identity),
                     reads=[("ps", b0)], writes=[("QT", j)], toks=[("RD", 1)])
                R.op("dve", lambda e, gg=gg, b1=b1: e.tensor_copy(out=QT[:, gg, 512:1024], in_=psb[b1][:, :]),
                     reads=[("ps", b1)], writes=[("QT", j)], toks=[("RD", 1)])
                ci += 1
            elif kind == "AZ":
                slot = next_wslot()
                b0, b1, _ = inproj(slot, None)
                gg, j = idx % 4, idx // 4
                for hf, bk in ((0, b0), (1, b1)):
                    R.op("act", lambda e, hf=hf, bk=bk: e.activation(out=thz[:, 0], in_=psb[bk][:, :],
                                                                     func=ACTF.Tanh, scale=0.5),
                         reads=[("ps", bk)], writes=[("thz", 0)], toks=[("RB", 1)])
                    R.op("dve", lambda e, hf=hf, bk=bk, gg=gg: e.scalar_tensor_tensor(
                        out=siluz[:, gg, hf * 512:(hf + 1) * 512], in0=thz[:, 0], scalar=1.0, in1=psb[bk][:, :],
                        op0=ALU.add, op1=ALU.mult),
                        reads=[("thz", 0), ("ps", bk)], writes=[("siluz", j)], toks=[("RB", 1), ("RD", 1)])
                ci += 1
                if gg == 3:
                    attention(j)
            elif kind == "H":
                c = idx
                slot = next_wslot()
                b0, b1, (hps, hb) = inproj(slot, "edge")
                if nchunk[0] % 2 == 0 and cv_next[0] < 32:
                    issue_wout_convert(cv_next[0])
                    cv_next[0] += 1
                R.op("act", lambda e, b0=b0: e.activation(out=Hs[:, 1:513], in_=psb[b0][:, :], func=ACTF.Identity),
                     reads=[("ps", b0)], writes=["Hs"], toks=[("RD", 2)])
                R.op("act", lambda e, b1=b1: e.activation(out=Hs[:, 513:1025], in_=psb[b1][:, :], func=ACTF.Identity),
                     reads=[("ps", b1)], writes=["Hs"], toks=[("RD", 2)])
                R.op("act", lambda e, hps=hps: e.activation(out=Hs[:, 0:1], in_=hps[:, 0:1], func=ACTF.Identity),
                     reads=[("ps", hb)], writes=["Hs"], toks=[("RD", 2)])
                R.op("act", lambda e, hps=hps: e.activation(out=Hs[:, 1025:1026], in_=hps[:, 1:2], func=ACTF.Identity),
                     reads=[("ps", hb)], writes=["Hs"], toks=[("RD", 2)])
                slot = next_wslot()
                b0, b1, (hps, hb) = inproj(slot, "edge")
                if nchunk[0] % 2 == 0 and cv_next[0] < 32:
                    issue_wout_convert(cv_next[0])
                    cv_next[0] += 1
                R.op("dve", lambda e, b0=b0: e.tensor_tensor(out=ub[:, 1:513], in0=psb[b0][:, :], in1=Hs[:, 1:513],
                                                             op=ALU.mult),
                     reads=[("ps", b0), "Hs"], writes=["ub"], toks=[("RD", 2)])
                R.op("dve", lambda e, b1=b1: e.tensor_tensor(out=ub[:, 513:1025], in0=psb[b1][:, :],
                                                             in1=Hs[:, 513:1025], op=ALU.mult),
                     reads=[("ps", b1), "Hs"], writes=["ub"], toks=[("RD", 2)])
                R.op("dve", lambda e, hps=hps: e.scalar_tensor_tensor(out=ub[:, 0:1], in0=hps[:, 0:1],
                                                                      scalar=kb_sb[:, 10:11], in1=Hs[:, 0:1],
                                                                      op0=ALU.mult, op1=ALU.mult),
                     reads=[("ps", hb), "Hs", "kb"], writes=["ub"], toks=[("RD", 2)])
                R.op("dve", lambda e, hps=hps: e.scalar_tensor_tensor(out=ub[:, 1025:1026], in0=hps[:, 1:2],
                                                                      scalar=kb_sb[:, 11:12], in1=Hs[:, 1025:1026],
                                                                      op0=ALU.mult, op1=ALU.mult),
                     reads=[("ps", hb), "Hs", "kb"], writes=["ub"], toks=[("RD", 2)])
                R.op("pool", lambda e, c=c: e.tensor_scalar(out=vb, in0=ub[:, 1:1025], scalar1=cw_sb[:, c, 1:2],
                                                            scalar2=None, op0=ALU.mult),
                     reads=["ub", "cw"], writes=["vb"], toks=[("RD", 2)])
                R.op("dve", lambda e, c=c: e.scalar_tensor_tensor(out=vb, in0=ub[:, 0:1024], scalar=cw_sb[:, c, 0:1],
                                                                  in1=vb, op0=ALU.mult, op1=ALU.add),
                     reads=["ub", "cw", "vb"], writes=["vb"], toks=[("RD", 2)])
                R.op("dve", lambda e, c=c: e.scalar_tensor_tensor(out=vb, in0=ub[:, 2:1026], scalar=cw_sb[:, c, 2:3],
                                                                  in1=vb, op0=ALU.mult, op1=ALU.add),
                     reads=["ub", "cw", "vb"], writes=["vb"], toks=[("RD", 2)])
                slot = next_wslot()
                b0, b1, _ = inproj(slot, None)
                if nchunk[0] % 2 == 0 and cv_next[0] < 32:
                    issue_wout_convert(cv_next[0])
                    cv_next[0] += 1
                for hf, bk in ((0, b0), (1, b1)):
                    R.op("dve", lambda e, hf=hf, bk=bk: e.tensor_tensor(out=v2b[:, hf * 512:(hf + 1) * 512],
                                                                        in0=psb[bk][:, :],
                                                                        in1=vb[:, hf * 512:(hf + 1) * 512],
                                                                        op=ALU.mult),
                         reads=[("ps", bk), "vb"], writes=[("v2b", hf)], toks=[("RD", 2)])
                slot = next_wslot()
                b0, b1, _ = inproj(slot, None)
                if nchunk[0] % 2 == 0 and cv_next[0] < 32:
                    issue_wout_convert(cv_next[0])
                    cv_next[0] += 1
                for hf, bk in ((0, b0), (1, b1)):
                    R.op("act", lambda e, hf=hf, bk=bk: e.activation(out=thb[:, hf * 512:(hf + 1) * 512],
                                                                     in_=psb[bk][:, :], func=ACTF.Tanh, scale=0.5),
                         reads=[("ps", bk)], writes=[("thb", hf)], toks=[("RD", 2)])
                    R.op("dve", lambda e, hf=hf, bk=bk: e.scalar_tensor_tensor(
                        out=sb_[:, hf * 512:(hf + 1) * 512], in0=thb[:, hf * 512:(hf + 1) * 512], scalar=1.0,
                        in1=psb[bk][:, :], op0=ALU.add, op1=ALU.mult),
                        reads=[("thb", hf), ("ps", bk)], writes=[("sb", hf)], toks=[("RD", 2)])
                    R.op("pool", lambda e, hf=hf, c=c: e.tensor_tensor(
                        out=ycatT[:, c, hf * 512:(hf + 1) * 512], in0=sb_[:, hf * 512:(hf + 1) * 512],
                        in1=v2b[:, hf * 512:(hf + 1) * 512], op=ALU.mult),
                        reads=[("sb", hf), ("v2b", hf)], writes=[("ycat", c)], toks=[("RD", 2), ("RB", 2)])
                ci += 4
            else:
                raise AssertionError(kind)

        ycat_all = [("ycat", c) for c in range(KC)]
        pcount = [0]
        xcount = [0]
        out_ops = []
        pend_out = []

        def flush_out():
            (dst, src, tl_, r_) = pend_out.pop(0)
            out_ops.append(dma("act", dst, src, s_out[tl_ * 8 + r_], reads=[("z", tl_, r_)], name="out"))

        def final_ln(gi, r):
            pslot = pcount[0] % 2
            pcount[0] += 1
            for pi in (2, 3):
                dma("sp", par[:, pi, pslot], prow[pi, :, r * 512:(r + 1) * 512], s_par[pi][pslot],
                    writes=[("par", pi, pslot)], toks=[("RD", 3)])
            for tl in range(4):
                tslot = tl
                zsl = zbuf[:, tl, r * 512:(r + 1) * 512]
                R.op("act", lambda e, zsl=zsl, tl=tl, tslot=tslot: e.activation(
                    out=t2[:, tslot], in_=zsl, func=ACTF.Identity, scale=rz[:, tl:tl + 1],
                    bias=nmz[:, tl:tl + 1]),
                    reads=[("z", tl, r), ("rz", tl), ("nmz", tl)], writes=[("t2", tslot)], toks=[("RD", 3)])
                R.op("dve", lambda e, tslot=tslot, pslot=pslot: e.tensor_tensor(
                    out=t2[:, tslot], in0=t2[:, tslot], in1=par[:, 2, pslot], op=ALU.mult),
                    reads=[("t2", tslot), ("par", 2, pslot)], writes=[("t2", tslot)], toks=[("RD", 3)])
                R.op("dve", lambda e, zsl=zsl, tslot=tslot, pslot=pslot: e.tensor_tensor(
                    out=zsl, in0=t2[:, tslot], in1=par[:, 3, pslot], op=ALU.add),
                    reads=[("t2", tslot), ("par", 3, pslot)], writes=[("z", tl, r)], toks=[("RD", 3)])
                t0 = (gi * 4 + tl) * 128
                pend_out.append((out[t0:t0 + 128, r * 512:(r + 1) * 512], zsl, tl, r))
                while len(pend_out) > 3:
                    flush_out()

        for gi in range(2 if stop >= 6 else 0):
            for r in range(8):
                if gi == 1 and r < 4:
                    final_ln(0, 2 * r)
                    final_ln(0, 2 * r + 1)
                    if r == 3:
                        while pend_out:
                            flush_out()
                ps_par = r % 2
                banks = [4 * ps_par + tl for tl in range(4)]
                pslot = pcount[0] % 2
                pcount[0] += 1
                for pi in range(2):
                    dma("sp", par[:, pi, pslot], prow[pi, :, r * 512:(r + 1) * 512], s_par[pi][pslot],
                        writes=[("par", pi, pslot)], toks=[("RD", 3)])
                R.op("dve", lambda e, pslot=pslot: e.tensor_single_scalar(out=par[:, 1, pslot], in_=par[:, 1, pslot],
                                                                          scalar=DN_ALPHA, op=ALU.mult),
                     reads=[("par", 1, pslot)], writes=[("par", 1, pslot)], toks=[("RD", 3)])
                for cq in range(4):
                    slot = next_wslot()

                    def mm2(e, slot=slot, cq=cq, banks=banks, gi=gi):
                        ins = None
                        for c8 in range(8):
                            c = cq * 8 + c8
                            for tl in range(4):
                                t0 = (gi * 4 + tl) * 128
                                ins = e.matmul(psb[banks[tl]][:, :], ycatT[:, c, t0:t0 + 128], wring2[:, slot, c8, :],
                                               start=(c == 0), stop=(c == KC - 1))
                        return ins
                    R.op("pe", mm2, reads=[("w", slot)] + ycat_all, writes=[("ps", bk) for bk in banks],
                         toks=[("RB", 2)], name="outproj")
                    issue_next_w()
                for tl in range(4):
                    tcg = 1 + gi * 4 + tl
                    xslot = xcount[0] % 3
                    xcount[0] += 1
                    tslot = xslot % 2
                    dma("sp", xt[:, xslot], x_in[tcg * 128:(tcg + 1) * 128, r * 512:(r + 1) * 512], s_xt[xslot],
                        writes=[("xt", xslot)], toks=[("RD", 3)])
                    R.op("act", lambda e, xslot=xslot, tslot=tslot, tcg=tcg: e.activation(
                        out=t1[:, tslot], in_=xt[:, xslot], func=ACTF.Identity, scale=arstd_all[:, tcg:tcg + 1],
                        bias=namr_all[:, tcg:tcg + 1]),
                        reads=[("xt", xslot), ("arstd", tcg), ("namr", tcg)], writes=[("t1", tslot)],
                        toks=[("RD", 3)])
                    R.op("dve", lambda e, tslot=tslot, pslot=pslot: e.tensor_tensor(
                        out=t1[:, tslot], in0=t1[:, tslot], in1=par[:, 0, pslot], op=ALU.mult),
                        reads=[("t1", tslot), ("par", 0, pslot)], writes=[("t1", tslot)], toks=[("RD", 3)])
                    R.op("dve", lambda e, xslot=xslot, tslot=tslot, pslot=pslot: e.tensor_tensor(
                        out=xa[:, xslot], in0=t1[:, tslot], in1=par[:, 1, pslot], op=ALU.add),
                        reads=[("t1", tslot), ("par", 1, pslot)], writes=[("xa", xslot)], toks=[("RD", 3)])
                    zsl = zbuf[:, tl, r * 512:(r + 1) * 512]
                    R.op("dve", lambda e, zsl=zsl, bk=banks[tl], xslot=xslot: e.tensor_tensor(
                        out=zsl, in0=psb[bk][:, :], in1=xa[:, xslot], op=ALU.add),
                        reads=[("ps", banks[tl]), ("xa", xslot)], writes=[("z", tl, r)], toks=[("RA", 1)])
                    R.op("dve", lambda e, zsl=zsl, tl=tl, r=r: e.bn_stats(out=stz[:, tl, r, :], in_=zsl),
                         reads=[("z", tl, r)], writes=[("stz", tl)])
            for tl in range(4):
                R.op("dve", lambda e, tl=tl: e.bn_aggr(out=mvz[:, tl, :], in_=stz[:, tl]),
                     reads=[("stz", tl)], writes=[("mvz", tl)])
            for tl in range(4):
                R.op("dve", lambda e, tl=tl: e.tensor_single_scalar(out=vez[:, tl:tl + 1], in_=mvz[:, tl, 1:2],
                                                                    scalar=LN_EPS, op=ALU.add),
                     reads=[("mvz", tl)], writes=[("vez", tl)])
            for tl in range(4):
                R.op("pool", lambda e, tl=tl: e.tensor_tensor(out=rz[:, tl:tl + 1], in0=vez[:, tl:tl + 1], in1=mhalf,
                                                              op=ALU.pow),
                     reads=[("vez", tl), "mhalf"], writes=[("rz", tl)])
            for tl in range(4):
                R.op("dve", lambda e, tl=tl: e.tensor_scalar(out=nmz[:, tl:tl + 1], in0=mvz[:, tl, 0:1],
                                                             scalar1=rz[:, tl:tl + 1], scalar2=-1.0,
                                                             op0=ALU.mult, op1=ALU.mult),
                     reads=[("mvz", tl), ("rz", tl)], writes=[("nmz", tl)])
            if gi == 1:
                for r in range(8):
                    final_ln(1, r)
                while pend_out:
                    flush_out()
        R.op("act", None, extra_deps=out_ops, name="fence")

        R.finalize(eng_sems)
        with nc.allow_low_precision("bf16 matmul operands, fp32 PSUM accumulation"):
            with nc.Block() as block:
                @block.sync
                def _(e):
                    R.emit_engine("sp", e)

                @block.scalar
                def _(e):
                    R.emit_engine("act", e)

                @block.gpsimd
                def _(e):
                    R.emit_engine("pool", e)

                @block.vector
                def _(e):
                    R.emit_engine("dve", e)

                @block.tensor
                def _(e):
                    R.emit_engine("pe", e)
    return nc


_PROG = {}


def _get_prog():
    if "nc" not in _PROG:
        _PROG["nc"] = build_program()
    return _PROG["nc"]


def make_in_maps(x, emb_ln_g, emb_ln_b, w_in, conv_w, sink, w_out, ln_g, ln_b):
    x = np.asarray(x, dtype=np.float32)
    x2 = x.reshape(SEQ, D_MODEL)
    xp = np.zeros((SEQ + 256, D_MODEL), np.float32)
    xp[128:128 + SEQ] = x2
    w_in2 = np.ascontiguousarray(np.asarray(w_in, np.float32).reshape(D_MODEL, PROJ))
    w_out2 = np.ascontiguousarray(np.asarray(w_out, np.float32).reshape(D_MODEL, D_MODEL))
    eg = np.asarray(emb_ln_g, np.float32).reshape(D_MODEL)
    eb = np.asarray(emb_ln_b, np.float32).reshape(D_MODEL)
    lg = np.asarray(ln_g, np.float32).reshape(D_MODEL)
    lb = np.asarray(ln_b, np.float32).reshape(D_MODEL)
    gbT = np.ascontiguousarray(np.concatenate([eg.reshape(KC, 128).T, eb.reshape(KC, 128).T], axis=1))
    cw = np.asarray(conv_w, np.float32).reshape(3, 16, 128)
    cwT = np.ascontiguousarray(cw.transpose(2, 1, 0).reshape(128, 48))
    sinkb = np.ascontiguousarray(np.broadcast_to(np.asarray(sink, np.float32).reshape(1, 16), (128, 16)))
    prow = np.ascontiguousarray(np.broadcast_to(np.stack([eg, eb, lg, lb])[:, None, :], (4, 128, D_MODEL)))
    in_maps = []
    for c in range(NCORES):
        kb = np.zeros((128, 12), np.float32)
        kb[:, 10] = 1.0
        kb[:, 11] = 1.0
        if c == 0:
            kb[:, 0] = NEGBIG
            kb[:, 10] = 0.0
        if c == NCORES - 1:
            kb[:, 9] = NEGBIG
            kb[:, 11] = 0.0
        in_maps.append({
            "x_in": np.ascontiguousarray(xp[c * TOK:c * TOK + TOK + 256]),
            "w_in": w_in2, "w_out": w_out2, "gbT": gbT, "cwT": cwT, "sinkb": sinkb, "kbias": kb, "prow": prow,
        })
    return in_maps


def kernel(x, emb_ln_g, emb_ln_b, w_in, conv_w, sink, w_out, ln_g, ln_b):
    in_maps = make_in_maps(x, emb_ln_g, emb_ln_b, w_in, conv_w, sink, w_out, ln_g, ln_b)
    nc = _get_prog()
    res = run_bass_kernel_spmd(nc, in_maps, core_ids=list(range(NCORES)))
    outs = [np.asarray(r["out"], dtype=np.float32) for r in res.results]
    return np.concatenate(outs, axis=0).reshape(1, SEQ, D_MODEL)
```
